# Optimizing a Trainium2 kernel written in Bass

```python
import math
import jax, jax.numpy as jnp
from jax import lax
import numpy as np

D_MODEL = 2048
BATCH = 4
SEQ = 2048
DEPTH = 1

N_DN_HEADS = 6
DN_HEAD_DIM = 128
DN_WIDTH = N_DN_HEADS * DN_HEAD_DIM
DN_QKV_WIDTH = 3 * DN_WIDTH
DN_CONV = 4
DN_CHUNK = 64
DT_MIN = 0.001
DT_MAX = 0.1
N_SWA_HEADS = 6
N_SWA_KV = 2
SWA_HEAD_DIM = 128
SWA_WIDTH = N_SWA_HEADS * SWA_HEAD_DIM
SWA_KV_WIDTH = N_SWA_KV * SWA_HEAD_DIM
WINDOW = 128
SWA_BLOCK = 128
MEM_LEN = 256
N_MEM_HEADS = 4
MEM_HEAD_DIM = 128
MEM_WIDTH = N_MEM_HEADS * MEM_HEAD_DIM
N_BUCKETS = 32
MAX_DISTANCE = 128
N_BRANCHES = 3
D_FF = 4 * D_MODEL
EPS = 1e-6

IN_SIZES = [DN_QKV_WIDTH, DN_WIDTH, N_DN_HEADS, N_DN_HEADS,
            SWA_WIDTH, SWA_KV_WIDTH, SWA_KV_WIDTH, MEM_WIDTH, N_BRANCHES * D_MODEL]
IN_WIDTH = sum(IN_SIZES)
IN_SPLITS = [int(v) for v in np.cumsum(IN_SIZES[:-1])]

kernel_name = "hybrid_gdn_swa_memxattn_layer"


def rms_norm(x, w, eps=EPS):
    xf = x.astype(jnp.float32)
    y = xf * lax.rsqrt(jnp.mean(xf * xf, axis=-1, keepdims=True) + eps)
    return (y * w.astype(jnp.float32)).astype(x.dtype)


def l2_norm(x, eps=EPS):
    xf = x.astype(jnp.float32)
    return xf * lax.rsqrt(jnp.sum(xf * xf, axis=-1, keepdims=True) + eps)


def causal_depthwise_conv(x, w):
    k = w.shape[0]
    return lax.conv_general_dilated(
        x, w[:, None, :].astype(x.dtype), window_strides=(1,), padding=[(k - 1, 0)],
        dimension_numbers=("NWC", "WIO", "NWC"), feature_group_count=x.shape[-1])


def t5_bucket(dist):
    n = jnp.maximum(dist, 0)
    max_exact = N_BUCKETS // 2
    nf = jnp.maximum(n, 1).astype(jnp.float32)
    large = max_exact + (jnp.log(nf / max_exact) / math.log(MAX_DISTANCE / max_exact)
                         * (N_BUCKETS - max_exact)).astype(jnp.int32)
    large = jnp.minimum(large, N_BUCKETS - 1)
    return jnp.where(n < max_exact, n, large)


def gated_delta_rule(q, k, v, beta, g):
    B, S, H, DK = q.shape
    DV = v.shape[-1]
    C = DN_CHUNK
    NC = S // C

    def chunks(t):
        return t.reshape(B, NC, C, H, -1).transpose(0, 3, 1, 2, 4)

    q, k, v = chunks(q), chunks(k), chunks(v)
    beta = beta.reshape(B, NC, C, H).transpose(0, 3, 1, 2)
    G = jnp.cumsum(g.reshape(B, NC, C, H).transpose(0, 3, 1, 2), axis=-1)
    idx = jnp.arange(C)
    strict = idx[:, None] > idx[None, :]
    incl = idx[:, None] >= idx[None, :]
    dG = G[..., :, None] - G[..., None, :]
    decay_strict = jnp.exp(jnp.where(strict, dG, -jnp.inf))
    decay_incl = jnp.exp(jnp.where(incl, dG, -jnp.inf))
    kk = jnp.einsum("bhnid,bhnjd->bhnij", k, k)
    a_mat = jnp.eye(C, dtype=jnp.float32) + beta[..., None] * kk * decay_strict
    rhs = jnp.concatenate([v * beta[..., None], k * (beta * jnp.exp(G))[..., None]], axis=-1)
    sol = lax.linalg.triangular_solve(a_mat, rhs, left_side=True, lower=True, unit_diagonal=True)
    u, w = sol[..., :DV], sol[..., DV:]
    p = jnp.einsum("bhnid,bhnjd->bhnij", q, k) * decay_incl
    qg = q * jnp.exp(G)[..., None]
    kd = k * jnp.exp(G[..., -1:] - G)[..., None]
    gc = jnp.exp(G[..., -1])
    xs = (jnp.moveaxis(u, 2, 0), jnp.moveaxis(w, 2, 0), jnp.moveaxis(p, 2, 0),
          jnp.moveaxis(qg, 2, 0), jnp.moveaxis(kd, 2, 0), jnp.moveaxis(gc, 2, 0))

    def step(state, inp):
        u_c, w_c, p_c, qg_c, kd_c, gc_c = inp
        delta = u_c - jnp.einsum("bhcd,bhdv->bhcv", w_c, state)
        o_c = jnp.einsum("bhcd,bhdv->bhcv", qg_c, state) + jnp.einsum("bhij,bhjv->bhiv", p_c, delta)
        state = gc_c[..., None, None] * state + jnp.einsum("bhcd,bhcv->bhdv", kd_c, delta)
        return state, o_c

    s0 = jnp.zeros((B, H, DK, DV), jnp.float32)
    _, o = lax.scan(step, s0, xs)
    return o.transpose(1, 0, 3, 2, 4).reshape(B, S, H, DV)


def sliding_window_attention(q, k, v, sinks, rel_bias):
    B, S, HQ, D = q.shape
    HKV = k.shape[2]
    G = HQ // HKV
    L = SWA_BLOCK
    NB = S // L
    qb = q.reshape(B, NB, L, HKV, G, D)

    def windows(t):
        tp = jnp.pad(t, ((0, 0), (L, 0), (0, 0), (0, 0))).reshape(B, NB + 1, L, HKV, D)
        return jnp.concatenate([tp[:, :-1], tp[:, 1:]], axis=2)

    kw, vw = windows(k), windows(v)
    logits = jnp.einsum("bnqhgd,bnkhd->bnhgqk", qb, kw).astype(jnp.float32) * (D ** -0.5)
    qi = jnp.arange(L)[:, None]
    kj = jnp.arange(2 * L)[None, :]
    dist = qi - kj + L
    bias = rel_bias[t5_bucket(dist)].astype(jnp.float32)
    bias = bias.transpose(2, 0, 1).reshape(HKV, G, L, 2 * L)
    key_pos = jnp.arange(NB)[:, None, None] * L + kj[None] - L
    valid = (dist >= 0) & (dist < WINDOW) & (key_pos >= 0)
    logits = jnp.where(valid[None, :, None, None], logits + bias, -jnp.inf)
    sink = jnp.broadcast_to(sinks.astype(jnp.float32).reshape(HKV, G, 1, 1), logits.shape[:-1] + (1,))
    probs = jax.nn.softmax(jnp.concatenate([logits, sink], axis=-1), axis=-1)[..., :-1]
    out = jnp.einsum("bnhgqk,bnkhd->bnqhgd", probs.astype(v.dtype), vw)
    return out.reshape(B, S, HQ * D)


def memory_cross_attention(q, k, v):
    B, S, H, D = q.shape
    logits = jnp.einsum("bshd,bmhd->bhsm", q, k).astype(jnp.float32) * (D ** -0.5)
    probs = jax.nn.softmax(logits, axis=-1)
    return jnp.einsum("bhsm,bmhd->bshd", probs.astype(v.dtype), v).reshape(B, S, H * D)


def setup_inputs(seed: int = 0) -> dict:
    key = jax.random.key(seed)
    ks = jax.random.split(key, 24)
    f32 = jnp.float32
    L = DEPTH

    def nrm(k, shape, scale):
        return jax.random.normal(k, shape, f32) * scale

    def gain(k, shape):
        return 1.0 + 0.02 * jax.random.normal(k, shape, f32)

    dt = jnp.exp(jax.random.uniform(ks[23], (L, N_DN_HEADS), f32, math.log(DT_MIN), math.log(DT_MAX)))
    return {
        "x": nrm(ks[0], (BATCH, SEQ, D_MODEL), 1.0),
        "mem": nrm(ks[1], (BATCH, MEM_LEN, D_MODEL), 1.0),
        "attn_norm_w": gain(ks[2], (L, D_MODEL)),
        "w_in": nrm(ks[3], (L, D_MODEL, IN_WIDTH), D_MODEL ** -0.5),
        "dn_conv_w": nrm(ks[4], (L, DN_CONV, DN_QKV_WIDTH), DN_CONV ** -0.5),
        "dn_A_log": jnp.log(jax.random.uniform(ks[5], (L, N_DN_HEADS), f32, 1.0, 16.0)),
        "dn_dt_bias": dt + jnp.log(-jnp.expm1(-dt)),
        "dn_out_norm_w": gain(ks[6], (L, DN_HEAD_DIM)),
        "swa_q_norm_w": gain(ks[7], (L, SWA_HEAD_DIM)),
        "swa_k_norm_w": gain(ks[8], (L, SWA_HEAD_DIM)),
        "swa_sinks": nrm(ks[9], (L, N_SWA_HEADS), 0.5),
        "rel_bias": nrm(ks[10], (N_BUCKETS, N_SWA_HEADS), 0.5),
        "mem_norm_w": gain(ks[11], (L, D_MODEL)),
        "w_mem_kv": nrm(ks[12], (L, D_MODEL, 2 * MEM_WIDTH), D_MODEL ** -0.5),
        "xq_norm_w": gain(ks[13], (L, MEM_HEAD_DIM)),
        "xk_norm_w": gain(ks[14], (L, MEM_HEAD_DIM)),
        "p_dn": nrm(ks[15], (L, DN_WIDTH, D_MODEL), DN_WIDTH ** -0.5),
        "p_swa": nrm(ks[16], (L, SWA_WIDTH, D_MODEL), SWA_WIDTH ** -0.5),
        "p_mem": nrm(ks[17], (L, MEM_WIDTH, D_MODEL), MEM_WIDTH ** -0.5),
        "w_out": nrm(ks[18], (L, D_MODEL, D_MODEL), D_MODEL ** -0.5),
        "mlp_norm_w": gain(ks[19], (L, D_MODEL)),
        "w_mlp_up": nrm(ks[20], (L, D_MODEL, D_FF), D_MODEL ** -0.5),
        "w_mlp_down": nrm(ks[21], (L, D_FF, D_MODEL), D_FF ** -0.5),
    }


def reference(x, mem, attn_norm_w, w_in, dn_conv_w, dn_A_log, dn_dt_bias, dn_out_norm_w,
              swa_q_norm_w, swa_k_norm_w, swa_sinks, rel_bias, mem_norm_w, w_mem_kv,
              xq_norm_w, xk_norm_w, p_dn, p_swa, p_mem, w_out, mlp_norm_w, w_mlp_up, w_mlp_down):
    B, S, _ = x.shape
    M = mem.shape[1]
    f32 = jnp.float32
    for l in range(DEPTH):
        h = rms_norm(x, attn_norm_w[l])
        proj = h @ w_in[l]
        dn_qkv, dn_z, dn_b, dn_a, sq, sk, sv, mq, gate_logits = jnp.split(proj, IN_SPLITS, axis=-1)

        qkv = jax.nn.silu(causal_depthwise_conv(dn_qkv, dn_conv_w[l]))
        q_dn, k_dn, v_dn = jnp.split(qkv, 3, axis=-1)
        q_dn = l2_norm(q_dn.reshape(B, S, N_DN_HEADS, DN_HEAD_DIM)) * (DN_HEAD_DIM ** -0.5)
        k_dn = l2_norm(k_dn.reshape(B, S, N_DN_HEADS, DN_HEAD_DIM))
        v_dn = v_dn.reshape(B, S, N_DN_HEADS, DN_HEAD_DIM).astype(f32)
        beta = jax.nn.sigmoid(dn_b.astype(f32))
        g = -jnp.exp(dn_A_log[l].astype(f32)) * jax.nn.softplus(dn_a.astype(f32) + dn_dt_bias[l].astype(f32))
        o_dn = gated_delta_rule(q_dn, k_dn, v_dn, beta, g)
        z = dn_z.reshape(B, S, N_DN_HEADS, DN_HEAD_DIM).astype(f32)
        o_dn = (rms_norm(o_dn, dn_out_norm_w[l]) * jax.nn.silu(z)).astype(x.dtype).reshape(B, S, DN_WIDTH)

        q_s = rms_norm(sq.reshape(B, S, N_SWA_HEADS, SWA_HEAD_DIM), swa_q_norm_w[l])
        k_s = rms_norm(sk.reshape(B, S, N_SWA_KV, SWA_HEAD_DIM), swa_k_norm_w[l])
        v_s = sv.reshape(B, S, N_SWA_KV, SWA_HEAD_DIM)
        o_swa = sliding_window_attention(q_s, k_s, v_s, swa_sinks[l], rel_bias)

        mkv = rms_norm(mem, mem_norm_w[l]) @ w_mem_kv[l]
        mk, mv = jnp.split(mkv, 2, axis=-1)
        q_m = rms_norm(mq.reshape(B, S, N_MEM_HEADS, MEM_HEAD_DIM), xq_norm_w[l])
        k_m = rms_norm(mk.reshape(B, M, N_MEM_HEADS, MEM_HEAD_DIM), xk_norm_w[l])
        v_m = mv.reshape(B, M, N_MEM_HEADS, MEM_HEAD_DIM)
        o_mem = memory_cross_attention(q_m, k_m, v_m)

        gates = jax.nn.sigmoid(gate_logits.astype(f32)).astype(x.dtype).reshape(B, S, N_BRANCHES, D_MODEL)
        merged = (gates[:, :, 0] * (o_dn @ p_dn[l])
                  + gates[:, :, 1] * (o_swa @ p_swa[l])
                  + gates[:, :, 2] * (o_mem @ p_mem[l]))
        x = x + merged @ w_out[l]

        h2 = rms_norm(x, mlp_norm_w[l])
        x = x + jnp.square(jax.nn.relu(h2 @ w_mlp_up[l])) @ w_mlp_down[l]
    return x
```

```python
import os
import numpy as np
import ml_dtypes
from contextlib import ExitStack
import concourse.bass as bass
import concourse.mybir as mybir
from concourse.bass_utils import run_bass_kernel_spmd

F32 = mybir.dt.float32
BF16 = mybir.dt.bfloat16
AF = mybir.ActivationFunctionType
ALU = mybir.AluOpType

EPS = 1e-6
D = 2048
TOK = 1024
NT = 8
OFF_QKV, OFF_Z, OFF_B, OFF_A, OFF_SQ, OFF_SK, OFF_SV, OFF_MQ, OFF_G = 0, 2304, 3072, 3078, 3084, 3852, 4108, 4364, 4876
IN_W = 11020
NEG = -30000.0

C_ID, C_U, C_SL, C_NSL, C_ONE, C_CONV, C_DNW, C_WQ, C_WK, C_XQ, C_XK, C_ALOG, C_DTB, C_SINK, NCP = \
    0, 128, 256, 384, 512, 640, 712, 713, 714, 715, 716, 717, 723, 729, 736


class Buf:
    __slots__ = ("name", "lw", "rd")

    def __init__(self, name):
        self.name = name
        self.lw = None
        self.rd = {}


class Prog:
    ENG = ("pe", "act", "dve", "pool", "sp")

    def __init__(self):
        self.ops = {e: [] for e in self.ENG}
        self.cnt = {e: 0 for e in self.ENG}
        self.waited = {e: {} for e in self.ENG}
        self.dcnt = {}
        self.swq = []

    def _deps(self, reads, writes):
        toks = []
        for b in reads:
            if b.lw is not None:
                toks.append(b.lw)
        for b in writes:
            if b.lw is not None:
                toks.append(b.lw)
            toks.extend(b.rd.items())
        return toks

    def _filter(self, eng, toks):
        w = self.waited[eng]
        out = {}
        for (k, v) in toks:
            if w.get(k, 0) >= v:
                continue
            if out.get(k, 0) < v:
                out[k] = v
        for k, v in out.items():
            w[k] = v
        return list(out.items())

    def _record(self, tok, reads, writes):
        for b in writes:
            b.lw = tok
            b.rd = {}
        for b in reads:
            if b in writes:
                continue
            if b.rd.get(tok[0], 0) < tok[1]:
                b.rd[tok[0]] = tok[1]

    def op(self, eng, fn, reads=(), writes=()):
        extra = [b for b in reads if b.name.startswith("ps") and b not in writes]
        if extra:
            writes = list(writes) + extra
        waits = self._filter(eng, self._deps(reads, writes))
        self.cnt[eng] += 1
        tok = ("E" + eng, self.cnt[eng])
        self.ops[eng].append((waits, fn, ("E" + eng, 1)))
        self._record(tok, reads, writes)
        return tok

    def dma(self, eng, fn, sembuf, reads=(), writes=(), ndesc=0):
        toks = self._deps(reads, writes)
        if eng == "pool" and ndesc:
            while self.swq and sum(n for _, n in self.swq) + ndesc > 640:
                toks.append(self.swq.pop(0)[0])
        waits = self._filter(eng, toks)
        key = "D" + sembuf.name
        self.dcnt[key] = self.dcnt.get(key, 0) + 16
        tok = (key, self.dcnt[key])
        self.ops[eng].append((waits, fn, (key, 16)))
        self._record(tok, reads, writes)
        if eng == "pool" and ndesc:
            self.swq.append((tok, ndesc))
        return tok

    def barrier(self):
        toks = [("E" + e, self.cnt[e]) for e in self.ENG if self.cnt[e] > 0] + list(self.dcnt.items())
        for e in self.ENG:
            waits = self._filter(e, toks)
            if waits:
                self.ops[e].append((waits, None, None))

    def final_wait(self, eng="sp"):
        waits = self._filter(eng, list(self.dcnt.items()))
        self.ops[eng].append((waits, None, None))

    def emit(self, nc):
        keys = ["E" + e for e in self.ENG] + sorted(self.dcnt.keys())
        with ExitStack() as st:
            sems = {}
            for k in keys:
                sems[k] = st.enter_context(nc.semaphore("s_" + k))
            block = st.enter_context(nc.Block())
            binders = {"pe": block.tensor, "act": block.scalar, "dve": block.vector,
                       "pool": block.gpsimd, "sp": block.sync}
            for eng in self.ENG:
                ops = self.ops[eng]

                def body(e, ops=ops):
                    for waits, fn, inc in ops:
                        for k, v in waits:
                            e.wait_ge(sems[k], v)
                        if fn is None:
                            continue
                        ins = fn(e)
                        ins.then_inc(sems[inc[0]], inc[1])
                binders[eng](body)


class StopBuild(Exception):
    pass


class T:
    __slots__ = ("ap", "b", "off", "words")

    def __init__(self, ap, name, off=None, words=None):
        self.ap = ap
        self.b = Buf(name)
        self.off = off
        self.words = words

    def dump(self):
        return (self, self.off, self.words)


def build(stop_after=None, dbg=None):
    nc = bass.Bass("TRN2", target_bir_lowering=False)
    P = Prog()
    dr = {}

    def din(name, shape, dt=F32):
        dr[name] = nc.dram_tensor(name, shape, dt, kind="ExternalInput").ap()
        return dr[name]

    x_d = din("x", [TOK, D])
    xp_d = din("xp", [TOK, D])
    mem_d = din("mem", [256, D])
    win_d = din("w_in", [D, IN_W])
    wmkv_d = din("w_mem_kv", [D, 1024])
    pall_d = din("p_all", [D, D])
    wout_d = din("w_out", [D, D])
    wup_d = din("w_up", [D, 4 * D])
    wdn_d = din("w_down", [4 * D, D])
    nw3_d = din("nw3", [3, D])
    cpack_d = din("cpack", [128, NCP])
    cb16_d = din("cb16", [128, 256], BF16)
    swab_d = din("swab", [3, 128, 768])
    y_d = nc.dram_tensor("y", [TOK, D], F32, kind="ExternalOutput").ap()
    dbg_d = None
    if dbg is not None:
        dbg_d = nc.dram_tensor("dbg", [128, dbg], F32, kind="ExternalOutput").ap()

    with ExitStack() as st:
        AW = 52800
        arena = st.enter_context(nc.sbuf_tensor("arena", [128, AW], F32))
        psum = [st.enter_context(nc.psum_tensor(f"ps{i}", [128, 512], F32)) for i in range(8)]
        psB = [[Buf(f"ps{i}")] * 4 for i in range(8)]

        uid = [0]

        def view(off, words, name, dt=F32, shape=None):
            ap = arena[:, off:off + words]
            if dt == BF16:
                ap = ap.bitcast(BF16)
            if shape is not None:
                ap = ap[:, 0:int(np.prod(shape))]
                if len(shape) == 2:
                    ap = ap.rearrange("p (a b) -> p a b", a=shape[0])
                elif len(shape) == 3:
                    ap = ap.rearrange("p (a b c) -> p a b c", a=shape[0], b=shape[1])
            uid[0] += 1
            return T(ap, f"{name}_{uid[0]}", off, words)

        class Alloc:
            def __init__(self, lo, hi):
                self.lo, self.hi, self.p = lo, hi, lo

            def get(self, words, name, dt=F32, shape=None):
                o = self.p
                self.p += words
                assert self.p <= self.hi, (name, self.p, self.hi)
                return view(o, words, name, dt, shape)

        def bl(ts):
            out = []
            for t in ts:
                if isinstance(t, T):
                    out.append(t.b)
                elif isinstance(t, Buf):
                    out.append(t)
                elif isinstance(t, (list, tuple)):
                    out.extend(bl(t))
            return out

        def mm(out_ap, outb, lhsT, rhs, rd, start=True, stop=True):
            P.op("pe", lambda e: e.matmul(out_ap, lhsT=lhsT, rhs=rhs, start=start, stop=stop), reads=bl(rd), writes=bl([outb]))

        def tr(out_ap, outb, in_ap, ident_ap, rd):
            P.op("pe", lambda e: e.transpose(out=out_ap, in_=in_ap, identity=ident_ap), reads=bl(rd), writes=bl([outb]))

        def act(out_ap, in_ap, func, rd, wr, scale=None, bias=None, accum=None, eng="act"):
            kw = {}
            if scale is not None:
                kw["scale"] = scale
            if bias is not None:
                kw["bias"] = bias
            if accum is not None:
                kw["accum_out"] = accum
            P.op("act", lambda e: e.activation(out=out_ap, in_=in_ap, func=func, **kw), reads=bl(rd), writes=bl(wr))

        def tt(out_ap, a, b, op, rd, wr, eng="dve"):
            P.op(eng, lambda e: e.tensor_tensor(out=out_ap, in0=a, in1=b, op=op), reads=bl(rd), writes=bl(wr))

        def ts(out_ap, a, s1, op0, rd, wr, s2=None, op1=None, eng="dve"):
            if op1 is None:
                P.op(eng, lambda e: e.tensor_scalar(out=out_ap, in0=a, scalar1=s1, scalar2=None, op0=op0), reads=bl(rd), writes=bl(wr))
            else:
                P.op(eng, lambda e: e.tensor_scalar(out=out_ap, in0=a, scalar1=s1, scalar2=s2, op0=op0, op1=op1), reads=bl(rd), writes=bl(wr))

        def stt(out_ap, a, s, b, op0, op1, rd, wr):
            P.op("dve", lambda e: e.scalar_tensor_tensor(out=out_ap, in0=a, scalar=s, in1=b, op0=op0, op1=op1), reads=bl(rd), writes=bl(wr))

        def cp(out_ap, in_ap, rd, wr, eng="dve"):
            if eng == "act":
                P.op("act", lambda e: e.copy(out=out_ap, in_=in_ap), reads=bl(rd), writes=bl(wr))
            else:
                P.op(eng, lambda e: e.tensor_copy(out=out_ap, in_=in_ap), reads=bl(rd), writes=bl(wr))

        def recip(out_ap, in_ap, rd, wr):
            P.op("dve", lambda e: e.reciprocal(out=out_ap, in_=in_ap), reads=bl(rd), writes=bl(wr))

        def dma(q, out_ap, in_ap, semT, rd=(), wr=()):
            nd = 0
            if q == "pool":
                shp = list(out_ap.shape)
                nd = int(np.prod(shp[:-1])) // 16 + 2
            P.dma(q, lambda e: e.dma_start(out=out_ap, in_=in_ap), semT.b if isinstance(semT, T) else semT, reads=bl(rd), writes=bl(wr), ndesc=nd)

        dumps = []

        def chk(name, dl):
            if stop_after == name:
                dumps.extend(dl)
                raise StopBuild()

        def wload(slab, w_dram, c0, n, kt=16, r0=0):
            dma("pool", slab.ap, w_dram[r0:r0 + kt * 128, c0:c0 + n].rearrange("(k p) n -> p k n", p=128), slab, wr=[slab])

        def rstd_from_ssq(out_ap, tmp_ap, ssq_ap, n, rd, wr):
            act(tmp_ap, ssq_ap, AF.Ln, rd, wr, scale=1.0 / n, bias=EPS)
            act(out_ap, tmp_ap, AF.Exp, wr, wr, scale=-0.5)

        CONST = Alloc(0, 4096)
        cpk = CONST.get(NCP, "cpack")
        cb16 = CONST.get(128, "cb16", BF16)
        wbc = CONST.get(2048, "wbc")
        smalls = CONST.get(64, "smalls")
        identf = cpk.ap[:, C_ID:C_ID + 128]
        Umat = cpk.ap[:, C_U:C_U + 128]
        SLm = cpk.ap[:, C_SL:C_SL + 128]
        NSLm = cpk.ap[:, C_NSL:C_NSL + 128]
        onesf = cpk.ap[:, C_ONE:C_ONE + 128]
        identb = cb16.ap[:, 0:128]
        onesb = cb16.ap[:, 128:256]
        hT = view(4096, 8192, "hT", BF16, shape=(16, TOK))
        hP = view(4096 + 8192, 8192, "hP", BF16, shape=(16, TOK))
        oT = T(hP.ap, "oT", hP.off, hP.words)
        W0 = 4096 + 16384

        dma("sp", cpk.ap, cpack_d[:, :], cpk, wr=[cpk])
        dma("sp", cb16.ap, cb16_d[:, :], cb16, wr=[cb16])

        def norm_T(src_dram, tiles, nw_row, dst, WA, src_sb=None, load_w=True):
            xt = [WA.get(2048, "xt") for _ in range(2)] if src_sb is None else None
            junk = WA.get(1024, "junk", BF16)
            xn = WA.get(1024, "xn", BF16)
            stat = WA.get(8, "stat")
            if load_w:
                dma("sp", wbc.ap, nw3_d[nw_row].partition_broadcast(128), wbc, wr=[wbc])
            for i, (t, dt_) in enumerate(tiles):
                if src_sb is None:
                    xb = xt[i % 2]
                    dma("sp", xb.ap, src_dram[t * 128:(t + 1) * 128, :], xb, wr=[xb])
                    xin = xb.ap
                else:
                    xb, xin = src_sb(t)
                act(junk.ap, xin, AF.Square, [xb], [junk, stat], accum=stat.ap[:, 0:1])
                rstd_from_ssq(stat.ap[:, 2:3], stat.ap[:, 1:2], stat.ap[:, 0:1], D, [stat], [stat])
                stt(xn.ap, xin, stat.ap[:, 2:3], wbc.ap, ALU.mult, ALU.mult, [xb, stat, wbc], [xn])
                for half in range(2):
                    bank = half
                    pb = psum[bank][:].bitcast(BF16)
                    for j in range(8):
                        kt = half * 8 + j
                        tr(pb[:, j * 128:(j + 1) * 128], psB[bank], xn.ap[:, kt * 128:(kt + 1) * 128], identb, [xn, cb16])
                    cp(dst.ap[:, half * 8:(half + 1) * 8, dt_ * 128:(dt_ + 1) * 128], pb.rearrange("p (k t) -> p k t", k=8),
                       [psB[bank]], [dst], eng=("dve" if half == 0 else "act"))

        WA = Alloc(W0, AW)
        norm_T(x_d, [(t, t) for t in range(NT)], 0, hT, WA)
        norm_T(xp_d, [(t, t) for t in range(NT)], 0, hP, Alloc(WA.p, AW), load_w=False)
        hHalo = CONST.get(1024, "hHalo", BF16, shape=(16, 128))
        cp(hHalo.ap, hP.ap[:, :, 896:1024], [hP], [hHalo])
        if stop_after == "p0":
            return finish(nc, P, st, dbg_d, [(hT, 4096, 8192), (hP, 4096 + 8192, 8192)], dma, locals())
        P.barrier()

        GA = Alloc(W0, AW)
        qkvT = GA.get(9 * 1024, "qkvT", shape=(9, TOK))
        zs = GA.get(3 * 1024, "zs", shape=(3, TOK))
        raw = [GA.get(1028, "raw") for _ in range(2)]
        wsl = [GA.get(1024, "wsl", BF16, shape=(16, 128)) for _ in range(3)]
        wba = GA.get(128, "wba", BF16, shape=(16, 12))
        ba_all = GA.get(16 * 12, "ba_all", shape=(16, 12))
        beta_all = GA.get(16 * 6, "beta_all", shape=(16, 6))
        g_all = GA.get(16 * 6, "g_all", shape=(16, 6))
        sp_tmp = GA.get(16 * 6, "sp_tmp", shape=(16, 6))
        nexpA = GA.get(8, "nexpA")
        Sst = [GA.get(128, f"S{h}") for h in range(6)]
        carry = GA.get(64, "carry", shape=(18, 3))
        HT = []
        names = ["Xk", "kd", "Xv", "qg", "gSL", "gSLc", "eE", "eET", "t1", "pT", "R0", "R1", "RT0", "RT1", "Y", "wTn", "qgT", "delta", "on"]
        for h in range(3):
            HT.append({n: GA.get(128, f"{n}{h}") for n in names})
        csm = GA.get(64, "csm")
        csmq = GA.get(32, "csmq")
        wsl_i = [0]

        wload_ba = lambda: dma("pool", wba.ap[:, :, 0:12], win_d[:, OFF_B:OFF_B + 12].rearrange("(k p) n -> p k n", p=128), wba, wr=[wba])
        wload_ba()
        for tt_i in range(16):
            src = hP if tt_i < 8 else hT
            tl = tt_i % 8
            for kt in range(16):
                mm(psum[7][:, 0:12], psB[7][0], src.ap[:, kt, tl * 128:(tl + 1) * 128], wba.ap[:, kt, 0:12], [src, wba], start=(kt == 0), stop=(kt == 15))
            cp(ba_all.ap[:, tt_i, :], psum[7][:, 0:12], [psB[7][0]], [ba_all])
        act(beta_all.ap, ba_all.ap[:, :, 0:6], AF.Exp, [ba_all], [beta_all], scale=-1.0)
        ts(beta_all.ap, beta_all.ap, 1.0, ALU.add, [beta_all], [beta_all])
        recip(beta_all.ap, beta_all.ap, [beta_all], [beta_all])
        act(nexpA.ap[:, 0:6], cpk.ap[:, C_ALOG:C_ALOG + 6], AF.Exp, [cpk], [nexpA])
        for tt_i in range(16):
            tt(sp_tmp.ap[:, tt_i, :], ba_all.ap[:, tt_i, 6:12], cpk.ap[:, C_DTB:C_DTB + 6], ALU.add, [ba_all, cpk], [sp_tmp])
        act(sp_tmp.ap, sp_tmp.ap, AF.Exp, [sp_tmp], [sp_tmp])
        act(sp_tmp.ap, sp_tmp.ap, AF.Ln, [sp_tmp], [sp_tmp], bias=1.0)
        for tt_i in range(16):
            stt(g_all.ap[:, tt_i, :], sp_tmp.ap[:, tt_i, :], -1.0, nexpA.ap[:, 0:6], ALU.mult, ALU.mult, [sp_tmp, nexpA], [g_all])
        if stop_after == "ba":
            return finish(nc, P, st, dbg_d, [ba_all.dump(), beta_all.dump(), g_all.dump()], dma, locals())
        for h in range(6):
            P.op("dve", lambda e, h=h: e.memset(Sst[h].ap, 0.0), writes=[Sst[h].b])
        P.op("dve", lambda e: e.memset(carry.ap, 0.0), writes=[carry.b])

        def gdn_proj(hh, stage):
            src = hP if stage == 0 else hT
            fts = [hh * 3 + j for j in range(3)] + [6 + hh * 3 + j for j in range(3)] + [12 + hh * 3 + j for j in range(3)]
            for li, ft in enumerate(fts):
                w = wsl[wsl_i[0] % 3]
                wsl_i[0] += 1
                wload(w, win_d, OFF_QKV + ft * 128, 128)
                rb = raw[li % 2]
                for c2 in range(2):
                    bank = 2 + c2
                    for kt in range(16):
                        mm(psum[bank][:, :], psB[bank], w.ap[:, kt, :], src.ap[:, kt, c2 * 512:(c2 + 1) * 512], [w, src], start=(kt == 0), stop=(kt == 15))
                    cp(rb.ap[:, 3 + c2 * 512:3 + (c2 + 1) * 512], psum[bank][:, :], [psB[bank]], [rb], eng=("act" if c2 == 0 else "dve"))
                cp(rb.ap[:, 0:3], carry.ap[:, ft, :], [carry], [rb])
                cp(carry.ap[:, ft, :], rb.ap[:, 1024:1027], [rb], [carry])
                cw = cpk.ap[:, C_CONV + ft * 4:C_CONV + ft * 4 + 4]
                acc = qkvT.ap[:, li, :]
                ts(acc, rb.ap[:, 0:1024], cw[:, 0:1], ALU.mult, [rb, cpk], [qkvT])
                for k in range(1, 4):
                    stt(acc, rb.ap[:, k:k + 1024], cw[:, k:k + 1], acc, ALU.mult, ALU.add, [rb, cpk, qkvT], [qkvT])
                act(acc, acc, AF.Silu, [qkvT], [qkvT])
            if stage == 1:
                for j in range(3):
                    w = wsl[wsl_i[0] % 3]
                    wsl_i[0] += 1
                    wload(w, win_d, OFF_Z + (hh * 3 + j) * 128, 128)
                    for c2 in range(2):
                        bank = 2 + c2
                        for kt in range(16):
                            mm(psum[bank][:, :], psB[bank], w.ap[:, kt, :], hT.ap[:, kt, c2 * 512:(c2 + 1) * 512], [w, hT], start=(kt == 0), stop=(kt == 15))
                        act(zs.ap[:, j, c2 * 512:(c2 + 1) * 512], psum[bank][:, :], AF.Silu, [psB[bank]], [zs])

        def gdn_chunks(hh, stage):
            own = stage == 1
            for c in range(8):
                tt_i = stage * 8 + c
                cs = slice(c * 128, (c + 1) * 128)
                h0 = hh * 3
                g3 = g_all.ap[:, tt_i, h0:h0 + 3]
                b3 = beta_all.ap[:, tt_i, h0:h0 + 3]
                mm(psum[0][:, 0:3], psB[0][0], Umat, g3, [cpk, g_all])
                mm(psum[0][:, 3:6], psB[0][0], SLm, g3, [cpk, g_all])
                mm(psum[0][:, 6:9], psB[0][0], onesf, g3, [cpk, g_all])
                act(csm.ap[:, 0:9], psum[0][:, 0:9], AF.Exp, [psB[0][0]], [csm])
                eG, eGr, gc = csm.ap[:, 0:3], csm.ap[:, 3:6], csm.ap[:, 6:9]
                chk("c0a", [csm.dump()])
                for h in range(3):
                    bank = 1
                    for j in range(3):
                        tr(psum[bank][:, j * 128:(j + 1) * 128], psB[bank][j], qkvT.ap[:, j * 3 + h, cs], identf, [qkvT, cpk])
                    Hh = HT[h]
                    if os.environ.get("GSKIP") == "tr":
                        chk("c0b", [csm.dump()])
                    act(Hh["t1"].ap, psum[bank][:, 0:128], AF.Square, [psB[bank][0]], [Hh["t1"], csmq], accum=csmq.ap[:, h:h + 1])
                    if os.environ.get("GSKIP") == "sq":
                        chk("c0b", [csm.dump()])
                    act(Hh["t1"].ap, psum[bank][:, 128:256], AF.Square, [psB[bank][1]], [Hh["t1"], csmq], accum=csmq.ap[:, 3 + h:4 + h])
                    if h == 2:
                        act(csmq.ap[:, 6:12], csmq.ap[:, 0:6], AF.Ln, [csmq], [csmq], bias=EPS)
                    cp(Hh["qg"].ap, psum[bank][:, 0:128], [psB[bank][0]], [Hh["qg"]], eng="dve")
                    cp(Hh["Xk"].ap, psum[bank][:, 128:256], [psB[bank][1]], [Hh["Xk"]], eng="act")
                    cp(Hh["Xv"].ap, psum[bank][:, 256:384], [psB[bank][2]], [Hh["Xv"]], eng="dve")
                    if os.environ.get("GSKIP") == "cp" + str(h):
                        chk("c0b", [csm.dump()])
                chk("c0b", [csm.dump(), csmq.dump()])
                act(csmq.ap[:, 12:15], csmq.ap[:, 6:9], AF.Exp, [csmq], [csmq], scale=-0.5)
                act(csm.ap[:, 12:15], csmq.ap[:, 9:12], AF.Exp, [csmq], [csm], scale=-0.5)
                ts(csm.ap[:, 15:18], csmq.ap[:, 9:12], -0.5, ALU.mult, [csmq], [csm])
                rk, lnrk = csm.ap[:, 12:15], csm.ap[:, 15:18]
                tt(csm.ap[:, 18:21], b3, rk, ALU.mult, [beta_all, csm], [csm])
                tt(csm.ap[:, 21:24], csm.ap[:, 18:21], eG, ALU.mult, [csm], [csm])
                tt(csm.ap[:, 24:27], rk, eGr, ALU.mult, [csm], [csm])
                ts(csm.ap[:, 27:30], csmq.ap[:, 12:15], float(128 ** -0.5), ALU.mult, [csmq], [csm])
                sA, sXk, skd, so = csm.ap[:, 18:21], csm.ap[:, 21:24], csm.ap[:, 24:27], csm.ap[:, 27:30]
                for h in range(3):
                    Hh = HT[h]
                    hcol = slice(h, h + 1)
                    ts(Hh["kd"].ap, Hh["Xk"].ap, skd[:, hcol], ALU.mult, [Hh["Xk"], csm], [Hh["kd"]])
                    ts(Hh["Xk"].ap, Hh["Xk"].ap, sXk[:, hcol], ALU.mult, [Hh["Xk"], csm], [Hh["Xk"]])
                    ts(Hh["Xv"].ap, Hh["Xv"].ap, b3[:, hcol], ALU.mult, [Hh["Xv"], beta_all], [Hh["Xv"]])
                    if own:
                        ts(Hh["qg"].ap, Hh["qg"].ap, eG[:, hcol], ALU.mult, [Hh["qg"], csm], [Hh["qg"]])
                    ts(Hh["gSL"].ap, SLm, g3[:, hcol], ALU.mult, [cpk, g_all], [Hh["gSL"]])
                    stt(Hh["gSLc"].ap, identf, lnrk[:, hcol], Hh["gSL"].ap, ALU.mult, ALU.add, [cpk, csm, Hh["gSL"]], [Hh["gSLc"]])
                for h in range(3):
                    Hh = HT[h]
                    hcol = slice(h, h + 1)
                    kTr = qkvT.ap[:, 3 + h, cs]
                    qTr = qkvT.ap[:, h, cs]
                    bank = 2 + 2 * h
                    mm(psum[bank][:, 0:128], psB[bank][0], kTr, kTr, [qkvT])
                    mm(psum[bank][:, 128:256], psB[bank][1], Umat, Hh["gSLc"].ap, [cpk, Hh["gSLc"]])
                    if own:
                        mm(psum[bank][:, 256:384], psB[bank][2], kTr, qTr, [qkvT])
                        mm(psum[bank][:, 384:512], psB[bank][3], Hh["gSL"].ap, Umat, [cpk, Hh["gSL"]])
                    act(Hh["eE"].ap, psum[bank][:, 128:256], AF.Exp, [psB[bank][1]], [Hh["eE"]])
                    tt(Hh["t1"].ap, psum[bank][:, 0:128], Hh["eE"].ap, ALU.mult, [psB[bank][0], Hh["eE"]], [Hh["t1"]])
                    stt(Hh["RT0"].ap, Hh["t1"].ap, sA[:, hcol], NSLm, ALU.mult, ALU.mult, [Hh["t1"], csm, cpk], [Hh["RT0"]])
                    if own:
                        act(Hh["eET"].ap, psum[bank][:, 384:512], AF.Exp, [psB[bank][3]], [Hh["eET"]])
                        tt(Hh["t1"].ap, psum[bank][:, 256:384], Hh["eET"].ap, ALU.mult, [psB[bank][2], Hh["eET"]], [Hh["t1"]])
                        stt(Hh["pT"].ap, Hh["t1"].ap, rk[:, hcol], Umat, ALU.mult, ALU.mult, [Hh["t1"], csm, cpk], [Hh["pT"]])
                chk("c0d", [csm.dump(), csmq.dump()] + [HT[0][n].dump() for n in names])
                for h in range(3):
                    Hh = HT[h]
                    bank = 3 + 2 * h
                    tr(psum[bank][:, 0:128], psB[bank][0], Hh["RT0"].ap, identf, [Hh["RT0"], cpk])
                    cp(Hh["R0"].ap, psum[bank][:, 0:128], [psB[bank][0]], [Hh["R0"]], eng="act")
                    tt(Hh["Y"].ap, psum[bank][:, 0:128], identf, ALU.add, [psB[bank][0], cpk], [Hh["Y"]])
                for lvl in range(1, 7):
                    pr, nx = (lvl - 1) % 2, lvl % 2
                    for h in range(3):
                        Hh = HT[h]
                        bank = (2 + 2 * h) if lvl % 2 == 1 else (3 + 2 * h)
                        Rp, RTp = Hh[f"R{pr}"], Hh[f"RT{pr}"]
                        Rn, RTn = Hh[f"R{nx}"], Hh[f"RT{nx}"]
                        mm(psum[bank][:, 128:256], psB[bank][1], Rp.ap, RTp.ap, [Rp, RTp])
                        cp(RTn.ap, psum[bank][:, 128:256], [psB[bank][1]], [RTn], eng="act")
                        if lvl < 6:
                            mm(psum[bank][:, 0:128], psB[bank][0], RTp.ap, Rp.ap, [Rp, RTp])
                            cp(Rn.ap, psum[bank][:, 0:128], [psB[bank][0]], [Rn], eng="dve")
                        mm(psum[bank][:, 256:384], psB[bank][2], RTn.ap, Hh["Y"].ap, [RTn, Hh["Y"]])
                        tt(Hh["Y"].ap, psum[bank][:, 256:384], Hh["Y"].ap, ALU.add, [psB[bank][2], Hh["Y"]], [Hh["Y"]])
                chk("c0e", [csm.dump(), csmq.dump()] + [HT[0][n].dump() for n in names])
                for h in range(3):
                    Hh = HT[h]
                    bank = 3 + 2 * h
                    mm(psum[bank][:, 0:128], psB[bank][0], Hh["Xk"].ap, Hh["Y"].ap, [Hh["Xk"], Hh["Y"]])
                    act(Hh["wTn"].ap, psum[bank][:, 0:128], AF.Copy, [psB[bank][0]], [Hh["wTn"]], scale=-1.0)
                    if own:
                        tr(psum[bank][:, 128:256], psB[bank][1], Hh["qg"].ap, identf, [Hh["qg"], cpk])
                        cp(Hh["qgT"].ap, psum[bank][:, 128:256], [psB[bank][1]], [Hh["qgT"]], eng="dve")
                for h in range(3):
                    Hh = HT[h]
                    S = Sst[hh * 3 + h]
                    bank = 2 + 2 * h
                    mm(psum[bank][:, 0:128], psB[bank][0], Hh["Y"].ap, Hh["Xv"].ap, [Hh["Y"], Hh["Xv"]], start=True, stop=False)
                    mm(psum[bank][:, 0:128], psB[bank][0], Hh["wTn"].ap, S.ap, [Hh["wTn"], S], start=False, stop=True)
                    cp(Hh["delta"].ap, psum[bank][:, 0:128], [psB[bank][0]], [Hh["delta"]], eng="dve")
                    if own:
                        mm(psum[bank][:, 128:256], psB[bank][1], Hh["qgT"].ap, S.ap, [Hh["qgT"], S], start=True, stop=False)
                        mm(psum[bank][:, 128:256], psB[bank][1], Hh["pT"].ap, Hh["delta"].ap, [Hh["pT"], Hh["delta"]], start=False, stop=True)
                    mm(psum[bank][:, 256:384], psB[bank][2], Hh["kd"].ap, Hh["delta"].ap, [Hh["kd"], Hh["delta"]])
                    stt(S.ap, S.ap, gc[:, h:h + 1], psum[bank][:, 256:384], ALU.mult, ALU.add, [S, csm, psB[bank][2]], [S])
                    if h == 0:
                        chk("chunk0", [csm.dump(), csmq.dump()] + [Hh[n].dump() for n in ["Xk", "kd", "Xv", "RT0", "Y", "wTn", "delta"]] + [S.dump()])
                    if own:
                        ts(Hh["on"].ap, psum[bank][:, 128:256], so[:, h:h + 1], ALU.mult, [psB[bank][1], csm], [Hh["on"]])
                        act(Hh["t1"].ap, Hh["on"].ap, AF.Square, [Hh["on"]], [Hh["t1"], csmq], accum=csmq.ap[:, 16 + h:17 + h])
                if own:
                    act(csm.ap[:, 32:35], csmq.ap[:, 16:19], AF.Ln, [csmq], [csm], scale=1.0 / 128, bias=EPS)
                    act(csm.ap[:, 32:35], csm.ap[:, 32:35], AF.Exp, [csm], [csm], scale=-0.5)
                    for h in range(3):
                        Hh = HT[h]
                        ts(Hh["on"].ap, Hh["on"].ap, csm.ap[:, 32 + h:33 + h], ALU.mult, [Hh["on"], csm], [Hh["on"]])
                        tr(psum[1][:, 128:256], psB[1][1], Hh["on"].ap, identf, [Hh["on"], cpk])
                        stt(oT.ap[:, hh * 3 + h, cs], psum[1][:, 128:256], cpk.ap[:, C_DNW:C_DNW + 1], zs.ap[:, h, cs],
                            ALU.mult, ALU.mult, [psB[1][1], cpk, zs], [oT])

        try:
            for hh in range(2):
                gdn_proj(hh, 0)
                chk("gproj", [qkvT.dump()])
                gdn_chunks(hh, 0)
            chk("gpre", [Sst[0].dump(), Sst[5].dump()])
            P.barrier()
            for hh in range(2):
                gdn_proj(hh, 1)
                gdn_chunks(hh, 1)
            chk("gdn", [oT.dump()])
            P.barrier()
            SA = Alloc(W0, AW)
            swab = SA.get(3 * 768, "swab", shape=(3, 768))
            dma("sp", swab.ap, swab_d.rearrange("t k n -> k t n"), swab, wr=[swab])
            wg = SA.get(5120, "wg", BF16, shape=(16, 640))
            qn = SA.get(192, "qn", BF16)
            kn = SA.get(64, "kn", BF16)
            qTs = SA.get(192, "qTs", BF16)
            KT = [SA.get(64, f"KT{i}", BF16) for i in range(2)]
            VT = [SA.get(64, f"VT{i}", BF16) for i in range(2)]
            sc = SA.get(384, "sc")
            ETb = [SA.get(192, f"ET{i}", BF16) for i in range(2)]
            den = SA.get(384, "den")
            ssm = SA.get(16, "ssm")
            esink = SA.get(8, "esink")
            sjunk = SA.get(128, "sjunk")
            act(esink.ap[:, 0:6], cpk.ap[:, C_SINK:C_SINK + 6], AF.Exp, [cpk], [esink])
            for g in range(2):
                for (c0, n, o0) in ((OFF_SQ + g * 384, 384, 0), (OFF_SK + g * 128, 128, 384), (OFF_SV + g * 128, 128, 512)):
                    dma("pool", wg.ap[:, :, o0:o0 + n], win_d[:, c0:c0 + n].rearrange("(k p) n -> p k n", p=128), wg, wr=[wg])
                for ti in range(9):
                    src, sl = (hHalo, slice(0, 128)) if ti == 0 else (hT, slice((ti - 1) * 128, ti * 128))
                    cur, prv = ti % 2, 1 - (ti % 2)
                    for kt in range(16):
                        mm(psum[0][:, :], psB[0], src.ap[:, kt, sl], wg.ap[:, kt, 0:512], [src, wg], start=(kt == 0), stop=(kt == 15))
                    for kt in range(16):
                        mm(psum[1][:, 0:128], psB[1], src.ap[:, kt, sl], wg.ap[:, kt, 512:640], [src, wg], start=(kt == 0), stop=(kt == 15))
                    for j in range(4):
                        act(sjunk.ap, psum[0][:, j * 128:(j + 1) * 128], AF.Square, [psB[0]], [sjunk, ssm], accum=ssm.ap[:, j:j + 1])
                    rstd_from_ssq(ssm.ap[:, 8:12], ssm.ap[:, 4:8], ssm.ap[:, 0:4], 128, [ssm], [ssm])
                    if ti > 0:
                        for j in range(3):
                            ts(qn.ap[:, j * 128:(j + 1) * 128], psum[0][:, j * 128:(j + 1) * 128], ssm.ap[:, 8 + j:9 + j], ALU.mult, [psB[0], ssm], [qn])
                    ts(kn.ap, psum[0][:, 384:512], ssm.ap[:, 11:12], ALU.mult, [psB[0], ssm], [kn])
                    cp(VT[cur].ap, psum[1][:, 0:128], [psB[1]], [VT[cur]], eng="act")
                    pb = psum[2][:].bitcast(BF16)
                    if ti > 0:
                        for j in range(3):
                            tr(pb[:, j * 128:(j + 1) * 128], psB[2], qn.ap[:, j * 128:(j + 1) * 128], identb, [qn, cb16])
                    tr(pb[:, 384:512], psB[2], kn.ap, identb, [kn, cb16])
                    if ti > 0:
                        ts(qTs.ap, pb[:, 0:384], cpk.ap[:, C_WQ:C_WQ + 1], ALU.mult, [psB[2], cpk], [qTs])
                    ts(KT[cur].ap, pb[:, 384:512], cpk.ap[:, C_WK:C_WK + 1], ALU.mult, [psB[2], cpk], [KT[cur]])
                    if ti == 0:
                        continue
                    for kb, (Kt, bank) in enumerate(((KT[prv], 3), (KT[cur], 4))):
                        mm(psum[bank][:, 0:384], psB[bank], Kt.ap, qTs.ap, [Kt, qTs])
                        bsel = 0 if kb == 1 else (2 if ti == 1 else 1)
                        stt(sc.ap, psum[bank][:, 0:384], float(128 ** -0.5), swab.ap[:, bsel, g * 384:(g + 1) * 384], ALU.mult, ALU.add,
                            [psB[bank], swab], [sc])
                        act(ETb[kb].ap, sc.ap, AF.Exp, [sc], [ETb[kb]])
                    mm(psum[5][:, 0:384], psB[5], VT[prv].ap, ETb[0].ap, [VT[prv], ETb[0]], start=True, stop=False)
                    mm(psum[5][:, 0:384], psB[5], VT[cur].ap, ETb[1].ap, [VT[cur], ETb[1]], start=False, stop=True)
                    mm(psum[6][:, 0:384], psB[6], onesb, ETb[0].ap, [cb16, ETb[0]], start=True, stop=False)
                    mm(psum[6][:, 0:384], psB[6], onesb, ETb[1].ap, [cb16, ETb[1]], start=False, stop=True)
                    for j in range(3):
                        ts(den.ap[:, j * 128:(j + 1) * 128], psum[6][:, j * 128:(j + 1) * 128], esink.ap[:, 3 * g + j:3 * g + j + 1], ALU.add,
                           [psB[6], esink], [den])
                    recip(den.ap, den.ap, [den], [den])
                    tt(oT.ap[:, 6 + 3 * g:9 + 3 * g, sl], psum[5][:, 0:384].rearrange("p (a b) -> p a b", a=3),
                       den.ap.rearrange("p (a b) -> p a b", a=3), ALU.mult, [psB[5], den], [oT])
            chk("swa", [oT.dump()])
            MA = Alloc(SA.p, AW)
            hmT = MA.get(2048, "hmT", BF16, shape=(16, 256))
            wm = MA.get(4096, "wm", BF16, shape=(16, 512))
            KM = MA.get(512, "KM", BF16, shape=(4, 256))
            VM = [MA.get(256, f"VM{i}", BF16) for i in range(2)]
            kmn = MA.get(256, "kmn", BF16)
            qmT = MA.get(256, "qmT", BF16)
            EM = MA.get(512, "EM", BF16)
            dnm = MA.get(512, "dnm")
            msm = MA.get(16, "msm")
            mjunk = MA.get(128, "mjunk")
            norm_T(mem_d, [(0, 0), (1, 1)], 1, hmT, Alloc(MA.p, AW))

            def headnorm_T(src_ps_bank, dst3d, wcol):
                for j in range(4):
                    act(mjunk.ap, psum[src_ps_bank][:, j * 128:(j + 1) * 128], AF.Square, [psB[src_ps_bank]], [mjunk, msm], accum=msm.ap[:, j:j + 1])
                rstd_from_ssq(msm.ap[:, 8:12], msm.ap[:, 4:8], msm.ap[:, 0:4], 128, [msm], [msm])
                for j in range(4):
                    ts(kmn.ap[:, j * 128:(j + 1) * 128], psum[src_ps_bank][:, j * 128:(j + 1) * 128], msm.ap[:, 8 + j:9 + j], ALU.mult,
                       [psB[src_ps_bank], msm], [kmn])
                pb = psum[3][:].bitcast(BF16)
                for j in range(4):
                    tr(pb[:, j * 128:(j + 1) * 128], psB[3], kmn.ap[:, j * 128:(j + 1) * 128], identb, [kmn, cb16])
                ts(dst3d[0], pb[:, 0:512].rearrange("p (a b) -> p a b", a=4), cpk.ap[:, wcol:wcol + 1], ALU.mult, [psB[3], cpk], [dst3d[1]])

            wload(wm, wmkv_d, 0, 512)
            for mt in range(2):
                for kt in range(16):
                    mm(psum[2][:, :], psB[2], hmT.ap[:, kt, mt * 128:(mt + 1) * 128], wm.ap[:, kt, :], [hmT, wm], start=(kt == 0), stop=(kt == 15))
                headnorm_T(2, (KM.ap[:, :, mt * 128:(mt + 1) * 128], KM), C_XK)
            wload(wm, wmkv_d, 512, 512)
            for mt in range(2):
                for kt in range(16):
                    mm(psum[2][:, :], psB[2], hmT.ap[:, kt, mt * 128:(mt + 1) * 128], wm.ap[:, kt, :], [hmT, wm], start=(kt == 0), stop=(kt == 15))
                cp(VM[mt].ap, psum[2][:, :], [psB[2]], [VM[mt]])
            wload(wm, win_d, OFF_MQ, 512)
            for t in range(8):
                tsl = slice(t * 128, (t + 1) * 128)
                for kt in range(16):
                    mm(psum[2][:, :], psB[2], hT.ap[:, kt, tsl], wm.ap[:, kt, :], [hT, wm], start=(kt == 0), stop=(kt == 15))
                headnorm_T(2, (qmT.ap.rearrange("p (a b) -> p a b", a=4), qmT), C_XQ)
                for mt in range(2):
                    for h in range(4):
                        mm(psum[4 + mt][:, h * 128:(h + 1) * 128], psB[4 + mt], KM.ap[:, h, mt * 128:(mt + 1) * 128], qmT.ap[:, h * 128:(h + 1) * 128], [KM, qmT])
                    act(EM.ap[:, mt * 512:(mt + 1) * 512], psum[4 + mt][:, :], AF.Exp, [psB[4 + mt]], [EM], scale=float(128 ** -0.5))
                for h in range(4):
                    for mt in range(2):
                        mm(psum[6][:, h * 128:(h + 1) * 128], psB[6], VM[mt].ap[:, h * 128:(h + 1) * 128], EM.ap[:, mt * 512 + h * 128:mt * 512 + (h + 1) * 128],
                           [VM[mt], EM], start=(mt == 0), stop=(mt == 1))
                for mt in range(2):
                    mm(psum[7][:, :], psB[7], onesb, EM.ap[:, mt * 512:(mt + 1) * 512], [cb16, EM], start=(mt == 0), stop=(mt == 1))
                recip(dnm.ap, psum[7][:, :], [psB[7]], [dnm])
                tt(oT.ap[:, 12:16, tsl], psum[6][:, :].rearrange("p (a b) -> p a b", a=4), dnm.ap.rearrange("p (a b) -> p a b", a=4), ALU.mult,
                   [psB[6], dnm], [oT])
            chk("mem", [oT.dump()])
            P.barrier()
            GA2 = Alloc(W0, AW)
            mT = GA2.get(8192, "mT", BF16, shape=(16, TOK))
            mslab = [GA2.get(4096, f"msl{i}", BF16, shape=(16, 512)) for i in range(3)]
            sg = [GA2.get(512, f"sg{i}") for i in range(6)]
            accs = [GA2.get(512, f"acc{i}") for i in range(2)]
            wo = [GA2.get(4096, "wo0", BF16, shape=(16, 512)), mslab[1]]
            brk = (range(0, 6), range(6, 12), range(12, 16))

            def load_mslab(mt):
                sl = mslab[mt % 3]
                dma("pool", sl.ap[:, :, 0:128], pall_d[:, mt * 128:(mt + 1) * 128].rearrange("(k p) n -> p k n", p=128), sl, wr=[sl])
                for br in range(3):
                    c0 = OFF_G + br * 2048 + mt * 128
                    dma("pool", sl.ap[:, :, 128 * (br + 1):128 * (br + 2)], win_d[:, c0:c0 + 128].rearrange("(k p) n -> p k n", p=128), sl, wr=[sl])

            load_mslab(0)
            load_mslab(1)
            pcount = 0
            for mt in range(16):
                if mt + 2 < 16:
                    load_mslab(mt + 2)
                elif mt + 2 < 18:
                    wload(wo[mt + 2 - 16], wout_d, (mt + 2 - 16) * 512, 512)
                sl = mslab[mt % 3]
                for c2 in range(2):
                    cols = slice(c2 * 512, (c2 + 1) * 512)
                    acc = accs[c2]
                    for br in range(3):
                        gb = c2 * 3 + br
                        pbk = 6 + (pcount % 2)
                        pcount += 1
                        sgb = sg[c2 * 3 + br]
                        for kt in range(16):
                            mm(psum[gb][:, :], psB[gb], sl.ap[:, kt, 128 * (br + 1):128 * (br + 2)], hT.ap[:, kt, cols], [sl, hT], start=(kt == 0), stop=(kt == 15))
                        act(sgb.ap, psum[gb][:, :], AF.Sigmoid, [psB[gb]], [sgb])
                        kts = list(brk[br])
                        for kt in kts:
                            mm(psum[pbk][:, :], psB[pbk], sl.ap[:, kt, 0:128], oT.ap[:, kt, cols], [sl, oT], start=(kt == kts[0]), stop=(kt == kts[-1]))
                        if br == 0:
                            tt(acc.ap, psum[pbk][:, :], sgb.ap, ALU.mult, [psB[pbk], sgb], [acc])
                        else:
                            tt(sgb.ap, psum[pbk][:, :], sgb.ap, ALU.mult, [psB[pbk], sgb], [sgb])
                            if br == 1:
                                tt(acc.ap, acc.ap, sgb.ap, ALU.add, [acc, sgb], [acc])
                            else:
                                tt(mT.ap[:, mt, cols], acc.ap, sgb.ap, ALU.add, [acc, sgb], [mT])
            chk("merge", [mT.dump()])
            x1 = T(arena[:, 4096:4096 + 16384].rearrange("p (t d) -> p t d", t=8), "x1", 4096, 16384)
            x1t = [Buf(f"x1t{t}") for t in range(8)]
            for t in range(8):
                dma("sp", x1.ap[:, t, :], x_d[t * 128:(t + 1) * 128, :], x1t[t], wr=[x1t[t], hT, oT, hP])
            pc = 0
            for cc in range(4):
                w = wo[cc % 2]
                if cc >= 2:
                    wload(w, wout_d, cc * 512, 512)
                for t in range(8):
                    bank = pc % 6
                    pc += 1
                    for kt in range(16):
                        mm(psum[bank][:, :], psB[bank], mT.ap[:, kt, t * 128:(t + 1) * 128], w.ap[:, kt, :], [mT, w], start=(kt == 0), stop=(kt == 15))
                    tt(x1.ap[:, t, cc * 512:(cc + 1) * 512], psum[bank][:, :], x1.ap[:, t, cc * 512:(cc + 1) * 512], ALU.add, [psB[bank], x1t[t]], [x1t[t]])
            chk("x1", [(x1t[t], 4096 + t * 2048, 2048) for t in range(8)])
            P.barrier()
            ML = Alloc(W0, AW)
            h2T = ML.get(8192, "h2T", BF16, shape=(16, TOK))
            wu = [ML.get(4096, f"wu{i}", BF16, shape=(16, 512)) for i in range(2)]
            wd = [ML.get(4096, f"wd{i}", BF16, shape=(4, 2048)) for i in range(2)]
            aT = [ML.get(2048, f"aT{i}", BF16, shape=(4, TOK)) for i in range(2)]
            rl = [ML.get(512, f"rl{i}") for i in range(2)]

            def load_wu(fb):
                wload(wu[fb % 2], wup_d, fb * 512, 512)

            def load_wd(fb):
                dma("pool", wd[fb % 2].ap, wdn_d[fb * 512:(fb + 1) * 512, :].rearrange("(k p) n -> p k n", p=128), wd[fb % 2], wr=[wd[fb % 2]])

            load_wu(0)
            load_wd(0)
            norm_T(None, [(t, t) for t in range(8)], 2, h2T, Alloc(ML.p, AW), src_sb=lambda t: (x1t[t], x1.ap[:, t, :]))
            upc = [0]

            def mlp_up(fb):
                w = wu[fb % 2]
                a = aT[fb % 2]
                for f4 in range(4):
                    for tc in range(2):
                        bank = upc[0] % 2
                        r = rl[upc[0] % 2]
                        upc[0] += 1
                        for kt in range(16):
                            mm(psum[bank][:, :], psB[bank], w.ap[:, kt, f4 * 128:(f4 + 1) * 128], h2T.ap[:, kt, tc * 512:(tc + 1) * 512], [w, h2T],
                               start=(kt == 0), stop=(kt == 15))
                        act(r.ap, psum[bank][:, :], AF.Relu, [psB[bank]], [r])
                        tt(a.ap[:, f4, tc * 512:(tc + 1) * 512], r.ap, r.ap, ALU.mult, [r], [a])

            dnc = [0]

            def mlp_down(fb):
                w = wd[fb % 2]
                a = aT[fb % 2]
                for t in range(8):
                    for cc in range(4):
                        bank = 2 + dnc[0] % 6
                        dnc[0] += 1
                        for k in range(4):
                            mm(psum[bank][:, :], psB[bank], a.ap[:, k, t * 128:(t + 1) * 128], w.ap[:, k, cc * 512:(cc + 1) * 512], [a, w],
                               start=(k == 0), stop=(k == 3))
                        tt(x1.ap[:, t, cc * 512:(cc + 1) * 512], psum[bank][:, :], x1.ap[:, t, cc * 512:(cc + 1) * 512], ALU.add, [psB[bank], x1t[t]], [x1t[t]])

            for fb in range(16):
                if fb + 1 < 16:
                    load_wu(fb + 1)
                mlp_up(fb)
                if fb > 0:
                    mlp_down(fb - 1)
                if fb + 1 < 16:
                    load_wd(fb + 1)
            mlp_down(15)
            for t in range(8):
                dma("sp", y_d[t * 128:(t + 1) * 128, :], x1.ap[:, t, :], x1t[t], rd=[x1t[t]])
        except StopBuild:
            pass
        return finish(nc, P, st, dbg_d, dumps, dma, locals())


def finish(nc, P, st, dbg_d, dumps, dma, L):
    if dbg_d is not None:
        off = 0
        arena = L["arena"]
        for (t, a0, words) in dumps:
            dma("sp", dbg_d[:, off:off + words], arena[:, a0:a0 + words], t, rd=[t])
            off += words
    P.final_wait()
    P.emit(nc)
    return nc


def _t5_bucket(dist):
    import math
    n = np.maximum(dist, 0)
    max_exact = 16
    nf = np.maximum(n, 1).astype(np.float32)
    large = max_exact + (np.log(nf / max_exact) / math.log(128 / max_exact) * (32 - max_exact)).astype(np.int32)
    large = np.minimum(large, 31)
    return np.where(n < max_exact, n, large)


def make_in_maps(inp):
    f32 = np.float32
    x = np.asarray(inp["x"], f32)
    mem = np.asarray(inp["mem"], f32)
    w_in = np.ascontiguousarray(np.asarray(inp["w_in"], f32)[0])
    w_mem_kv = np.ascontiguousarray(np.asarray(inp["w_mem_kv"], f32)[0])
    p_all = np.ascontiguousarray(np.concatenate([np.asarray(inp["p_dn"], f32)[0], np.asarray(inp["p_swa"], f32)[0],
                                                 np.asarray(inp["p_mem"], f32)[0]], axis=0))
    w_out = np.ascontiguousarray(np.asarray(inp["w_out"], f32)[0])
    w_up = np.ascontiguousarray(np.asarray(inp["w_mlp_up"], f32)[0])
    w_down = np.ascontiguousarray(np.asarray(inp["w_mlp_down"], f32)[0])
    nw3 = np.ascontiguousarray(np.stack([np.asarray(inp["attn_norm_w"], f32)[0], np.asarray(inp["mem_norm_w"], f32)[0],
                                         np.asarray(inp["mlp_norm_w"], f32)[0]], axis=0))
    cp = np.zeros((128, NCP), f32)
    idx = np.arange(128)
    cp[:, C_ID:C_ID + 128] = np.eye(128, dtype=f32)
    cp[:, C_U:C_U + 128] = (idx[:, None] <= idx[None, :]).astype(f32)
    cp[:, C_SL:C_SL + 128] = (idx[:, None] > idx[None, :]).astype(f32)
    cp[:, C_NSL:C_NSL + 128] = -(idx[:, None] > idx[None, :]).astype(f32)
    cp[:, C_ONE:C_ONE + 128] = 1.0
    cw = np.asarray(inp["dn_conv_w"], f32)[0]
    cp[:, C_CONV:C_CONV + 72] = cw.reshape(4, 18, 128).transpose(2, 1, 0).reshape(128, 72)
    cp[:, C_DNW] = np.asarray(inp["dn_out_norm_w"], f32)[0]
    cp[:, C_WQ] = np.asarray(inp["swa_q_norm_w"], f32)[0]
    cp[:, C_WK] = np.asarray(inp["swa_k_norm_w"], f32)[0]
    cp[:, C_XQ] = np.asarray(inp["xq_norm_w"], f32)[0]
    cp[:, C_XK] = np.asarray(inp["xk_norm_w"], f32)[0]
    cp[:, C_ALOG:C_ALOG + 6] = np.asarray(inp["dn_A_log"], f32)[0][None, :]
    cp[:, C_DTB:C_DTB + 6] = np.asarray(inp["dn_dt_bias"], f32)[0][None, :]
    cp[:, C_SINK:C_SINK + 6] = np.asarray(inp["swa_sinks"], f32)[0][None, :]
    cb16 = np.concatenate([np.eye(128, dtype=f32), np.ones((128, 128), f32)], axis=1).astype(ml_dtypes.bfloat16)
    rb = np.asarray(inp["rel_bias"], f32)
    qi = np.arange(128)[:, None]
    kj = np.arange(256)[None, :]
    dist = qi - kj + 128
    valid = (dist >= 0) & (dist < 128)
    bias = rb[_t5_bucket(dist)]
    bias = np.where(valid[:, :, None], bias, f32(NEG)).astype(f32)
    biasT = bias.transpose(1, 2, 0)
    b_prev = np.ascontiguousarray(biasT[0:128].reshape(128, 768))
    b_cur = np.ascontiguousarray(biasT[128:256].reshape(128, 768))
    b_none = np.full((128, 768), NEG, f32)
    maps = []
    for c in range(8):
        b, hf = c // 2, c % 2
        xo = np.ascontiguousarray(x[b, hf * TOK:(hf + 1) * TOK])
        xp = np.ascontiguousarray(x[b, 0:TOK]) if hf == 1 else np.zeros((TOK, D), f32)
        swab = np.stack([b_cur, b_prev, b_prev if hf == 1 else b_none], axis=0)
        maps.append({"x": xo, "xp": xp, "mem": np.ascontiguousarray(mem[b]), "w_in": w_in, "w_mem_kv": w_mem_kv,
                     "p_all": p_all, "w_out": w_out, "w_up": w_up, "w_down": w_down, "nw3": nw3, "cpack": cp,
                     "cb16": cb16, "swab": np.ascontiguousarray(swab)})
    return maps


_NC_CACHE = {}


def kernel(**inputs):
    maps = make_in_maps(inputs)
    if "nc" not in _NC_CACHE:
        _NC_CACHE["nc"] = build()
    res = run_bass_kernel_spmd(_NC_CACHE["nc"], maps, core_ids=list(range(8)))
    out = np.zeros((4, 2048, D), np.float32)
    for c in range(8):
        b, hf = c // 2, c % 2
        out[b, hf * TOK:(hf + 1) * TOK] = res.results[c]["y"]
    return out
```

```python
import os
import numpy as np
import ml_dtypes
from contextlib import ExitStack
import concourse.bass as bass
import concourse.mybir as mybir
from concourse.bass_utils import run_bass_kernel_spmd

F32 = mybir.dt.float32
BF16 = mybir.dt.bfloat16
AF = mybir.ActivationFunctionType
ALU = mybir.AluOpType

EPS = 1e-6
D = 2048
TOK = 1024
NT = 8
OFF_QKV, OFF_Z, OFF_B, OFF_A, OFF_SQ, OFF_SK, OFF_SV, OFF_MQ, OFF_G = 0, 2304, 3072, 3078, 3084, 3852, 4108, 4364, 4876
IN_W = 11020
NEG = -30000.0

C_ID, C_U, C_SL, C_NSL, C_ONE, C_CONV, C_DNW, C_WQ, C_WK, C_XQ, C_XK, C_ALOG, C_DTB, C_SINK, NCP = \
    0, 128, 256, 384, 512, 640, 712, 713, 714, 715, 716, 717, 723, 729, 736


class Buf:
    __slots__ = ("name", "lw", "rd")

    def __init__(self, name):
        self.name = name
        self.lw = None
        self.rd = {}


class Prog:
    ENG = ("pe", "act", "dve", "pool", "sp")

    def __init__(self):
        self.ops = {e: [] for e in self.ENG}
        self.cnt = {e: 0 for e in self.ENG}
        self.waited = {e: {} for e in self.ENG}
        self.dcnt = {}
        self.swq = []

    def _deps(self, reads, writes):
        toks = []
        for b in reads:
            if b.lw is not None:
                toks.append(b.lw)
        for b in writes:
            if b.lw is not None:
                toks.append(b.lw)
            toks.extend(b.rd.items())
        return toks

    def _filter(self, eng, toks):
        w = self.waited[eng]
        out = {}
        for (k, v) in toks:
            if eng == "pe" and k == "Epe":
                continue
            if w.get(k, 0) >= v:
                continue
            if out.get(k, 0) < v:
                out[k] = v
        for k, v in out.items():
            w[k] = v
        return list(out.items())

    def _record(self, tok, reads, writes):
        for b in writes:
            b.lw = tok
            b.rd = {}
        for b in reads:
            if b in writes:
                continue
            if b.rd.get(tok[0], 0) < tok[1]:
                b.rd[tok[0]] = tok[1]

    def op(self, eng, fn, reads=(), writes=()):
        extra = [b for b in reads if b.name.startswith("ps") and b not in writes]
        if extra:
            writes = list(writes) + extra
        waits = self._filter(eng, self._deps(reads, writes))
        self.cnt[eng] += 1
        tok = ("E" + eng, self.cnt[eng])
        self.ops[eng].append((waits, fn, ("E" + eng, 1)))
        self._record(tok, reads, writes)
        return tok

    def dma(self, eng, fn, sembuf, reads=(), writes=(), ndesc=0):
        toks = self._deps(reads, writes)
        if eng == "pool" and ndesc:
            while self.swq and sum(n for _, n in self.swq) + ndesc > 640:
                toks.append(self.swq.pop(0)[0])
        waits = self._filter(eng, toks)
        key = "D" + sembuf.name
        self.dcnt[key] = self.dcnt.get(key, 0) + 16
        tok = (key, self.dcnt[key])
        self.ops[eng].append((waits, fn, (key, 16)))
        self._record(tok, reads, writes)
        if eng == "pool" and ndesc:
            self.swq.append((tok, ndesc))
        return tok

    def barrier(self):
        toks = [("E" + e, self.cnt[e]) for e in self.ENG if self.cnt[e] > 0] + list(self.dcnt.items())
        for e in self.ENG:
            waits = self._filter(e, toks)
            if waits:
                self.ops[e].append((waits, None, None))

    def final_wait(self, eng="sp"):
        waits = self._filter(eng, list(self.dcnt.items()))
        self.ops[eng].append((waits, None, None))

    def emit(self, nc):
        keys = ["E" + e for e in self.ENG] + sorted(self.dcnt.keys())
        with ExitStack() as st:
            sems = {}
            for k in keys:
                sems[k] = st.enter_context(nc.semaphore("s_" + k))
            block = st.enter_context(nc.Block())
            binders = {"pe": block.tensor, "act": block.scalar, "dve": block.vector,
                       "pool": block.gpsimd, "sp": block.sync}
            for eng in self.ENG:
                ops = self.ops[eng]

                def body(e, ops=ops):
                    for waits, fn, inc in ops:
                        for k, v in waits:
                            e.wait_ge(sems[k], v)
                        if fn is None:
                            continue
                        ins = fn(e)
                        ins.then_inc(sems[inc[0]], inc[1])
                binders[eng](body)


class StopBuild(Exception):
    pass


class T:
    __slots__ = ("ap", "b", "off", "words")

    def __init__(self, ap, name, off=None, words=None):
        self.ap = ap
        self.b = Buf(name)
        self.off = off
        self.words = words

    def dump(self):
        return (self, self.off, self.words)


def build(stop_after=None, dbg=None):
    nc = bass.Bass("TRN2", target_bir_lowering=False)
    P = Prog()
    dr = {}

    def din(name, shape, dt=F32):
        dr[name] = nc.dram_tensor(name, shape, dt, kind="ExternalInput").ap()
        return dr[name]

    x_d = din("x", [TOK, D])
    xp_d = din("xp", [TOK, D])
    mem_d = din("mem", [256, D])
    win_d = din("w_in", [D, IN_W])
    wmkv_d = din("w_mem_kv", [D, 1024])
    pall_d = din("p_all", [D, D])
    wout_d = din("w_out", [D, D])
    wup_d = din("w_up", [D, 4 * D])
    wdn_d = din("w_down", [4 * D, D])
    nw3_d = din("nw3", [3, D])
    cpack_d = din("cpack", [128, NCP])
    cb16_d = din("cb16", [128, 256], BF16)
    swab_d = din("swab", [3, 128, 768])
    y_d = nc.dram_tensor("y", [TOK, D], F32, kind="ExternalOutput").ap()
    dbg_d = None
    if dbg is not None:
        dbg_d = nc.dram_tensor("dbg", [128, dbg], F32, kind="ExternalOutput").ap()

    with ExitStack() as st:
        AW = 52800
        arena = st.enter_context(nc.sbuf_tensor("arena", [128, AW], F32))
        psum = [st.enter_context(nc.psum_tensor(f"ps{i}", [128, 512], F32)) for i in range(8)]
        psB = [[Buf(f"ps{i}")] * 4 for i in range(8)]

        uid = [0]

        def view(off, words, name, dt=F32, shape=None):
            ap = arena[:, off:off + words]
            if dt == BF16:
                ap = ap.bitcast(BF16)
            if shape is not None:
                ap = ap[:, 0:int(np.prod(shape))]
                if len(shape) == 2:
                    ap = ap.rearrange("p (a b) -> p a b", a=shape[0])
                elif len(shape) == 3:
                    ap = ap.rearrange("p (a b c) -> p a b c", a=shape[0], b=shape[1])
            uid[0] += 1
            return T(ap, f"{name}_{uid[0]}", off, words)

        class Alloc:
            def __init__(self, lo, hi):
                self.lo, self.hi, self.p = lo, hi, lo

            def get(self, words, name, dt=F32, shape=None):
                o = self.p
                self.p += words
                assert self.p <= self.hi, (name, self.p, self.hi)
                return view(o, words, name, dt, shape)

        def bl(ts):
            out = []
            for t in ts:
                if isinstance(t, T):
                    out.append(t.b)
                elif isinstance(t, Buf):
                    out.append(t)
                elif isinstance(t, (list, tuple)):
                    out.extend(bl(t))
            return out

        def mm(out_ap, outb, lhsT, rhs, rd, start=True, stop=True):
            P.op("pe", lambda e: e.matmul(out_ap, lhsT=lhsT, rhs=rhs, start=start, stop=stop), reads=bl(rd), writes=bl([outb]))

        def tr(out_ap, outb, in_ap, ident_ap, rd):
            P.op("pe", lambda e: e.transpose(out=out_ap, in_=in_ap, identity=ident_ap), reads=bl(rd), writes=bl([outb]))

        def act(out_ap, in_ap, func, rd, wr, scale=None, bias=None, accum=None, eng="act"):
            kw = {}
            if scale is not None:
                kw["scale"] = scale
            if bias is not None:
                kw["bias"] = bias
            if accum is not None:
                kw["accum_out"] = accum
            P.op("act", lambda e: e.activation(out=out_ap, in_=in_ap, func=func, **kw), reads=bl(rd), writes=bl(wr))

        def tt(out_ap, a, b, op, rd, wr, eng="dve"):
            P.op(eng, lambda e: e.tensor_tensor(out=out_ap, in0=a, in1=b, op=op), reads=bl(rd), writes=bl(wr))

        def ts(out_ap, a, s1, op0, rd, wr, s2=None, op1=None, eng="dve"):
            if op1 is None:
                P.op(eng, lambda e: e.tensor_scalar(out=out_ap, in0=a, scalar1=s1, scalar2=None, op0=op0), reads=bl(rd), writes=bl(wr))
            else:
                P.op(eng, lambda e: e.tensor_scalar(out=out_ap, in0=a, scalar1=s1, scalar2=s2, op0=op0, op1=op1), reads=bl(rd), writes=bl(wr))

        def stt(out_ap, a, s, b, op0, op1, rd, wr):
            P.op("dve", lambda e: e.scalar_tensor_tensor(out=out_ap, in0=a, scalar=s, in1=b, op0=op0, op1=op1), reads=bl(rd), writes=bl(wr))

        def cp(out_ap, in_ap, rd, wr, eng="dve"):
            if eng == "act":
                P.op("act", lambda e: e.copy(out=out_ap, in_=in_ap), reads=bl(rd), writes=bl(wr))
            else:
                P.op(eng, lambda e: e.tensor_copy(out=out_ap, in_=in_ap), reads=bl(rd), writes=bl(wr))

        def recip(out_ap, in_ap, rd, wr):
            P.op("dve", lambda e: e.reciprocal(out=out_ap, in_=in_ap), reads=bl(rd), writes=bl(wr))

        def dma(q, out_ap, in_ap, semT, rd=(), wr=()):
            nd = 0
            if q == "pool":
                shp = list(out_ap.shape)
                nd = int(np.prod(shp[:-1])) // 16 + 2
            P.dma(q, lambda e: e.dma_start(out=out_ap, in_=in_ap), semT.b if isinstance(semT, T) else semT, reads=bl(rd), writes=bl(wr), ndesc=nd)

        dumps = []

        def chk(name, dl):
            if stop_after == name:
                dumps.extend(dl)
                raise StopBuild()

        def wload(slab, w_dram, c0, n, kt=16, r0=0):
            dma("pool", slab.ap, w_dram[r0:r0 + kt * 128, c0:c0 + n].rearrange("(k p) n -> p k n", p=128), slab, wr=[slab])

        def rstd_from_ssq(out_ap, tmp_ap, ssq_ap, n, rd, wr):
            act(tmp_ap, ssq_ap, AF.Ln, rd, wr, scale=1.0 / n, bias=EPS)
            act(out_ap, tmp_ap, AF.Exp, wr, wr, scale=-0.5)

        CONST = Alloc(0, 4096)
        cpk = CONST.get(NCP, "cpack")
        cb16 = CONST.get(128, "cb16", BF16)
        wbc = CONST.get(2048, "wbc")
        smalls = CONST.get(64, "smalls")
        identf = cpk.ap[:, C_ID:C_ID + 128]
        Umat = cpk.ap[:, C_U:C_U + 128]
        SLm = cpk.ap[:, C_SL:C_SL + 128]
        NSLm = cpk.ap[:, C_NSL:C_NSL + 128]
        onesf = cpk.ap[:, C_ONE:C_ONE + 128]
        identb = cb16.ap[:, 0:128]
        onesb = cb16.ap[:, 128:256]
        hT = view(4096, 8192, "hT", BF16, shape=(16, TOK))
        hP = view(4096 + 8192, 8192, "hP", BF16, shape=(16, TOK))
        oT = T(hP.ap, "oT", hP.off, hP.words)
        W0 = 4096 + 16384

        dma("sp", cpk.ap, cpack_d[:, :], cpk, wr=[cpk])
        dma("sp", cb16.ap, cb16_d[:, :], cb16, wr=[cb16])

        def norm_T(src_dram, tiles, nw_row, dst, WA, src_sb=None, load_w=True):
            xt = [WA.get(2048, "xt") for _ in range(2)] if src_sb is None else None
            junk = WA.get(1024, "junk", BF16)
            xn = WA.get(1024, "xn", BF16)
            stat = WA.get(8, "stat")
            if load_w:
                dma("sp", wbc.ap, nw3_d[nw_row].partition_broadcast(128), wbc, wr=[wbc])
            for i, (t, dt_) in enumerate(tiles):
                if src_sb is None:
                    xb = xt[i % 2]
                    dma("sp", xb.ap, src_dram[t * 128:(t + 1) * 128, :], xb, wr=[xb])
                    xin = xb.ap
                else:
                    xb, xin = src_sb(t)
                act(junk.ap, xin, AF.Square, [xb], [junk, stat], accum=stat.ap[:, 0:1])
                rstd_from_ssq(stat.ap[:, 2:3], stat.ap[:, 1:2], stat.ap[:, 0:1], D, [stat], [stat])
                stt(xn.ap, xin, stat.ap[:, 2:3], wbc.ap, ALU.mult, ALU.mult, [xb, stat, wbc], [xn])
                for half in range(2):
                    bank = half
                    pb = psum[bank][:].bitcast(BF16)
                    for j in range(8):
                        kt = half * 8 + j
                        tr(pb[:, j * 128:(j + 1) * 128], psB[bank], xn.ap[:, kt * 128:(kt + 1) * 128], identb, [xn, cb16])
                    cp(dst.ap[:, half * 8:(half + 1) * 8, dt_ * 128:(dt_ + 1) * 128], pb.rearrange("p (k t) -> p k t", k=8),
                       [psB[bank]], [dst], eng=("dve" if half == 0 else "act"))

        WA = Alloc(W0, AW)
        norm_T(x_d, [(t, t) for t in range(NT)], 0, hT, WA)
        norm_T(xp_d, [(t, t) for t in range(NT)], 0, hP, Alloc(WA.p, AW), load_w=False)
        hHalo = CONST.get(1024, "hHalo", BF16, shape=(16, 128))
        cp(hHalo.ap, hP.ap[:, :, 896:1024], [hP], [hHalo])
        if stop_after == "p0":
            return finish(nc, P, st, dbg_d, [(hT, 4096, 8192), (hP, 4096 + 8192, 8192)], dma, locals())
        P.barrier()

        GA = Alloc(W0, AW)
        qkvT = GA.get(9 * 1024, "qkvT", shape=(9, TOK))
        zs = GA.get(3 * 1024, "zs", shape=(3, TOK))
        raw = [GA.get(1028, "raw") for _ in range(2)]
        wsl = [GA.get(1024, "wsl", BF16, shape=(16, 128)) for _ in range(3)]
        wba = GA.get(128, "wba", BF16, shape=(16, 12))
        ba_all = GA.get(16 * 12, "ba_all", shape=(16, 12))
        beta_all = GA.get(16 * 6, "beta_all", shape=(16, 6))
        g_all = GA.get(16 * 6, "g_all", shape=(16, 6))
        sp_tmp = GA.get(16 * 6, "sp_tmp", shape=(16, 6))
        nexpA = GA.get(8, "nexpA")
        Sst = [GA.get(128, f"S{h}") for h in range(6)]
        carry = GA.get(64, "carry", shape=(18, 3))
        HT = []
        names = ["Xk", "kd", "Xv", "qg", "gSL", "gSLc", "eE", "eET", "t1", "pT", "R0", "R1", "RT0", "RT1", "Y", "wTn", "qgT", "delta", "on"]
        for h in range(3):
            HT.append({n: GA.get(128, f"{n}{h}") for n in names})
        csm = GA.get(64, "csm")
        csmq = GA.get(32, "csmq")
        wsl_i = [0]

        wload_ba = lambda: dma("pool", wba.ap[:, :, 0:12], win_d[:, OFF_B:OFF_B + 12].rearrange("(k p) n -> p k n", p=128), wba, wr=[wba])
        wload_ba()
        for tt_i in range(16):
            src = hP if tt_i < 8 else hT
            tl = tt_i % 8
            for kt in range(16):
                mm(psum[7][:, 0:12], psB[7][0], src.ap[:, kt, tl * 128:(tl + 1) * 128], wba.ap[:, kt, 0:12], [src, wba], start=(kt == 0), stop=(kt == 15))
            cp(ba_all.ap[:, tt_i, :], psum[7][:, 0:12], [psB[7][0]], [ba_all])
        act(beta_all.ap, ba_all.ap[:, :, 0:6], AF.Exp, [ba_all], [beta_all], scale=-1.0)
        ts(beta_all.ap, beta_all.ap, 1.0, ALU.add, [beta_all], [beta_all])
        recip(beta_all.ap, beta_all.ap, [beta_all], [beta_all])
        act(nexpA.ap[:, 0:6], cpk.ap[:, C_ALOG:C_ALOG + 6], AF.Exp, [cpk], [nexpA])
        for tt_i in range(16):
            tt(sp_tmp.ap[:, tt_i, :], ba_all.ap[:, tt_i, 6:12], cpk.ap[:, C_DTB:C_DTB + 6], ALU.add, [ba_all, cpk], [sp_tmp])
        act(sp_tmp.ap, sp_tmp.ap, AF.Exp, [sp_tmp], [sp_tmp])
        act(sp_tmp.ap, sp_tmp.ap, AF.Ln, [sp_tmp], [sp_tmp], bias=1.0)
        for tt_i in range(16):
            stt(g_all.ap[:, tt_i, :], sp_tmp.ap[:, tt_i, :], -1.0, nexpA.ap[:, 0:6], ALU.mult, ALU.mult, [sp_tmp, nexpA], [g_all])
        if stop_after == "ba":
            return finish(nc, P, st, dbg_d, [ba_all.dump(), beta_all.dump(), g_all.dump()], dma, locals())
        for h in range(6):
            P.op("dve", lambda e, h=h: e.memset(Sst[h].ap, 0.0), writes=[Sst[h].b])
        P.op("dve", lambda e: e.memset(carry.ap, 0.0), writes=[carry.b])

        def gdn_proj(hh, stage):
            src = hP if stage == 0 else hT
            fts = [hh * 3 + j for j in range(3)] + [6 + hh * 3 + j for j in range(3)] + [12 + hh * 3 + j for j in range(3)]
            for li, ft in enumerate(fts):
                w = wsl[wsl_i[0] % 3]
                wsl_i[0] += 1
                wload(w, win_d, OFF_QKV + ft * 128, 128)
                rb = raw[li % 2]
                for c2 in range(2):
                    bank = 2 + c2
                    for kt in range(16):
                        mm(psum[bank][:, :], psB[bank], w.ap[:, kt, :], src.ap[:, kt, c2 * 512:(c2 + 1) * 512], [w, src], start=(kt == 0), stop=(kt == 15))
                    cp(rb.ap[:, 3 + c2 * 512:3 + (c2 + 1) * 512], psum[bank][:, :], [psB[bank]], [rb], eng=("act" if c2 == 0 else "dve"))
                cp(rb.ap[:, 0:3], carry.ap[:, ft, :], [carry], [rb])
                cp(carry.ap[:, ft, :], rb.ap[:, 1024:1027], [rb], [carry])
                cw = cpk.ap[:, C_CONV + ft * 4:C_CONV + ft * 4 + 4]
                acc = qkvT.ap[:, li, :]
                ts(acc, rb.ap[:, 0:1024], cw[:, 0:1], ALU.mult, [rb, cpk], [qkvT])
                for k in range(1, 4):
                    stt(acc, rb.ap[:, k:k + 1024], cw[:, k:k + 1], acc, ALU.mult, ALU.add, [rb, cpk, qkvT], [qkvT])
                act(acc, acc, AF.Silu, [qkvT], [qkvT])
            if stage == 1:
                for j in range(3):
                    w = wsl[wsl_i[0] % 3]
                    wsl_i[0] += 1
                    wload(w, win_d, OFF_Z + (hh * 3 + j) * 128, 128)
                    for c2 in range(2):
                        bank = 2 + c2
                        for kt in range(16):
                            mm(psum[bank][:, :], psB[bank], w.ap[:, kt, :], hT.ap[:, kt, c2 * 512:(c2 + 1) * 512], [w, hT], start=(kt == 0), stop=(kt == 15))
                        act(zs.ap[:, j, c2 * 512:(c2 + 1) * 512], psum[bank][:, :], AF.Silu, [psB[bank]], [zs])

        def gdn_chunks(hh, stage):
            own = stage == 1
            for c in range(8):
                tt_i = stage * 8 + c
                cs = slice(c * 128, (c + 1) * 128)
                h0 = hh * 3
                g3 = g_all.ap[:, tt_i, h0:h0 + 3]
                b3 = beta_all.ap[:, tt_i, h0:h0 + 3]
                mm(psum[0][:, 0:3], psB[0][0], Umat, g3, [cpk, g_all])
                mm(psum[0][:, 3:6], psB[0][0], SLm, g3, [cpk, g_all])
                mm(psum[0][:, 6:9], psB[0][0], onesf, g3, [cpk, g_all])
                act(csm.ap[:, 0:9], psum[0][:, 0:9], AF.Exp, [psB[0][0]], [csm])
                eG, eGr, gc = csm.ap[:, 0:3], csm.ap[:, 3:6], csm.ap[:, 6:9]
                chk("c0a", [csm.dump()])
                for h in range(3):
                    bank = 1
                    for j in range(3):
                        tr(psum[bank][:, j * 128:(j + 1) * 128], psB[bank][j], qkvT.ap[:, j * 3 + h, cs], identf, [qkvT, cpk])
                    Hh = HT[h]
                    if os.environ.get("GSKIP") == "tr":
                        chk("c0b", [csm.dump()])
                    act(Hh["t1"].ap, psum[bank][:, 0:128], AF.Square, [psB[bank][0]], [Hh["t1"], csmq], accum=csmq.ap[:, h:h + 1])
                    if os.environ.get("GSKIP") == "sq":
                        chk("c0b", [csm.dump()])
                    act(Hh["t1"].ap, psum[bank][:, 128:256], AF.Square, [psB[bank][1]], [Hh["t1"], csmq], accum=csmq.ap[:, 3 + h:4 + h])
                    if h == 2:
                        act(csmq.ap[:, 6:12], csmq.ap[:, 0:6], AF.Ln, [csmq], [csmq], bias=EPS)
                    cp(Hh["qg"].ap, psum[bank][:, 0:128], [psB[bank][0]], [Hh["qg"]], eng="dve")
                    cp(Hh["Xk"].ap, psum[bank][:, 128:256], [psB[bank][1]], [Hh["Xk"]], eng="act")
                    cp(Hh["Xv"].ap, psum[bank][:, 256:384], [psB[bank][2]], [Hh["Xv"]], eng="dve")
                    if os.environ.get("GSKIP") == "cp" + str(h):
                        chk("c0b", [csm.dump()])
                chk("c0b", [csm.dump(), csmq.dump()])
                act(csmq.ap[:, 12:15], csmq.ap[:, 6:9], AF.Exp, [csmq], [csmq], scale=-0.5)
                act(csm.ap[:, 12:15], csmq.ap[:, 9:12], AF.Exp, [csmq], [csm], scale=-0.5)
                ts(csm.ap[:, 15:18], csmq.ap[:, 9:12], -0.5, ALU.mult, [csmq], [csm])
                rk, lnrk = csm.ap[:, 12:15], csm.ap[:, 15:18]
                tt(csm.ap[:, 18:21], b3, rk, ALU.mult, [beta_all, csm], [csm])
                tt(csm.ap[:, 21:24], csm.ap[:, 18:21], eG, ALU.mult, [csm], [csm])
                tt(csm.ap[:, 24:27], rk, eGr, ALU.mult, [csm], [csm])
                ts(csm.ap[:, 27:30], csmq.ap[:, 12:15], float(128 ** -0.5), ALU.mult, [csmq], [csm])
                sA, sXk, skd, so = csm.ap[:, 18:21], csm.ap[:, 21:24], csm.ap[:, 24:27], csm.ap[:, 27:30]
                for h in range(3):
                    Hh = HT[h]
                    hcol = slice(h, h + 1)
                    ts(Hh["kd"].ap, Hh["Xk"].ap, skd[:, hcol], ALU.mult, [Hh["Xk"], csm], [Hh["kd"]])
                    ts(Hh["Xk"].ap, Hh["Xk"].ap, sXk[:, hcol], ALU.mult, [Hh["Xk"], csm], [Hh["Xk"]])
                    ts(Hh["Xv"].ap, Hh["Xv"].ap, b3[:, hcol], ALU.mult, [Hh["Xv"], beta_all], [Hh["Xv"]])
                    if own:
                        ts(Hh["qg"].ap, Hh["qg"].ap, eG[:, hcol], ALU.mult, [Hh["qg"], csm], [Hh["qg"]])
                    ts(Hh["gSL"].ap, SLm, g3[:, hcol], ALU.mult, [cpk, g_all], [Hh["gSL"]])
                    stt(Hh["gSLc"].ap, identf, lnrk[:, hcol], Hh["gSL"].ap, ALU.mult, ALU.add, [cpk, csm, Hh["gSL"]], [Hh["gSLc"]])
                for h in range(3):
                    Hh = HT[h]
                    hcol = slice(h, h + 1)
                    kTr = qkvT.ap[:, 3 + h, cs]
                    qTr = qkvT.ap[:, h, cs]
                    bank = 2 + 2 * h
                    mm(psum[bank][:, 0:128], psB[bank][0], kTr, kTr, [qkvT])
                    mm(psum[bank][:, 128:256], psB[bank][1], Umat, Hh["gSLc"].ap, [cpk, Hh["gSLc"]])
                    if own:
                        mm(psum[bank][:, 256:384], psB[bank][2], kTr, qTr, [qkvT])
                        mm(psum[bank][:, 384:512], psB[bank][3], Hh["gSL"].ap, Umat, [cpk, Hh["gSL"]])
                    act(Hh["eE"].ap, psum[bank][:, 128:256], AF.Exp, [psB[bank][1]], [Hh["eE"]])
                    tt(Hh["t1"].ap, psum[bank][:, 0:128], Hh["eE"].ap, ALU.mult, [psB[bank][0], Hh["eE"]], [Hh["t1"]])
                    stt(Hh["RT0"].ap, Hh["t1"].ap, sA[:, hcol], NSLm, ALU.mult, ALU.mult, [Hh["t1"], csm, cpk], [Hh["RT0"]])
                    if own:
                        act(Hh["eET"].ap, psum[bank][:, 384:512], AF.Exp, [psB[bank][3]], [Hh["eET"]])
                        tt(Hh["t1"].ap, psum[bank][:, 256:384], Hh["eET"].ap, ALU.mult, [psB[bank][2], Hh["eET"]], [Hh["t1"]])
                        stt(Hh["pT"].ap, Hh["t1"].ap, rk[:, hcol], Umat, ALU.mult, ALU.mult, [Hh["t1"], csm, cpk], [Hh["pT"]])
                chk("c0d", [csm.dump(), csmq.dump()] + [HT[0][n].dump() for n in names])
                for h in range(3):
                    Hh = HT[h]
                    bank = 3 + 2 * h
                    tr(psum[bank][:, 0:128], psB[bank][0], Hh["RT0"].ap, identf, [Hh["RT0"], cpk])
                    cp(Hh["R0"].ap, psum[bank][:, 0:128], [psB[bank][0]], [Hh["R0"]], eng="act")
                    tt(Hh["Y"].ap, psum[bank][:, 0:128], identf, ALU.add, [psB[bank][0], cpk], [Hh["Y"]])
                for lvl in range(1, 7):
                    pr, nx = (lvl - 1) % 2, lvl % 2
                    for h in range(3):
                        Hh = HT[h]
                        bank = (2 + 2 * h) if lvl % 2 == 1 else (3 + 2 * h)
                        Rp, RTp = Hh[f"R{pr}"], Hh[f"RT{pr}"]
                        Rn, RTn = Hh[f"R{nx}"], Hh[f"RT{nx}"]
                        mm(psum[bank][:, 128:256], psB[bank][1], Rp.ap, RTp.ap, [Rp, RTp])
                        cp(RTn.ap, psum[bank][:, 128:256], [psB[bank][1]], [RTn], eng="act")
                        if lvl < 6:
                            mm(psum[bank][:, 0:128], psB[bank][0], RTp.ap, Rp.ap, [Rp, RTp])
                            cp(Rn.ap, psum[bank][:, 0:128], [psB[bank][0]], [Rn], eng="dve")
                        mm(psum[bank][:, 256:384], psB[bank][2], RTn.ap, Hh["Y"].ap, [RTn, Hh["Y"]])
                        tt(Hh["Y"].ap, psum[bank][:, 256:384], Hh["Y"].ap, ALU.add, [psB[bank][2], Hh["Y"]], [Hh["Y"]])
                chk("c0e", [csm.dump(), csmq.dump()] + [HT[0][n].dump() for n in names])
                for h in range(3):
                    Hh = HT[h]
                    bank = 3 + 2 * h
                    mm(psum[bank][:, 0:128], psB[bank][0], Hh["Xk"].ap, Hh["Y"].ap, [Hh["Xk"], Hh["Y"]])
                    act(Hh["wTn"].ap, psum[bank][:, 0:128], AF.Copy, [psB[bank][0]], [Hh["wTn"]], scale=-1.0)
                    if own:
                        tr(psum[bank][:, 128:256], psB[bank][1], Hh["qg"].ap, identf, [Hh["qg"], cpk])
                        cp(Hh["qgT"].ap, psum[bank][:, 128:256], [psB[bank][1]], [Hh["qgT"]], eng="dve")
                for h in range(3):
                    Hh = HT[h]
                    S = Sst[hh * 3 + h]
                    bank = 2 + 2 * h
                    mm(psum[bank][:, 0:128], psB[bank][0], Hh["Y"].ap, Hh["Xv"].ap, [Hh["Y"], Hh["Xv"]], start=True, stop=False)
                    mm(psum[bank][:, 0:128], psB[bank][0], Hh["wTn"].ap, S.ap, [Hh["wTn"], S], start=False, stop=True)
                    cp(Hh["delta"].ap, psum[bank][:, 0:128], [psB[bank][0]], [Hh["delta"]], eng="dve")
                    if own:
                        mm(psum[bank][:, 128:256], psB[bank][1], Hh["qgT"].ap, S.ap, [Hh["qgT"], S], start=True, stop=False)
                        mm(psum[bank][:, 128:256], psB[bank][1], Hh["pT"].ap, Hh["delta"].ap, [Hh["pT"], Hh["delta"]], start=False, stop=True)
                    mm(psum[bank][:, 256:384], psB[bank][2], Hh["kd"].ap, Hh["delta"].ap, [Hh["kd"], Hh["delta"]])
                    stt(S.ap, S.ap, gc[:, h:h + 1], psum[bank][:, 256:384], ALU.mult, ALU.add, [S, csm, psB[bank][2]], [S])
                    if h == 0:
                        chk("chunk0", [csm.dump(), csmq.dump()] + [Hh[n].dump() for n in ["Xk", "kd", "Xv", "RT0", "Y", "wTn", "delta"]] + [S.dump()])
                    if own:
                        ts(Hh["on"].ap, psum[bank][:, 128:256], so[:, h:h + 1], ALU.mult, [psB[bank][1], csm], [Hh["on"]])
                        act(Hh["t1"].ap, Hh["on"].ap, AF.Square, [Hh["on"]], [Hh["t1"], csmq], accum=csmq.ap[:, 16 + h:17 + h])
                if own:
                    act(csm.ap[:, 32:35], csmq.ap[:, 16:19], AF.Ln, [csmq], [csm], scale=1.0 / 128, bias=EPS)
                    act(csm.ap[:, 32:35], csm.ap[:, 32:35], AF.Exp, [csm], [csm], scale=-0.5)
                    for h in range(3):
                        Hh = HT[h]
                        ts(Hh["on"].ap, Hh["on"].ap, csm.ap[:, 32 + h:33 + h], ALU.mult, [Hh["on"], csm], [Hh["on"]])
                        tr(psum[1][:, 128:256], psB[1][1], Hh["on"].ap, identf, [Hh["on"], cpk])
                        stt(oT.ap[:, hh * 3 + h, cs], psum[1][:, 128:256], cpk.ap[:, C_DNW:C_DNW + 1], zs.ap[:, h, cs],
                            ALU.mult, ALU.mult, [psB[1][1], cpk, zs], [oT])

        SKIP = os.environ.get("KSKIP", "").split(",")
        try:
            for hh in range(2):
                if "gdn" in SKIP:
                    break
                gdn_proj(hh, 0)
                chk("gproj", [qkvT.dump()])
                gdn_chunks(hh, 0)
            chk("gpre", [Sst[0].dump(), Sst[5].dump()])
            P.barrier()
            for hh in range(2):
                if "gdn" in SKIP:
                    break
                gdn_proj(hh, 1)
                gdn_chunks(hh, 1)
            chk("gdn", [oT.dump()])
            P.barrier()
            SA = Alloc(W0, AW)
            swab = SA.get(3 * 768, "swab", shape=(3, 768))
            dma("sp", swab.ap, swab_d.rearrange("t k n -> k t n"), swab, wr=[swab])
            wg = SA.get(5120, "wg", BF16, shape=(16, 640))
            qn = SA.get(192, "qn", BF16)
            kn = SA.get(64, "kn", BF16)
            qTs = SA.get(192, "qTs", BF16)
            KT = [SA.get(64, f"KT{i}", BF16) for i in range(2)]
            VT = [SA.get(64, f"VT{i}", BF16) for i in range(2)]
            sc = SA.get(384, "sc")
            ETb = [SA.get(192, f"ET{i}", BF16) for i in range(2)]
            den = SA.get(384, "den")
            ssm = SA.get(16, "ssm")
            esink = SA.get(8, "esink")
            sjunk = SA.get(128, "sjunk")
            act(esink.ap[:, 0:6], cpk.ap[:, C_SINK:C_SINK + 6], AF.Exp, [cpk], [esink])
            for g in range(2):
                for (c0, n, o0) in ((OFF_SQ + g * 384, 384, 0), (OFF_SK + g * 128, 128, 384), (OFF_SV + g * 128, 128, 512)):
                    dma("pool", wg.ap[:, :, o0:o0 + n], win_d[:, c0:c0 + n].rearrange("(k p) n -> p k n", p=128), wg, wr=[wg])
                for ti in range(9):
                    src, sl = (hHalo, slice(0, 128)) if ti == 0 else (hT, slice((ti - 1) * 128, ti * 128))
                    cur, prv = ti % 2, 1 - (ti % 2)
                    for kt in range(16):
                        mm(psum[0][:, :], psB[0], src.ap[:, kt, sl], wg.ap[:, kt, 0:512], [src, wg], start=(kt == 0), stop=(kt == 15))
                    for kt in range(16):
                        mm(psum[1][:, 0:128], psB[1], src.ap[:, kt, sl], wg.ap[:, kt, 512:640], [src, wg], start=(kt == 0), stop=(kt == 15))
                    for j in range(4):
                        act(sjunk.ap, psum[0][:, j * 128:(j + 1) * 128], AF.Square, [psB[0]], [sjunk, ssm], accum=ssm.ap[:, j:j + 1])
                    rstd_from_ssq(ssm.ap[:, 8:12], ssm.ap[:, 4:8], ssm.ap[:, 0:4], 128, [ssm], [ssm])
                    if ti > 0:
                        for j in range(3):
                            ts(qn.ap[:, j * 128:(j + 1) * 128], psum[0][:, j * 128:(j + 1) * 128], ssm.ap[:, 8 + j:9 + j], ALU.mult, [psB[0], ssm], [qn])
                    ts(kn.ap, psum[0][:, 384:512], ssm.ap[:, 11:12], ALU.mult, [psB[0], ssm], [kn])
                    cp(VT[cur].ap, psum[1][:, 0:128], [psB[1]], [VT[cur]], eng="act")
                    pb = psum[2][:].bitcast(BF16)
                    if ti > 0:
                        for j in range(3):
                            tr(pb[:, j * 128:(j + 1) * 128], psB[2], qn.ap[:, j * 128:(j + 1) * 128], identb, [qn, cb16])
                    tr(pb[:, 384:512], psB[2], kn.ap, identb, [kn, cb16])
                    if ti > 0:
                        ts(qTs.ap, pb[:, 0:384], cpk.ap[:, C_WQ:C_WQ + 1], ALU.mult, [psB[2], cpk], [qTs])
                    ts(KT[cur].ap, pb[:, 384:512], cpk.ap[:, C_WK:C_WK + 1], ALU.mult, [psB[2], cpk], [KT[cur]])
                    if ti == 0:
                        continue
                    for kb, (Kt, bank) in enumerate(((KT[prv], 3), (KT[cur], 4))):
                        mm(psum[bank][:, 0:384], psB[bank], Kt.ap, qTs.ap, [Kt, qTs])
                        bsel = 0 if kb == 1 else (2 if ti == 1 else 1)
                        stt(sc.ap, psum[bank][:, 0:384], float(128 ** -0.5), swab.ap[:, bsel, g * 384:(g + 1) * 384], ALU.mult, ALU.add,
                            [psB[bank], swab], [sc])
                        act(ETb[kb].ap, sc.ap, AF.Exp, [sc], [ETb[kb]])
                    mm(psum[5][:, 0:384], psB[5], VT[prv].ap, ETb[0].ap, [VT[prv], ETb[0]], start=True, stop=False)
                    mm(psum[5][:, 0:384], psB[5], VT[cur].ap, ETb[1].ap, [VT[cur], ETb[1]], start=False, stop=True)
                    mm(psum[6][:, 0:384], psB[6], onesb, ETb[0].ap, [cb16, ETb[0]], start=True, stop=False)
                    mm(psum[6][:, 0:384], psB[6], onesb, ETb[1].ap, [cb16, ETb[1]], start=False, stop=True)
                    for j in range(3):
                        ts(den.ap[:, j * 128:(j + 1) * 128], psum[6][:, j * 128:(j + 1) * 128], esink.ap[:, 3 * g + j:3 * g + j + 1], ALU.add,
                           [psB[6], esink], [den])
                    recip(den.ap, den.ap, [den], [den])
                    tt(oT.ap[:, 6 + 3 * g:9 + 3 * g, sl], psum[5][:, 0:384].rearrange("p (a b) -> p a b", a=3),
                       den.ap.rearrange("p (a b) -> p a b", a=3), ALU.mult, [psB[5], den], [oT])
            chk("swa", [oT.dump()])
            MA = Alloc(SA.p, AW)
            hmT = MA.get(2048, "hmT", BF16, shape=(16, 256))
            wm = MA.get(4096, "wm", BF16, shape=(16, 512))
            KM = MA.get(512, "KM", BF16, shape=(4, 256))
            VM = [MA.get(256, f"VM{i}", BF16) for i in range(2)]
            kmn = MA.get(256, "kmn", BF16)
            qmT = MA.get(256, "qmT", BF16)
            EM = MA.get(512, "EM", BF16)
            dnm = MA.get(512, "dnm")
            msm = MA.get(16, "msm")
            mjunk = MA.get(128, "mjunk")
            norm_T(mem_d, [(0, 0), (1, 1)], 1, hmT, Alloc(MA.p, AW))

            def headnorm_T(src_ps_bank, dst3d, wcol):
                for j in range(4):
                    act(mjunk.ap, psum[src_ps_bank][:, j * 128:(j + 1) * 128], AF.Square, [psB[src_ps_bank]], [mjunk, msm], accum=msm.ap[:, j:j + 1])
                rstd_from_ssq(msm.ap[:, 8:12], msm.ap[:, 4:8], msm.ap[:, 0:4], 128, [msm], [msm])
                for j in range(4):
                    ts(kmn.ap[:, j * 128:(j + 1) * 128], psum[src_ps_bank][:, j * 128:(j + 1) * 128], msm.ap[:, 8 + j:9 + j], ALU.mult,
                       [psB[src_ps_bank], msm], [kmn])
                pb = psum[3][:].bitcast(BF16)
                for j in range(4):
                    tr(pb[:, j * 128:(j + 1) * 128], psB[3], kmn.ap[:, j * 128:(j + 1) * 128], identb, [kmn, cb16])
                ts(dst3d[0], pb[:, 0:512].rearrange("p (a b) -> p a b", a=4), cpk.ap[:, wcol:wcol + 1], ALU.mult, [psB[3], cpk], [dst3d[1]])

            wload(wm, wmkv_d, 0, 512)
            for mt in range(2):
                for kt in range(16):
                    mm(psum[2][:, :], psB[2], hmT.ap[:, kt, mt * 128:(mt + 1) * 128], wm.ap[:, kt, :], [hmT, wm], start=(kt == 0), stop=(kt == 15))
                headnorm_T(2, (KM.ap[:, :, mt * 128:(mt + 1) * 128], KM), C_XK)
            wload(wm, wmkv_d, 512, 512)
            for mt in range(2):
                for kt in range(16):
                    mm(psum[2][:, :], psB[2], hmT.ap[:, kt, mt * 128:(mt + 1) * 128], wm.ap[:, kt, :], [hmT, wm], start=(kt == 0), stop=(kt == 15))
                cp(VM[mt].ap, psum[2][:, :], [psB[2]], [VM[mt]])
            wload(wm, win_d, OFF_MQ, 512)
            for t in range(8):
                tsl = slice(t * 128, (t + 1) * 128)
                for kt in range(16):
                    mm(psum[2][:, :], psB[2], hT.ap[:, kt, tsl], wm.ap[:, kt, :], [hT, wm], start=(kt == 0), stop=(kt == 15))
                headnorm_T(2, (qmT.ap.rearrange("p (a b) -> p a b", a=4), qmT), C_XQ)
                for mt in range(2):
                    for h in range(4):
                        mm(psum[4 + mt][:, h * 128:(h + 1) * 128], psB[4 + mt], KM.ap[:, h, mt * 128:(mt + 1) * 128], qmT.ap[:, h * 128:(h + 1) * 128], [KM, qmT])
                    act(EM.ap[:, mt * 512:(mt + 1) * 512], psum[4 + mt][:, :], AF.Exp, [psB[4 + mt]], [EM], scale=float(128 ** -0.5))
                for h in range(4):
                    for mt in range(2):
                        mm(psum[6][:, h * 128:(h + 1) * 128], psB[6], VM[mt].ap[:, h * 128:(h + 1) * 128], EM.ap[:, mt * 512 + h * 128:mt * 512 + (h + 1) * 128],
                           [VM[mt], EM], start=(mt == 0), stop=(mt == 1))
                for mt in range(2):
                    mm(psum[7][:, :], psB[7], onesb, EM.ap[:, mt * 512:(mt + 1) * 512], [cb16, EM], start=(mt == 0), stop=(mt == 1))
                recip(dnm.ap, psum[7][:, :], [psB[7]], [dnm])
                tt(oT.ap[:, 12:16, tsl], psum[6][:, :].rearrange("p (a b) -> p a b", a=4), dnm.ap.rearrange("p (a b) -> p a b", a=4), ALU.mult,
                   [psB[6], dnm], [oT])
            chk("mem", [oT.dump()])
            P.barrier()
            GA2 = Alloc(W0, AW)
            mT = GA2.get(8192, "mT", BF16, shape=(16, TOK))
            mslab = [GA2.get(4096, f"msl{i}", BF16, shape=(16, 512)) for i in range(3)]
            sg = [GA2.get(512, f"sg{i}") for i in range(6)]
            accs = [GA2.get(512, f"acc{i}") for i in range(2)]
            wo = [GA2.get(4096, "wo0", BF16, shape=(16, 512)), mslab[1]]
            brk = (range(0, 6), range(6, 12), range(12, 16))

            def load_mslab(mt):
                sl = mslab[mt % 3]
                dma("pool", sl.ap[:, :, 0:128], pall_d[:, mt * 128:(mt + 1) * 128].rearrange("(k p) n -> p k n", p=128), sl, wr=[sl])
                for br in range(3):
                    c0 = OFF_G + br * 2048 + mt * 128
                    dma("pool", sl.ap[:, :, 128 * (br + 1):128 * (br + 2)], win_d[:, c0:c0 + 128].rearrange("(k p) n -> p k n", p=128), sl, wr=[sl])

            load_mslab(0)
            load_mslab(1)
            pcount = 0
            for mt in range(16):
                if mt + 2 < 16:
                    load_mslab(mt + 2)
                elif mt + 2 < 18:
                    wload(wo[mt + 2 - 16], wout_d, (mt + 2 - 16) * 512, 512)
                sl = mslab[mt % 3]
                for c2 in range(2):
                    cols = slice(c2 * 512, (c2 + 1) * 512)
                    acc = accs[c2]
                    for br in range(3):
                        gb = c2 * 3 + br
                        pbk = 6 + (pcount % 2)
                        pcount += 1
                        sgb = sg[c2 * 3 + br]
                        for kt in range(16):
                            mm(psum[gb][:, :], psB[gb], sl.ap[:, kt, 128 * (br + 1):128 * (br + 2)], hT.ap[:, kt, cols], [sl, hT], start=(kt == 0), stop=(kt == 15))
                        act(sgb.ap, psum[gb][:, :], AF.Sigmoid, [psB[gb]], [sgb])
                        kts = list(brk[br])
                        for kt in kts:
                            mm(psum[pbk][:, :], psB[pbk], sl.ap[:, kt, 0:128], oT.ap[:, kt, cols], [sl, oT], start=(kt == kts[0]), stop=(kt == kts[-1]))
                        if br == 0:
                            tt(acc.ap, psum[pbk][:, :], sgb.ap, ALU.mult, [psB[pbk], sgb], [acc])
                        else:
                            tt(sgb.ap, psum[pbk][:, :], sgb.ap, ALU.mult, [psB[pbk], sgb], [sgb])
                            if br == 1:
                                tt(acc.ap, acc.ap, sgb.ap, ALU.add, [acc, sgb], [acc])
                            else:
                                tt(mT.ap[:, mt, cols], acc.ap, sgb.ap, ALU.add, [acc, sgb], [mT])
            chk("merge", [mT.dump()])
            x1 = T(arena[:, 4096:4096 + 16384].rearrange("p (t d) -> p t d", t=8), "x1", 4096, 16384)
            x1t = [Buf(f"x1t{t}") for t in range(8)]
            for t in range(8):
                dma("sp", x1.ap[:, t, :], x_d[t * 128:(t + 1) * 128, :], x1t[t], wr=[x1t[t], hT, oT, hP])
            pc = 0
            for cc in range(4):
                w = wo[cc % 2]
                if cc >= 2:
                    wload(w, wout_d, cc * 512, 512)
                for t in range(8):
                    bank = pc % 6
                    pc += 1
                    for kt in range(16):
                        mm(psum[bank][:, :], psB[bank], mT.ap[:, kt, t * 128:(t + 1) * 128], w.ap[:, kt, :], [mT, w], start=(kt == 0), stop=(kt == 15))
                    tt(x1.ap[:, t, cc * 512:(cc + 1) * 512], psum[bank][:, :], x1.ap[:, t, cc * 512:(cc + 1) * 512], ALU.add, [psB[bank], x1t[t]], [x1t[t]])
            chk("x1", [(x1t[t], 4096 + t * 2048, 2048) for t in range(8)])
            P.barrier()
            ML = Alloc(W0, AW)
            h2T = ML.get(8192, "h2T", BF16, shape=(16, TOK))
            wu = [ML.get(4096, f"wu{i}", BF16, shape=(16, 512)) for i in range(2)]
            wd = [ML.get(4096, f"wd{i}", BF16, shape=(4, 2048)) for i in range(2)]
            aT = [ML.get(2048, f"aT{i}", BF16, shape=(4, TOK)) for i in range(2)]
            rl = [ML.get(512, f"rl{i}") for i in range(2)]

            def load_wu(fb):
                wload(wu[fb % 2], wup_d, fb * 512, 512)

            def load_wd(fb):
                dma("pool", wd[fb % 2].ap, wdn_d[fb * 512:(fb + 1) * 512, :].rearrange("(k p) n -> p k n", p=128), wd[fb % 2], wr=[wd[fb % 2]])

            load_wu(0)
            load_wd(0)
            norm_T(None, [(t, t) for t in range(8)], 2, h2T, Alloc(ML.p, AW), src_sb=lambda t: (x1t[t], x1.ap[:, t, :]))
            upc = [0]

            def mlp_up(fb):
                w = wu[fb % 2]
                a = aT[fb % 2]
                for f4 in range(4):
                    for tc in range(2):
                        bank = upc[0] % 2
                        r = rl[upc[0] % 2]
                        upc[0] += 1
                        for kt in range(16):
                            mm(psum[bank][:, :], psB[bank], w.ap[:, kt, f4 * 128:(f4 + 1) * 128], h2T.ap[:, kt, tc * 512:(tc + 1) * 512], [w, h2T],
                               start=(kt == 0), stop=(kt == 15))
                        act(r.ap, psum[bank][:, :], AF.Relu, [psB[bank]], [r])
                        tt(a.ap[:, f4, tc * 512:(tc + 1) * 512], r.ap, r.ap, ALU.mult, [r], [a])

            dnc = [0]

            def mlp_down(fb):
                w = wd[fb % 2]
                a = aT[fb % 2]
                for t in range(8):
                    for cc in range(4):
                        bank = 2 + dnc[0] % 6
                        dnc[0] += 1
                        for k in range(4):
                            mm(psum[bank][:, :], psB[bank], a.ap[:, k, t * 128:(t + 1) * 128], w.ap[:, k, cc * 512:(cc + 1) * 512], [a, w],
                               start=(k == 0), stop=(k == 3))
                        tt(x1.ap[:, t, cc * 512:(cc + 1) * 512], psum[bank][:, :], x1.ap[:, t, cc * 512:(cc + 1) * 512], ALU.add, [psB[bank], x1t[t]], [x1t[t]])

            for fb in range(16):
                if fb + 1 < 16:
                    load_wu(fb + 1)
                mlp_up(fb)
                if fb > 0:
                    mlp_down(fb - 1)
                if fb + 1 < 16:
                    load_wd(fb + 1)
            mlp_down(15)
            for t in range(8):
                dma("sp", y_d[t * 128:(t + 1) * 128, :], x1.ap[:, t, :], x1t[t], rd=[x1t[t]])
        except StopBuild:
            pass
        return finish(nc, P, st, dbg_d, dumps, dma, locals())


def finish(nc, P, st, dbg_d, dumps, dma, L):
    if dbg_d is not None:
        off = 0
        arena = L["arena"]
        for (t, a0, words) in dumps:
            dma("sp", dbg_d[:, off:off + words], arena[:, a0:a0 + words], t, rd=[t])
            off += words
    P.final_wait()
    P.emit(nc)
    return nc


def _t5_bucket(dist):
    import math
    n = np.maximum(dist, 0)
    max_exact = 16
    nf = np.maximum(n, 1).astype(np.float32)
    large = max_exact + (np.log(nf / max_exact) / math.log(128 / max_exact) * (32 - max_exact)).astype(np.int32)
    large = np.minimum(large, 31)
    return np.where(n < max_exact, n, large)


def make_in_maps(inp):
    f32 = np.float32
    x = np.asarray(inp["x"], f32)
    mem = np.asarray(inp["mem"], f32)
    w_in = np.ascontiguousarray(np.asarray(inp["w_in"], f32)[0])
    w_mem_kv = np.ascontiguousarray(np.asarray(inp["w_mem_kv"], f32)[0])
    p_all = np.ascontiguousarray(np.concatenate([np.asarray(inp["p_dn"], f32)[0], np.asarray(inp["p_swa"], f32)[0],
                                                 np.asarray(inp["p_mem"], f32)[0]], axis=0))
    w_out = np.ascontiguousarray(np.asarray(inp["w_out"], f32)[0])
    w_up = np.ascontiguousarray(np.asarray(inp["w_mlp_up"], f32)[0])
    w_down = np.ascontiguousarray(np.asarray(inp["w_mlp_down"], f32)[0])
    nw3 = np.ascontiguousarray(np.stack([np.asarray(inp["attn_norm_w"], f32)[0], np.asarray(inp["mem_norm_w"], f32)[0],
                                         np.asarray(inp["mlp_norm_w"], f32)[0]], axis=0))
    cp = np.zeros((128, NCP), f32)
    idx = np.arange(128)
    cp[:, C_ID:C_ID + 128] = np.eye(128, dtype=f32)
    cp[:, C_U:C_U + 128] = (idx[:, None] <= idx[None, :]).astype(f32)
    cp[:, C_SL:C_SL + 128] = (idx[:, None] > idx[None, :]).astype(f32)
    cp[:, C_NSL:C_NSL + 128] = -(idx[:, None] > idx[None, :]).astype(f32)
    cp[:, C_ONE:C_ONE + 128] = 1.0
    cw = np.asarray(inp["dn_conv_w"], f32)[0]
    cp[:, C_CONV:C_CONV + 72] = cw.reshape(4, 18, 128).transpose(2, 1, 0).reshape(128, 72)
    cp[:, C_DNW] = np.asarray(inp["dn_out_norm_w"], f32)[0]
    cp[:, C_WQ] = np.asarray(inp["swa_q_norm_w"], f32)[0]
    cp[:, C_WK] = np.asarray(inp["swa_k_norm_w"], f32)[0]
    cp[:, C_XQ] = np.asarray(inp["xq_norm_w"], f32)[0]
    cp[:, C_XK] = np.asarray(inp["xk_norm_w"], f32)[0]
    cp[:, C_ALOG:C_ALOG + 6] = np.asarray(inp["dn_A_log"], f32)[0][None, :]
    cp[:, C_DTB:C_DTB + 6] = np.asarray(inp["dn_dt_bias"], f32)[0][None, :]
    cp[:, C_SINK:C_SINK + 6] = np.asarray(inp["swa_sinks"], f32)[0][None, :]
    cb16 = np.concatenate([np.eye(128, dtype=f32), np.ones((128, 128), f32)], axis=1).astype(ml_dtypes.bfloat16)
    rb = np.asarray(inp["rel_bias"], f32)
    qi = np.arange(128)[:, None]
    kj = np.arange(256)[None, :]
    dist = qi - kj + 128
    valid = (dist >= 0) & (dist < 128)
    bias = rb[_t5_bucket(dist)]
    bias = np.where(valid[:, :, None], bias, f32(NEG)).astype(f32)
    biasT = bias.transpose(1, 2, 0)
    b_prev = np.ascontiguousarray(biasT[0:128].reshape(128, 768))
    b_cur = np.ascontiguousarray(biasT[128:256].reshape(128, 768))
    b_none = np.full((128, 768), NEG, f32)
    maps = []
    for c in range(8):
        b, hf = c // 2, c % 2
        xo = np.ascontiguousarray(x[b, hf * TOK:(hf + 1) * TOK])
        xp = np.ascontiguousarray(x[b, 0:TOK]) if hf == 1 else np.zeros((TOK, D), f32)
        swab = np.stack([b_cur, b_prev, b_prev if hf == 1 else b_none], axis=0)
        maps.append({"x": xo, "xp": xp, "mem": np.ascontiguousarray(mem[b]), "w_in": w_in, "w_mem_kv": w_mem_kv,
                     "p_all": p_all, "w_out": w_out, "w_up": w_up, "w_down": w_down, "nw3": nw3, "cpack": cp,
                     "cb16": cb16, "swab": np.ascontiguousarray(swab)})
    return maps


_NC_CACHE = {}


def kernel(**inputs):
    maps = make_in_maps(inputs)
    if "nc" not in _NC_CACHE:
        _NC_CACHE["nc"] = build()
    res = run_bass_kernel_spmd(_NC_CACHE["nc"], maps, core_ids=list(range(8)))
    out = np.zeros((4, 2048, D), np.float32)
    for c in range(8):
        b, hf = c // 2, c % 2
        out[b, hf * TOK:(hf + 1) * TOK] = res.results[c]["y"]
    return out
```

```python
import os
import numpy as np
import ml_dtypes
from contextlib import ExitStack
import concourse.bass as bass
import concourse.mybir as mybir
from concourse.bass_utils import run_bass_kernel_spmd

F32 = mybir.dt.float32
BF16 = mybir.dt.bfloat16
AF = mybir.ActivationFunctionType
ALU = mybir.AluOpType

EPS = 1e-6
D = 2048
TOK = 1024
NT = 8
OFF_QKV, OFF_Z, OFF_B, OFF_A, OFF_SQ, OFF_SK, OFF_SV, OFF_MQ, OFF_G = 0, 2304, 3072, 3078, 3084, 3852, 4108, 4364, 4876
IN_W = 11020
NEG = -30000.0

C_ID, C_U, C_SL, C_NSL, C_ONE, C_CONV, C_DNW, C_WQ, C_WK, C_XQ, C_XK, C_ALOG, C_DTB, C_SINK, NCP = \
    0, 128, 256, 384, 512, 640, 712, 713, 714, 715, 716, 717, 723, 729, 736


class Buf:
    __slots__ = ("name", "lw", "rd")

    def __init__(self, name):
        self.name = name
        self.lw = None
        self.rd = {}


class Prog:
    ENG = ("pe", "act", "dve", "pool", "sp")

    def __init__(self):
        self.ops = {e: [] for e in self.ENG}
        self.cnt = {e: 0 for e in self.ENG}
        self.waited = {e: {} for e in self.ENG}
        self.dcnt = {}
        self.swq = []

    def _deps(self, reads, writes):
        toks = []
        for b in reads:
            if b.lw is not None:
                toks.append(b.lw)
        for b in writes:
            if b.lw is not None:
                toks.append(b.lw)
            toks.extend(b.rd.items())
        return toks

    def _filter(self, eng, toks):
        w = self.waited[eng]
        out = {}
        for (k, v) in toks:
            if eng == "pe" and k == "Epe":
                continue
            if w.get(k, 0) >= v:
                continue
            if out.get(k, 0) < v:
                out[k] = v
        for k, v in out.items():
            w[k] = v
        return list(out.items())

    def _record(self, tok, reads, writes):
        for b in writes:
            b.lw = tok
            b.rd = {}
        for b in reads:
            if b in writes:
                continue
            if b.rd.get(tok[0], 0) < tok[1]:
                b.rd[tok[0]] = tok[1]

    def op(self, eng, fn, reads=(), writes=()):
        extra = [b for b in reads if b.name.startswith("ps") and b not in writes]
        if extra:
            writes = list(writes) + extra
        waits = self._filter(eng, self._deps(reads, writes))
        self.cnt[eng] += 1
        tok = ("E" + eng, self.cnt[eng])
        self.ops[eng].append((waits, fn, ("E" + eng, 1)))
        self._record(tok, reads, writes)
        return tok

    def dma(self, eng, fn, sembuf, reads=(), writes=(), ndesc=0):
        toks = self._deps(reads, writes)
        if eng == "pool" and ndesc:
            while self.swq and sum(n for _, n in self.swq) + ndesc > 640:
                toks.append(self.swq.pop(0)[0])
        waits = self._filter(eng, toks)
        key = "D" + sembuf.name
        self.dcnt[key] = self.dcnt.get(key, 0) + 16
        tok = (key, self.dcnt[key])
        self.ops[eng].append((waits, fn, (key, 16)))
        self._record(tok, reads, writes)
        if eng == "pool" and ndesc:
            self.swq.append((tok, ndesc))
        return tok

    def barrier(self):
        toks = [("E" + e, self.cnt[e]) for e in self.ENG if self.cnt[e] > 0] + list(self.dcnt.items())
        for e in self.ENG:
            waits = self._filter(e, toks)
            if waits:
                self.ops[e].append((waits, None, None))

    def final_wait(self, eng="sp"):
        waits = self._filter(eng, list(self.dcnt.items()))
        self.ops[eng].append((waits, None, None))

    def emit(self, nc):
        keys = ["E" + e for e in self.ENG] + sorted(self.dcnt.keys())
        with ExitStack() as st:
            sems = {}
            for k in keys:
                sems[k] = st.enter_context(nc.semaphore("s_" + k))
            block = st.enter_context(nc.Block())
            binders = {"pe": block.tensor, "act": block.scalar, "dve": block.vector,
                       "pool": block.gpsimd, "sp": block.sync}
            for eng in self.ENG:
                ops = self.ops[eng]

                def body(e, ops=ops):
                    for waits, fn, inc in ops:
                        for k, v in waits:
                            e.wait_ge(sems[k], v)
                        if fn is None:
                            continue
                        ins = fn(e)
                        ins.then_inc(sems[inc[0]], inc[1])
                binders[eng](body)


class StopBuild(Exception):
    pass


class T:
    __slots__ = ("ap", "b", "off", "words")

    def __init__(self, ap, name, off=None, words=None):
        self.ap = ap
        self.b = Buf(name)
        self.off = off
        self.words = words

    def dump(self):
        return (self, self.off, self.words)


def build(stop_after=None, dbg=None):
    nc = bass.Bass("TRN2", target_bir_lowering=False)
    P = Prog()
    dr = {}

    def din(name, shape, dt=F32):
        dr[name] = nc.dram_tensor(name, shape, dt, kind="ExternalInput").ap()
        return dr[name]

    x_d = din("x", [TOK, D])
    xp_d = din("xp", [TOK, D])
    mem_d = din("mem", [256, D])
    win_d = din("w_in", [D, IN_W])
    wmkv_d = din("w_mem_kv", [D, 1024])
    pall_d = din("p_all", [D, D])
    wout_d = din("w_out", [D, D])
    wup_d = din("w_up", [D, 4 * D])
    wdn_d = din("w_down", [4 * D, D])
    nw3_d = din("nw3", [3, D])
    cpack_d = din("cpack", [128, NCP])
    cb16_d = din("cb16", [128, 256], BF16)
    swab_d = din("swab", [3, 128, 768])
    y_d = nc.dram_tensor("y", [TOK, D], F32, kind="ExternalOutput").ap()
    dbg_d = None
    if dbg is not None:
        dbg_d = nc.dram_tensor("dbg", [128, dbg], F32, kind="ExternalOutput").ap()

    with ExitStack() as st:
        AW = 52800
        arena = st.enter_context(nc.sbuf_tensor("arena", [128, AW], F32))
        psum = [st.enter_context(nc.psum_tensor(f"ps{i}", [128, 512], F32)) for i in range(8)]
        psB = [[Buf(f"ps{i}")] * 4 for i in range(8)]

        uid = [0]

        def view(off, words, name, dt=F32, shape=None):
            ap = arena[:, off:off + words]
            if dt == BF16:
                ap = ap.bitcast(BF16)
            if shape is not None:
                ap = ap[:, 0:int(np.prod(shape))]
                if len(shape) == 2:
                    ap = ap.rearrange("p (a b) -> p a b", a=shape[0])
                elif len(shape) == 3:
                    ap = ap.rearrange("p (a b c) -> p a b c", a=shape[0], b=shape[1])
            uid[0] += 1
            return T(ap, f"{name}_{uid[0]}", off, words)

        class Alloc:
            def __init__(self, lo, hi):
                self.lo, self.hi, self.p = lo, hi, lo

            def get(self, words, name, dt=F32, shape=None):
                o = self.p
                self.p += words
                assert self.p <= self.hi, (name, self.p, self.hi)
                return view(o, words, name, dt, shape)

        def bl(ts):
            out = []
            for t in ts:
                if isinstance(t, T):
                    out.append(t.b)
                elif isinstance(t, Buf):
                    out.append(t)
                elif isinstance(t, (list, tuple)):
                    out.extend(bl(t))
            return out

        def mm(out_ap, outb, lhsT, rhs, rd, start=True, stop=True):
            P.op("pe", lambda e: e.matmul(out_ap, lhsT=lhsT, rhs=rhs, start=start, stop=stop), reads=bl(rd), writes=bl([outb]))

        def tr(out_ap, outb, in_ap, ident_ap, rd):
            P.op("pe", lambda e: e.transpose(out=out_ap, in_=in_ap, identity=ident_ap), reads=bl(rd), writes=bl([outb]))

        def act(out_ap, in_ap, func, rd, wr, scale=None, bias=None, accum=None, eng="act"):
            kw = {}
            if scale is not None:
                kw["scale"] = scale
            if bias is not None:
                kw["bias"] = bias
            if accum is not None:
                kw["accum_out"] = accum
            P.op("act", lambda e: e.activation(out=out_ap, in_=in_ap, func=func, **kw), reads=bl(rd), writes=bl(wr))

        def tt(out_ap, a, b, op, rd, wr, eng="dve"):
            P.op(eng, lambda e: e.tensor_tensor(out=out_ap, in0=a, in1=b, op=op), reads=bl(rd), writes=bl(wr))

        def ts(out_ap, a, s1, op0, rd, wr, s2=None, op1=None, eng="dve"):
            if op1 is None:
                P.op(eng, lambda e: e.tensor_scalar(out=out_ap, in0=a, scalar1=s1, scalar2=None, op0=op0), reads=bl(rd), writes=bl(wr))
            else:
                P.op(eng, lambda e: e.tensor_scalar(out=out_ap, in0=a, scalar1=s1, scalar2=s2, op0=op0, op1=op1), reads=bl(rd), writes=bl(wr))

        def stt(out_ap, a, s, b, op0, op1, rd, wr):
            P.op("dve", lambda e: e.scalar_tensor_tensor(out=out_ap, in0=a, scalar=s, in1=b, op0=op0, op1=op1), reads=bl(rd), writes=bl(wr))

        def cp(out_ap, in_ap, rd, wr, eng="dve"):
            if eng == "act":
                P.op("act", lambda e: e.copy(out=out_ap, in_=in_ap), reads=bl(rd), writes=bl(wr))
            else:
                P.op(eng, lambda e: e.tensor_copy(out=out_ap, in_=in_ap), reads=bl(rd), writes=bl(wr))

        def recip(out_ap, in_ap, rd, wr):
            P.op("dve", lambda e: e.reciprocal(out=out_ap, in_=in_ap), reads=bl(rd), writes=bl(wr))

        def dma(q, out_ap, in_ap, semT, rd=(), wr=()):
            nd = 0
            if q == "pool":
                shp = list(out_ap.shape)
                nd = int(np.prod(shp[:-1])) // 16 + 2
            P.dma(q, lambda e: e.dma_start(out=out_ap, in_=in_ap), semT.b if isinstance(semT, T) else semT, reads=bl(rd), writes=bl(wr), ndesc=nd)

        dumps = []

        def chk(name, dl):
            if stop_after == name:
                dumps.extend(dl)
                raise StopBuild()

        def wload(slab, w_dram, c0, n, kt=16, r0=0):
            dma("pool", slab.ap, w_dram[r0:r0 + kt * 128, c0:c0 + n].rearrange("(k p) n -> p k n", p=128), slab, wr=[slab])

        def rstd_from_ssq(out_ap, tmp_ap, ssq_ap, n, rd, wr):
            act(tmp_ap, ssq_ap, AF.Ln, rd, wr, scale=1.0 / n, bias=EPS)
            act(out_ap, tmp_ap, AF.Exp, wr, wr, scale=-0.5)

        CONST = Alloc(0, 4096)
        cpk = CONST.get(NCP, "cpack")
        cb16 = CONST.get(128, "cb16", BF16)
        wbc = CONST.get(2048, "wbc")
        smalls = CONST.get(64, "smalls")
        identf = cpk.ap[:, C_ID:C_ID + 128]
        Umat = cpk.ap[:, C_U:C_U + 128]
        SLm = cpk.ap[:, C_SL:C_SL + 128]
        NSLm = cpk.ap[:, C_NSL:C_NSL + 128]
        onesf = cpk.ap[:, C_ONE:C_ONE + 128]
        identb = cb16.ap[:, 0:128]
        onesb = cb16.ap[:, 128:256]
        hT = view(4096, 8192, "hT", BF16, shape=(16, TOK))
        hP = view(4096 + 8192, 8192, "hP", BF16, shape=(16, TOK))
        oT = T(hP.ap, "oT", hP.off, hP.words)
        W0 = 4096 + 16384

        dma("sp", cpk.ap, cpack_d[:, :], cpk, wr=[cpk])
        dma("sp", cb16.ap, cb16_d[:, :], cb16, wr=[cb16])

        def norm_T(src_dram, tiles, nw_row, dst, WA, src_sb=None, load_w=True):
            xt = [WA.get(2048, "xt") for _ in range(2)] if src_sb is None else None
            junk = WA.get(1024, "junk", BF16)
            xn = WA.get(1024, "xn", BF16)
            stat = WA.get(8, "stat")
            if load_w:
                dma("sp", wbc.ap, nw3_d[nw_row].partition_broadcast(128), wbc, wr=[wbc])
            for i, (t, dt_) in enumerate(tiles):
                if src_sb is None:
                    xb = xt[i % 2]
                    dma("sp", xb.ap, src_dram[t * 128:(t + 1) * 128, :], xb, wr=[xb])
                    xin = xb.ap
                else:
                    xb, xin = src_sb(t)
                act(junk.ap, xin, AF.Square, [xb], [junk, stat], accum=stat.ap[:, 0:1])
                rstd_from_ssq(stat.ap[:, 2:3], stat.ap[:, 1:2], stat.ap[:, 0:1], D, [stat], [stat])
                stt(xn.ap, xin, stat.ap[:, 2:3], wbc.ap, ALU.mult, ALU.mult, [xb, stat, wbc], [xn])
                for half in range(2):
                    bank = half
                    pb = psum[bank][:].bitcast(BF16)
                    for j in range(8):
                        kt = half * 8 + j
                        tr(pb[:, j * 128:(j + 1) * 128], psB[bank], xn.ap[:, kt * 128:(kt + 1) * 128], identb, [xn, cb16])
                    cp(dst.ap[:, half * 8:(half + 1) * 8, dt_ * 128:(dt_ + 1) * 128], pb.rearrange("p (k t) -> p k t", k=8),
                       [psB[bank]], [dst], eng=("dve" if half == 0 else "act"))

        WA = Alloc(W0, AW)
        norm_T(x_d, [(t, t) for t in range(NT)], 0, hT, WA)
        norm_T(xp_d, [(t, t) for t in range(NT)], 0, hP, Alloc(WA.p, AW), load_w=False)
        hHalo = CONST.get(1024, "hHalo", BF16, shape=(16, 128))
        cp(hHalo.ap, hP.ap[:, :, 896:1024], [hP], [hHalo])
        if stop_after == "p0":
            return finish(nc, P, st, dbg_d, [(hT, 4096, 8192), (hP, 4096 + 8192, 8192)], dma, locals())
        P.barrier()

        GA = Alloc(W0, AW)
        qkvT = GA.get(9 * 1024, "qkvT", shape=(9, TOK))
        zs = GA.get(3 * 1024, "zs", shape=(3, TOK))
        raw = [GA.get(1028, "raw") for _ in range(2)]
        wsl = [GA.get(1024, "wsl", BF16, shape=(16, 128)) for _ in range(3)]
        wba = GA.get(128, "wba", BF16, shape=(16, 12))
        ba_all = GA.get(16 * 12, "ba_all", shape=(16, 12))
        beta_all = GA.get(16 * 6, "beta_all", shape=(16, 6))
        g_all = GA.get(16 * 6, "g_all", shape=(16, 6))
        sp_tmp = GA.get(16 * 6, "sp_tmp", shape=(16, 6))
        nexpA = GA.get(8, "nexpA")
        Sst = [GA.get(128, f"S{h}") for h in range(6)]
        carry = GA.get(64, "carry", shape=(18, 3))
        HT = []
        names = ["Xk", "kd", "Xv", "qg", "gSL", "gSLc", "eE", "eET", "t1", "pT", "R0", "R1", "RT0", "RT1", "Y", "wTn", "qgT", "delta", "on"]
        GB = Alloc(wbc.off, wbc.off + 2048)
        for s_ in range(6):
            dct = {}
            for n in names:
                al = GB if (GB.p + 128 <= GB.hi) else GA
                dct[n] = al.get(128, f"{n}{s_}")
            HT.append(dct)
        csmL = [GA.get(64, f"csm{i}") for i in range(2)]
        csmqL = [GA.get(32, f"csmq{i}") for i in range(2)]
        wsl_i = [0]

        wload_ba = lambda: dma("pool", wba.ap[:, :, 0:12], win_d[:, OFF_B:OFF_B + 12].rearrange("(k p) n -> p k n", p=128), wba, wr=[wba])
        wload_ba()
        for tt_i in range(16):
            src = hP if tt_i < 8 else hT
            tl = tt_i % 8
            for kt in range(16):
                mm(psum[7][:, 0:12], psB[7][0], src.ap[:, kt, tl * 128:(tl + 1) * 128], wba.ap[:, kt, 0:12], [src, wba], start=(kt == 0), stop=(kt == 15))
            cp(ba_all.ap[:, tt_i, :], psum[7][:, 0:12], [psB[7][0]], [ba_all])
        act(beta_all.ap, ba_all.ap[:, :, 0:6], AF.Exp, [ba_all], [beta_all], scale=-1.0)
        ts(beta_all.ap, beta_all.ap, 1.0, ALU.add, [beta_all], [beta_all])
        recip(beta_all.ap, beta_all.ap, [beta_all], [beta_all])
        act(nexpA.ap[:, 0:6], cpk.ap[:, C_ALOG:C_ALOG + 6], AF.Exp, [cpk], [nexpA])
        for tt_i in range(16):
            tt(sp_tmp.ap[:, tt_i, :], ba_all.ap[:, tt_i, 6:12], cpk.ap[:, C_DTB:C_DTB + 6], ALU.add, [ba_all, cpk], [sp_tmp])
        act(sp_tmp.ap, sp_tmp.ap, AF.Exp, [sp_tmp], [sp_tmp])
        act(sp_tmp.ap, sp_tmp.ap, AF.Ln, [sp_tmp], [sp_tmp], bias=1.0)
        for tt_i in range(16):
            stt(g_all.ap[:, tt_i, :], sp_tmp.ap[:, tt_i, :], -1.0, nexpA.ap[:, 0:6], ALU.mult, ALU.mult, [sp_tmp, nexpA], [g_all])
        if stop_after == "ba":
            return finish(nc, P, st, dbg_d, [ba_all.dump(), beta_all.dump(), g_all.dump()], dma, locals())
        for h in range(6):
            P.op("dve", lambda e, h=h: e.memset(Sst[h].ap, 0.0), writes=[Sst[h].b])
        P.op("dve", lambda e: e.memset(carry.ap, 0.0), writes=[carry.b])

        def gdn_proj(hh, stage):
            src = hP if stage == 0 else hT
            fts = [hh * 3 + j for j in range(3)] + [6 + hh * 3 + j for j in range(3)] + [12 + hh * 3 + j for j in range(3)]
            for li, ft in enumerate(fts):
                w = wsl[wsl_i[0] % 3]
                wsl_i[0] += 1
                wload(w, win_d, OFF_QKV + ft * 128, 128)
                rb = raw[li % 2]
                for c2 in range(2):
                    bank = 2 + c2
                    for kt in range(16):
                        mm(psum[bank][:, :], psB[bank], w.ap[:, kt, :], src.ap[:, kt, c2 * 512:(c2 + 1) * 512], [w, src], start=(kt == 0), stop=(kt == 15))
                    cp(rb.ap[:, 3 + c2 * 512:3 + (c2 + 1) * 512], psum[bank][:, :], [psB[bank]], [rb], eng=("act" if c2 == 0 else "dve"))
                cp(rb.ap[:, 0:3], carry.ap[:, ft, :], [carry], [rb])
                cp(carry.ap[:, ft, :], rb.ap[:, 1024:1027], [rb], [carry])
                cw = cpk.ap[:, C_CONV + ft * 4:C_CONV + ft * 4 + 4]
                acc = qkvT.ap[:, li, :]
                ts(acc, rb.ap[:, 0:1024], cw[:, 0:1], ALU.mult, [rb, cpk], [qkvT])
                for k in range(1, 4):
                    stt(acc, rb.ap[:, k:k + 1024], cw[:, k:k + 1], acc, ALU.mult, ALU.add, [rb, cpk, qkvT], [qkvT])
                act(acc, acc, AF.Silu, [qkvT], [qkvT])
            if stage == 1:
                for j in range(3):
                    w = wsl[wsl_i[0] % 3]
                    wsl_i[0] += 1
                    wload(w, win_d, OFF_Z + (hh * 3 + j) * 128, 128)
                    for c2 in range(2):
                        bank = 2 + c2
                        for kt in range(16):
                            mm(psum[bank][:, :], psB[bank], w.ap[:, kt, :], hT.ap[:, kt, c2 * 512:(c2 + 1) * 512], [w, hT], start=(kt == 0), stop=(kt == 15))
                        act(zs.ap[:, j, c2 * 512:(c2 + 1) * 512], psum[bank][:, :], AF.Silu, [psB[bank]], [zs])

        def gdn_chunks(hh, stage):
            own = stage == 1
            h0 = hh * 3
            for cp_ in range(4):
                chunks = (2 * cp_, 2 * cp_ + 1)
                streams = [(ci, h) for ci in range(2) for h in range(3)]
                CS = [slice(c * 128, (c + 1) * 128) for c in chunks]
                G3 = [g_all.ap[:, stage * 8 + c, h0:h0 + 3] for c in chunks]
                B3 = [beta_all.ap[:, stage * 8 + c, h0:h0 + 3] for c in chunks]
                for ci in range(2):
                    o = 16 * ci
                    mm(psum[0][:, o:o + 3], psB[0][0], Umat, G3[ci], [cpk, g_all])
                    mm(psum[0][:, o + 3:o + 6], psB[0][0], SLm, G3[ci], [cpk, g_all])
                    mm(psum[0][:, o + 6:o + 9], psB[0][0], onesf, G3[ci], [cpk, g_all])
                    act(csmL[ci].ap[:, 0:9], psum[0][:, o:o + 9], AF.Exp, [psB[0][0]], [csmL[ci]])
                for s_, (ci, h) in enumerate(streams):
                    Hh, bank, csmq = HT[s_], 2 + s_, csmqL[ci]
                    for j in range(3):
                        tr(psum[bank][:, j * 128:(j + 1) * 128], psB[bank][j], qkvT.ap[:, j * 3 + h, CS[ci]], identf, [qkvT, cpk])
                    act(Hh["t1"].ap, psum[bank][:, 0:128], AF.Square, [psB[bank][0]], [Hh["t1"], csmq], accum=csmq.ap[:, h:h + 1])
                    act(Hh["t1"].ap, psum[bank][:, 128:256], AF.Square, [psB[bank][1]], [Hh["t1"], csmq], accum=csmq.ap[:, 3 + h:4 + h])
                    cp(Hh["qg"].ap, psum[bank][:, 0:128], [psB[bank][0]], [Hh["qg"]], eng="dve")
                    cp(Hh["Xk"].ap, psum[bank][:, 128:256], [psB[bank][1]], [Hh["Xk"]], eng="act")
                    cp(Hh["Xv"].ap, psum[bank][:, 256:384], [psB[bank][2]], [Hh["Xv"]], eng="dve")
                SC = []
                for ci in range(2):
                    csm, csmq = csmL[ci], csmqL[ci]
                    eG, eGr, gc = csm.ap[:, 0:3], csm.ap[:, 3:6], csm.ap[:, 6:9]
                    act(csmq.ap[:, 6:12], csmq.ap[:, 0:6], AF.Ln, [csmq], [csmq], bias=EPS)
                    act(csmq.ap[:, 12:15], csmq.ap[:, 6:9], AF.Exp, [csmq], [csmq], scale=-0.5)
                    act(csm.ap[:, 12:15], csmq.ap[:, 9:12], AF.Exp, [csmq], [csm], scale=-0.5)
                    ts(csm.ap[:, 15:18], csmq.ap[:, 9:12], -0.5, ALU.mult, [csmq], [csm])
                    rk, lnrk = csm.ap[:, 12:15], csm.ap[:, 15:18]
                    tt(csm.ap[:, 18:21], B3[ci], rk, ALU.mult, [beta_all, csm], [csm])
                    tt(csm.ap[:, 21:24], csm.ap[:, 18:21], eG, ALU.mult, [csm], [csm])
                    tt(csm.ap[:, 24:27], rk, eGr, ALU.mult, [csm], [csm])
                    ts(csm.ap[:, 27:30], csmq.ap[:, 12:15], float(128 ** -0.5), ALU.mult, [csmq], [csm])
                    SC.append(dict(eG=eG, eGr=eGr, gc=gc, rk=rk, lnrk=lnrk, sA=csm.ap[:, 18:21], sXk=csm.ap[:, 21:24],
                                   skd=csm.ap[:, 24:27], so=csm.ap[:, 27:30]))
                for s_, (ci, h) in enumerate(streams):
                    Hh, sc_, csm = HT[s_], SC[ci], csmL[ci]
                    hcol = slice(h, h + 1)
                    ts(Hh["kd"].ap, Hh["Xk"].ap, sc_["skd"][:, hcol], ALU.mult, [Hh["Xk"], csm], [Hh["kd"]])
                    ts(Hh["Xk"].ap, Hh["Xk"].ap, sc_["sXk"][:, hcol], ALU.mult, [Hh["Xk"], csm], [Hh["Xk"]])
                    ts(Hh["Xv"].ap, Hh["Xv"].ap, B3[ci][:, hcol], ALU.mult, [Hh["Xv"], beta_all], [Hh["Xv"]])
                    if own:
                        ts(Hh["qg"].ap, Hh["qg"].ap, sc_["eG"][:, hcol], ALU.mult, [Hh["qg"], csm], [Hh["qg"]])
                    ts(Hh["gSL"].ap, SLm, G3[ci][:, hcol], ALU.mult, [cpk, g_all], [Hh["gSL"]])
                    stt(Hh["gSLc"].ap, identf, sc_["lnrk"][:, hcol], Hh["gSL"].ap, ALU.mult, ALU.add, [cpk, csm, Hh["gSL"]], [Hh["gSLc"]])
                for s_, (ci, h) in enumerate(streams):
                    Hh, sc_, csm, bank = HT[s_], SC[ci], csmL[ci], 2 + s_
                    hcol = slice(h, h + 1)
                    kTr = qkvT.ap[:, 3 + h, CS[ci]]
                    qTr = qkvT.ap[:, h, CS[ci]]
                    mm(psum[bank][:, 0:128], psB[bank][0], kTr, kTr, [qkvT])
                    mm(psum[bank][:, 128:256], psB[bank][1], Umat, Hh["gSLc"].ap, [cpk, Hh["gSLc"]])
                    if own:
                        mm(psum[bank][:, 256:384], psB[bank][2], kTr, qTr, [qkvT])
                        mm(psum[bank][:, 384:512], psB[bank][3], Hh["gSL"].ap, Umat, [cpk, Hh["gSL"]])
                    act(Hh["eE"].ap, psum[bank][:, 128:256], AF.Exp, [psB[bank][1]], [Hh["eE"]])
                    tt(Hh["t1"].ap, psum[bank][:, 0:128], Hh["eE"].ap, ALU.mult, [psB[bank][0], Hh["eE"]], [Hh["t1"]])
                    stt(Hh["RT0"].ap, Hh["t1"].ap, sc_["sA"][:, hcol], NSLm, ALU.mult, ALU.mult, [Hh["t1"], csm, cpk], [Hh["RT0"]])
                    if own:
                        act(Hh["eET"].ap, psum[bank][:, 384:512], AF.Exp, [psB[bank][3]], [Hh["eET"]])
                        tt(Hh["t1"].ap, psum[bank][:, 256:384], Hh["eET"].ap, ALU.mult, [psB[bank][2], Hh["eET"]], [Hh["t1"]])
                        stt(Hh["pT"].ap, Hh["t1"].ap, sc_["rk"][:, hcol], Umat, ALU.mult, ALU.mult, [Hh["t1"], csm, cpk], [Hh["pT"]])
                for s_, (ci, h) in enumerate(streams):
                    Hh, bank = HT[s_], 2 + s_
                    tr(psum[bank][:, 0:128], psB[bank][0], Hh["RT0"].ap, identf, [Hh["RT0"], cpk])
                    cp(Hh["R0"].ap, psum[bank][:, 0:128], [psB[bank][0]], [Hh["R0"]], eng="act")
                    tt(Hh["Y"].ap, psum[bank][:, 0:128], identf, ALU.add, [psB[bank][0], cpk], [Hh["Y"]])
                for lvl in range(1, 7):
                    pr, nx = (lvl - 1) % 2, lvl % 2
                    for s_, (ci, h) in enumerate(streams):
                        Hh, bank = HT[s_], 2 + s_
                        Rp, RTp = Hh[f"R{pr}"], Hh[f"RT{pr}"]
                        Rn, RTn = Hh[f"R{nx}"], Hh[f"RT{nx}"]
                        mm(psum[bank][:, 128:256], psB[bank][1], Rp.ap, RTp.ap, [Rp, RTp])
                        if lvl < 6:
                            mm(psum[bank][:, 0:128], psB[bank][0], RTp.ap, Rp.ap, [Rp, RTp])
                        cp(RTn.ap, psum[bank][:, 128:256], [psB[bank][1]], [RTn], eng="act")
                        if lvl < 6:
                            cp(Rn.ap, psum[bank][:, 0:128], [psB[bank][0]], [Rn], eng="dve")
                    for s_, (ci, h) in enumerate(streams):
                        Hh, bank = HT[s_], 2 + s_
                        RTn = Hh[f"RT{nx}"]
                        mm(psum[bank][:, 256:384], psB[bank][2], RTn.ap, Hh["Y"].ap, [RTn, Hh["Y"]])
                        tt(Hh["Y"].ap, psum[bank][:, 256:384], Hh["Y"].ap, ALU.add, [psB[bank][2], Hh["Y"]], [Hh["Y"]])
                for s_, (ci, h) in enumerate(streams):
                    Hh, bank = HT[s_], 2 + s_
                    mm(psum[bank][:, 0:128], psB[bank][0], Hh["Xk"].ap, Hh["Y"].ap, [Hh["Xk"], Hh["Y"]])
                    if own:
                        tr(psum[bank][:, 128:256], psB[bank][1], Hh["qg"].ap, identf, [Hh["qg"], cpk])
                    act(Hh["wTn"].ap, psum[bank][:, 0:128], AF.Copy, [psB[bank][0]], [Hh["wTn"]], scale=-1.0)
                    if own:
                        cp(Hh["qgT"].ap, psum[bank][:, 128:256], [psB[bank][1]], [Hh["qgT"]], eng="dve")
                for s_, (ci, h) in enumerate(streams):
                    Hh, sc_, csm, csmq, bank = HT[s_], SC[ci], csmL[ci], csmqL[ci], 2 + s_
                    S = Sst[hh * 3 + h]
                    mm(psum[bank][:, 0:128], psB[bank][0], Hh["Y"].ap, Hh["Xv"].ap, [Hh["Y"], Hh["Xv"]], start=True, stop=False)
                    mm(psum[bank][:, 0:128], psB[bank][0], Hh["wTn"].ap, S.ap, [Hh["wTn"], S], start=False, stop=True)
                    cp(Hh["delta"].ap, psum[bank][:, 0:128], [psB[bank][0]], [Hh["delta"]], eng="dve")
                    if own:
                        mm(psum[bank][:, 128:256], psB[bank][1], Hh["qgT"].ap, S.ap, [Hh["qgT"], S], start=True, stop=False)
                        mm(psum[bank][:, 128:256], psB[bank][1], Hh["pT"].ap, Hh["delta"].ap, [Hh["pT"], Hh["delta"]], start=False, stop=True)
                    mm(psum[bank][:, 256:384], psB[bank][2], Hh["kd"].ap, Hh["delta"].ap, [Hh["kd"], Hh["delta"]])
                    stt(S.ap, S.ap, sc_["gc"][:, h:h + 1], psum[bank][:, 256:384], ALU.mult, ALU.add, [S, csm, psB[bank][2]], [S])
                    if own:
                        ts(Hh["on"].ap, psum[bank][:, 128:256], sc_["so"][:, h:h + 1], ALU.mult, [psB[bank][1], csm], [Hh["on"]])
                        act(Hh["t1"].ap, Hh["on"].ap, AF.Square, [Hh["on"]], [Hh["t1"], csmq], accum=csmq.ap[:, 16 + h:17 + h])
                if own:
                    for ci in range(2):
                        csm, csmq = csmL[ci], csmqL[ci]
                        act(csm.ap[:, 32:35], csmq.ap[:, 16:19], AF.Ln, [csmq], [csm], scale=1.0 / 128, bias=EPS)
                        act(csm.ap[:, 32:35], csm.ap[:, 32:35], AF.Exp, [csm], [csm], scale=-0.5)
                    for s_, (ci, h) in enumerate(streams):
                        Hh, csm = HT[s_], csmL[ci]
                        ts(Hh["on"].ap, Hh["on"].ap, csm.ap[:, 32 + h:33 + h], ALU.mult, [Hh["on"], csm], [Hh["on"]])
                    for s_, (ci, h) in enumerate(streams):
                        Hh, bank = HT[s_], 2 + s_
                        tr(psum[bank][:, 128:256], psB[bank][1], Hh["on"].ap, identf, [Hh["on"], cpk])
                        stt(oT.ap[:, hh * 3 + h, CS[ci]], psum[bank][:, 128:256], cpk.ap[:, C_DNW:C_DNW + 1], zs.ap[:, h, CS[ci]],
                            ALU.mult, ALU.mult, [psB[bank][1], cpk, zs], [oT])

        SKIP = os.environ.get("KSKIP", "").split(",")
        try:
            for hh in range(2):
                if "gdn" in SKIP:
                    break
                gdn_proj(hh, 0)
                chk("gproj", [qkvT.dump()])
                gdn_chunks(hh, 0)
            chk("gpre", [Sst[0].dump(), Sst[5].dump()])
            P.barrier()
            for hh in range(2):
                if "gdn" in SKIP:
                    break
                gdn_proj(hh, 1)
                gdn_chunks(hh, 1)
            chk("gdn", [oT.dump()])
            P.barrier()
            SA = Alloc(W0, AW)
            swab = SA.get(3 * 768, "swab", shape=(3, 768))
            dma("sp", swab.ap, swab_d.rearrange("t k n -> k t n"), swab, wr=[swab])
            wg = SA.get(5120, "wg", BF16, shape=(16, 640))
            qn = SA.get(192, "qn", BF16)
            kn = SA.get(64, "kn", BF16)
            qTs = SA.get(192, "qTs", BF16)
            KT = [SA.get(64, f"KT{i}", BF16) for i in range(2)]
            VT = [SA.get(64, f"VT{i}", BF16) for i in range(2)]
            sc = SA.get(384, "sc")
            ETb = [SA.get(192, f"ET{i}", BF16) for i in range(2)]
            den = SA.get(384, "den")
            ssm = SA.get(16, "ssm")
            esink = SA.get(8, "esink")
            sjunk = SA.get(128, "sjunk")
            act(esink.ap[:, 0:6], cpk.ap[:, C_SINK:C_SINK + 6], AF.Exp, [cpk], [esink])
            for g in range(2):
                for (c0, n, o0) in ((OFF_SQ + g * 384, 384, 0), (OFF_SK + g * 128, 128, 384), (OFF_SV + g * 128, 128, 512)):
                    dma("pool", wg.ap[:, :, o0:o0 + n], win_d[:, c0:c0 + n].rearrange("(k p) n -> p k n", p=128), wg, wr=[wg])
                for ti in range(9):
                    src, sl = (hHalo, slice(0, 128)) if ti == 0 else (hT, slice((ti - 1) * 128, ti * 128))
                    cur, prv = ti % 2, 1 - (ti % 2)
                    for kt in range(16):
                        mm(psum[0][:, :], psB[0], src.ap[:, kt, sl], wg.ap[:, kt, 0:512], [src, wg], start=(kt == 0), stop=(kt == 15))
                    for kt in range(16):
                        mm(psum[1][:, 0:128], psB[1], src.ap[:, kt, sl], wg.ap[:, kt, 512:640], [src, wg], start=(kt == 0), stop=(kt == 15))
                    for j in range(4):
                        act(sjunk.ap, psum[0][:, j * 128:(j + 1) * 128], AF.Square, [psB[0]], [sjunk, ssm], accum=ssm.ap[:, j:j + 1])
                    rstd_from_ssq(ssm.ap[:, 8:12], ssm.ap[:, 4:8], ssm.ap[:, 0:4], 128, [ssm], [ssm])
                    if ti > 0:
                        for j in range(3):
                            ts(qn.ap[:, j * 128:(j + 1) * 128], psum[0][:, j * 128:(j + 1) * 128], ssm.ap[:, 8 + j:9 + j], ALU.mult, [psB[0], ssm], [qn])
                    ts(kn.ap, psum[0][:, 384:512], ssm.ap[:, 11:12], ALU.mult, [psB[0], ssm], [kn])
                    cp(VT[cur].ap, psum[1][:, 0:128], [psB[1]], [VT[cur]], eng="act")
                    pb = psum[2][:].bitcast(BF16)
                    if ti > 0:
                        for j in range(3):
                            tr(pb[:, j * 128:(j + 1) * 128], psB[2], qn.ap[:, j * 128:(j + 1) * 128], identb, [qn, cb16])
                    tr(pb[:, 384:512], psB[2], kn.ap, identb, [kn, cb16])
                    if ti > 0:
                        ts(qTs.ap, pb[:, 0:384], cpk.ap[:, C_WQ:C_WQ + 1], ALU.mult, [psB[2], cpk], [qTs])
                    ts(KT[cur].ap, pb[:, 384:512], cpk.ap[:, C_WK:C_WK + 1], ALU.mult, [psB[2], cpk], [KT[cur]])
                    if ti == 0:
                        continue
                    for kb, (Kt, bank) in enumerate(((KT[prv], 3), (KT[cur], 4))):
                        mm(psum[bank][:, 0:384], psB[bank], Kt.ap, qTs.ap, [Kt, qTs])
                        bsel = 0 if kb == 1 else (2 if ti == 1 else 1)
                        stt(sc.ap, psum[bank][:, 0:384], float(128 ** -0.5), swab.ap[:, bsel, g * 384:(g + 1) * 384], ALU.mult, ALU.add,
                            [psB[bank], swab], [sc])
                        act(ETb[kb].ap, sc.ap, AF.Exp, [sc], [ETb[kb]])
                    mm(psum[5][:, 0:384], psB[5], VT[prv].ap, ETb[0].ap, [VT[prv], ETb[0]], start=True, stop=False)
                    mm(psum[5][:, 0:384], psB[5], VT[cur].ap, ETb[1].ap, [VT[cur], ETb[1]], start=False, stop=True)
                    mm(psum[6][:, 0:384], psB[6], onesb, ETb[0].ap, [cb16, ETb[0]], start=True, stop=False)
                    mm(psum[6][:, 0:384], psB[6], onesb, ETb[1].ap, [cb16, ETb[1]], start=False, stop=True)
                    for j in range(3):
                        ts(den.ap[:, j * 128:(j + 1) * 128], psum[6][:, j * 128:(j + 1) * 128], esink.ap[:, 3 * g + j:3 * g + j + 1], ALU.add,
                           [psB[6], esink], [den])
                    recip(den.ap, den.ap, [den], [den])
                    tt(oT.ap[:, 6 + 3 * g:9 + 3 * g, sl], psum[5][:, 0:384].rearrange("p (a b) -> p a b", a=3),
                       den.ap.rearrange("p (a b) -> p a b", a=3), ALU.mult, [psB[5], den], [oT])
            chk("swa", [oT.dump()])
            MA = Alloc(SA.p, AW)
            hmT = MA.get(2048, "hmT", BF16, shape=(16, 256))
            wm = MA.get(4096, "wm", BF16, shape=(16, 512))
            KM = MA.get(512, "KM", BF16, shape=(4, 256))
            VM = [MA.get(256, f"VM{i}", BF16) for i in range(2)]
            kmn = MA.get(256, "kmn", BF16)
            qmT = MA.get(256, "qmT", BF16)
            EM = MA.get(512, "EM", BF16)
            dnm = MA.get(512, "dnm")
            msm = MA.get(16, "msm")
            mjunk = MA.get(128, "mjunk")
            norm_T(mem_d, [(0, 0), (1, 1)], 1, hmT, Alloc(MA.p, AW))

            def headnorm_T(src_ps_bank, dst3d, wcol):
                for j in range(4):
                    act(mjunk.ap, psum[src_ps_bank][:, j * 128:(j + 1) * 128], AF.Square, [psB[src_ps_bank]], [mjunk, msm], accum=msm.ap[:, j:j + 1])
                rstd_from_ssq(msm.ap[:, 8:12], msm.ap[:, 4:8], msm.ap[:, 0:4], 128, [msm], [msm])
                for j in range(4):
                    ts(kmn.ap[:, j * 128:(j + 1) * 128], psum[src_ps_bank][:, j * 128:(j + 1) * 128], msm.ap[:, 8 + j:9 + j], ALU.mult,
                       [psB[src_ps_bank], msm], [kmn])
                pb = psum[3][:].bitcast(BF16)
                for j in range(4):
                    tr(pb[:, j * 128:(j + 1) * 128], psB[3], kmn.ap[:, j * 128:(j + 1) * 128], identb, [kmn, cb16])
                ts(dst3d[0], pb[:, 0:512].rearrange("p (a b) -> p a b", a=4), cpk.ap[:, wcol:wcol + 1], ALU.mult, [psB[3], cpk], [dst3d[1]])

            wload(wm, wmkv_d, 0, 512)
            for mt in range(2):
                for kt in range(16):
                    mm(psum[2][:, :], psB[2], hmT.ap[:, kt, mt * 128:(mt + 1) * 128], wm.ap[:, kt, :], [hmT, wm], start=(kt == 0), stop=(kt == 15))
                headnorm_T(2, (KM.ap[:, :, mt * 128:(mt + 1) * 128], KM), C_XK)
            wload(wm, wmkv_d, 512, 512)
            for mt in range(2):
                for kt in range(16):
                    mm(psum[2][:, :], psB[2], hmT.ap[:, kt, mt * 128:(mt + 1) * 128], wm.ap[:, kt, :], [hmT, wm], start=(kt == 0), stop=(kt == 15))
                cp(VM[mt].ap, psum[2][:, :], [psB[2]], [VM[mt]])
            wload(wm, win_d, OFF_MQ, 512)
            for t in range(8):
                tsl = slice(t * 128, (t + 1) * 128)
                for kt in range(16):
                    mm(psum[2][:, :], psB[2], hT.ap[:, kt, tsl], wm.ap[:, kt, :], [hT, wm], start=(kt == 0), stop=(kt == 15))
                headnorm_T(2, (qmT.ap.rearrange("p (a b) -> p a b", a=4), qmT), C_XQ)
                for mt in range(2):
                    for h in range(4):
                        mm(psum[4 + mt][:, h * 128:(h + 1) * 128], psB[4 + mt], KM.ap[:, h, mt * 128:(mt + 1) * 128], qmT.ap[:, h * 128:(h + 1) * 128], [KM, qmT])
                    act(EM.ap[:, mt * 512:(mt + 1) * 512], psum[4 + mt][:, :], AF.Exp, [psB[4 + mt]], [EM], scale=float(128 ** -0.5))
                for h in range(4):
                    for mt in range(2):
                        mm(psum[6][:, h * 128:(h + 1) * 128], psB[6], VM[mt].ap[:, h * 128:(h + 1) * 128], EM.ap[:, mt * 512 + h * 128:mt * 512 + (h + 1) * 128],
                           [VM[mt], EM], start=(mt == 0), stop=(mt == 1))
                for mt in range(2):
                    mm(psum[7][:, :], psB[7], onesb, EM.ap[:, mt * 512:(mt + 1) * 512], [cb16, EM], start=(mt == 0), stop=(mt == 1))
                recip(dnm.ap, psum[7][:, :], [psB[7]], [dnm])
                tt(oT.ap[:, 12:16, tsl], psum[6][:, :].rearrange("p (a b) -> p a b", a=4), dnm.ap.rearrange("p (a b) -> p a b", a=4), ALU.mult,
                   [psB[6], dnm], [oT])
            chk("mem", [oT.dump()])
            P.barrier()
            GA2 = Alloc(W0, AW)
            mT = GA2.get(8192, "mT", BF16, shape=(16, TOK))
            mslab = [GA2.get(4096, f"msl{i}", BF16, shape=(16, 512)) for i in range(3)]
            sg = [GA2.get(512, f"sg{i}") for i in range(6)]
            accs = [GA2.get(512, f"acc{i}") for i in range(2)]
            wo = [GA2.get(4096, "wo0", BF16, shape=(16, 512)), mslab[1]]
            brk = (range(0, 6), range(6, 12), range(12, 16))

            def load_mslab(mt):
                sl = mslab[mt % 3]
                dma("pool", sl.ap[:, :, 0:128], pall_d[:, mt * 128:(mt + 1) * 128].rearrange("(k p) n -> p k n", p=128), sl, wr=[sl])
                for br in range(3):
                    c0 = OFF_G + br * 2048 + mt * 128
                    dma("pool", sl.ap[:, :, 128 * (br + 1):128 * (br + 2)], win_d[:, c0:c0 + 128].rearrange("(k p) n -> p k n", p=128), sl, wr=[sl])

            load_mslab(0)
            load_mslab(1)
            pcount = 0
            for mt in range(16):
                if mt + 2 < 16:
                    load_mslab(mt + 2)
                elif mt + 2 < 18:
                    wload(wo[mt + 2 - 16], wout_d, (mt + 2 - 16) * 512, 512)
                sl = mslab[mt % 3]
                for c2 in range(2):
                    cols = slice(c2 * 512, (c2 + 1) * 512)
                    acc = accs[c2]
                    for br in range(3):
                        gb = c2 * 3 + br
                        pbk = 6 + (pcount % 2)
                        pcount += 1
                        sgb = sg[c2 * 3 + br]
                        for kt in range(16):
                            mm(psum[gb][:, :], psB[gb], sl.ap[:, kt, 128 * (br + 1):128 * (br + 2)], hT.ap[:, kt, cols], [sl, hT], start=(kt == 0), stop=(kt == 15))
                        act(sgb.ap, psum[gb][:, :], AF.Sigmoid, [psB[gb]], [sgb])
                        kts = list(brk[br])
                        for kt in kts:
                            mm(psum[pbk][:, :], psB[pbk], sl.ap[:, kt, 0:128], oT.ap[:, kt, cols], [sl, oT], start=(kt == kts[0]), stop=(kt == kts[-1]))
                        if br == 0:
                            tt(acc.ap, psum[pbk][:, :], sgb.ap, ALU.mult, [psB[pbk], sgb], [acc])
                        else:
                            tt(sgb.ap, psum[pbk][:, :], sgb.ap, ALU.mult, [psB[pbk], sgb], [sgb])
                            if br == 1:
                                tt(acc.ap, acc.ap, sgb.ap, ALU.add, [acc, sgb], [acc])
                            else:
                                tt(mT.ap[:, mt, cols], acc.ap, sgb.ap, ALU.add, [acc, sgb], [mT])
            chk("merge", [mT.dump()])
            x1 = T(arena[:, 4096:4096 + 16384].rearrange("p (t d) -> p t d", t=8), "x1", 4096, 16384)
            x1t = [Buf(f"x1t{t}") for t in range(8)]
            for t in range(8):
                dma("sp", x1.ap[:, t, :], x_d[t * 128:(t + 1) * 128, :], x1t[t], wr=[x1t[t], hT, oT, hP])
            pc = 0
            for cc in range(4):
                w = wo[cc % 2]
                if cc >= 2:
                    wload(w, wout_d, cc * 512, 512)
                for t in range(8):
                    bank = pc % 6
                    pc += 1
                    for kt in range(16):
                        mm(psum[bank][:, :], psB[bank], mT.ap[:, kt, t * 128:(t + 1) * 128], w.ap[:, kt, :], [mT, w], start=(kt == 0), stop=(kt == 15))
                    tt(x1.ap[:, t, cc * 512:(cc + 1) * 512], psum[bank][:, :], x1.ap[:, t, cc * 512:(cc + 1) * 512], ALU.add, [psB[bank], x1t[t]], [x1t[t]])
            chk("x1", [(x1t[t], 4096 + t * 2048, 2048) for t in range(8)])
            P.barrier()
            ML = Alloc(W0, AW)
            h2T = ML.get(8192, "h2T", BF16, shape=(16, TOK))
            wu = [ML.get(4096, f"wu{i}", BF16, shape=(16, 512)) for i in range(2)]
            wd = [ML.get(4096, f"wd{i}", BF16, shape=(4, 2048)) for i in range(2)]
            aT = [ML.get(2048, f"aT{i}", BF16, shape=(4, TOK)) for i in range(2)]
            rl = [ML.get(512, f"rl{i}") for i in range(2)]

            def load_wu(fb):
                wload(wu[fb % 2], wup_d, fb * 512, 512)

            def load_wd(fb):
                dma("pool", wd[fb % 2].ap, wdn_d[fb * 512:(fb + 1) * 512, :].rearrange("(k p) n -> p k n", p=128), wd[fb % 2], wr=[wd[fb % 2]])

            load_wu(0)
            load_wd(0)
            norm_T(None, [(t, t) for t in range(8)], 2, h2T, Alloc(ML.p, AW), src_sb=lambda t: (x1t[t], x1.ap[:, t, :]))
            upc = [0]

            def mlp_up(fb):
                w = wu[fb % 2]
                a = aT[fb % 2]
                for f4 in range(4):
                    for tc in range(2):
                        bank = upc[0] % 2
                        r = rl[upc[0] % 2]
                        upc[0] += 1
                        for kt in range(16):
                            mm(psum[bank][:, :], psB[bank], w.ap[:, kt, f4 * 128:(f4 + 1) * 128], h2T.ap[:, kt, tc * 512:(tc + 1) * 512], [w, h2T],
                               start=(kt == 0), stop=(kt == 15))
                        act(r.ap, psum[bank][:, :], AF.Relu, [psB[bank]], [r])
                        tt(a.ap[:, f4, tc * 512:(tc + 1) * 512], r.ap, r.ap, ALU.mult, [r], [a])

            dnc = [0]

            def mlp_down(fb):
                w = wd[fb % 2]
                a = aT[fb % 2]
                for t in range(8):
                    for cc in range(4):
                        bank = 2 + dnc[0] % 6
                        dnc[0] += 1
                        for k in range(4):
                            mm(psum[bank][:, :], psB[bank], a.ap[:, k, t * 128:(t + 1) * 128], w.ap[:, k, cc * 512:(cc + 1) * 512], [a, w],
                               start=(k == 0), stop=(k == 3))
                        tt(x1.ap[:, t, cc * 512:(cc + 1) * 512], psum[bank][:, :], x1.ap[:, t, cc * 512:(cc + 1) * 512], ALU.add, [psB[bank], x1t[t]], [x1t[t]])

            for fb in range(16):
                if fb + 1 < 16:
                    load_wu(fb + 1)
                mlp_up(fb)
                if fb > 0:
                    mlp_down(fb - 1)
                if fb + 1 < 16:
                    load_wd(fb + 1)
            mlp_down(15)
            for t in range(8):
                dma("sp", y_d[t * 128:(t + 1) * 128, :], x1.ap[:, t, :], x1t[t], rd=[x1t[t]])
        except StopBuild:
            pass
        return finish(nc, P, st, dbg_d, dumps, dma, locals())


def finish(nc, P, st, dbg_d, dumps, dma, L):
    if dbg_d is not None:
        off = 0
        arena = L["arena"]
        for (t, a0, words) in dumps:
            dma("sp", dbg_d[:, off:off + words], arena[:, a0:a0 + words], t, rd=[t])
            off += words
    P.final_wait()
    P.emit(nc)
    return nc


def _t5_bucket(dist):
    import math
    n = np.maximum(dist, 0)
    max_exact = 16
    nf = np.maximum(n, 1).astype(np.float32)
    large = max_exact + (np.log(nf / max_exact) / math.log(128 / max_exact) * (32 - max_exact)).astype(np.int32)
    large = np.minimum(large, 31)
    return np.where(n < max_exact, n, large)


def make_in_maps(inp):
    f32 = np.float32
    x = np.asarray(inp["x"], f32)
    mem = np.asarray(inp["mem"], f32)
    w_in = np.ascontiguousarray(np.asarray(inp["w_in"], f32)[0])
    w_mem_kv = np.ascontiguousarray(np.asarray(inp["w_mem_kv"], f32)[0])
    p_all = np.ascontiguousarray(np.concatenate([np.asarray(inp["p_dn"], f32)[0], np.asarray(inp["p_swa"], f32)[0],
                                                 np.asarray(inp["p_mem"], f32)[0]], axis=0))
    w_out = np.ascontiguousarray(np.asarray(inp["w_out"], f32)[0])
    w_up = np.ascontiguousarray(np.asarray(inp["w_mlp_up"], f32)[0])
    w_down = np.ascontiguousarray(np.asarray(inp["w_mlp_down"], f32)[0])
    nw3 = np.ascontiguousarray(np.stack([np.asarray(inp["attn_norm_w"], f32)[0], np.asarray(inp["mem_norm_w"], f32)[0],
                                         np.asarray(inp["mlp_norm_w"], f32)[0]], axis=0))
    cp = np.zeros((128, NCP), f32)
    idx = np.arange(128)
    cp[:, C_ID:C_ID + 128] = np.eye(128, dtype=f32)
    cp[:, C_U:C_U + 128] = (idx[:, None] <= idx[None, :]).astype(f32)
    cp[:, C_SL:C_SL + 128] = (idx[:, None] > idx[None, :]).astype(f32)
    cp[:, C_NSL:C_NSL + 128] = -(idx[:, None] > idx[None, :]).astype(f32)
    cp[:, C_ONE:C_ONE + 128] = 1.0
    cw = np.asarray(inp["dn_conv_w"], f32)[0]
    cp[:, C_CONV:C_CONV + 72] = cw.reshape(4, 18, 128).transpose(2, 1, 0).reshape(128, 72)
    cp[:, C_DNW] = np.asarray(inp["dn_out_norm_w"], f32)[0]
    cp[:, C_WQ] = np.asarray(inp["swa_q_norm_w"], f32)[0]
    cp[:, C_WK] = np.asarray(inp["swa_k_norm_w"], f32)[0]
    cp[:, C_XQ] = np.asarray(inp["xq_norm_w"], f32)[0]
    cp[:, C_XK] = np.asarray(inp["xk_norm_w"], f32)[0]
    cp[:, C_ALOG:C_ALOG + 6] = np.asarray(inp["dn_A_log"], f32)[0][None, :]
    cp[:, C_DTB:C_DTB + 6] = np.asarray(inp["dn_dt_bias"], f32)[0][None, :]
    cp[:, C_SINK:C_SINK + 6] = np.asarray(inp["swa_sinks"], f32)[0][None, :]
    cb16 = np.concatenate([np.eye(128, dtype=f32), np.ones((128, 128), f32)], axis=1).astype(ml_dtypes.bfloat16)
    rb = np.asarray(inp["rel_bias"], f32)
    qi = np.arange(128)[:, None]
    kj = np.arange(256)[None, :]
    dist = qi - kj + 128
    valid = (dist >= 0) & (dist < 128)
    bias = rb[_t5_bucket(dist)]
    bias = np.where(valid[:, :, None], bias, f32(NEG)).astype(f32)
    biasT = bias.transpose(1, 2, 0)
    b_prev = np.ascontiguousarray(biasT[0:128].reshape(128, 768))
    b_cur = np.ascontiguousarray(biasT[128:256].reshape(128, 768))
    b_none = np.full((128, 768), NEG, f32)
    maps = []
    for c in range(8):
        b, hf = c // 2, c % 2
        xo = np.ascontiguousarray(x[b, hf * TOK:(hf + 1) * TOK])
        xp = np.ascontiguousarray(x[b, 0:TOK]) if hf == 1 else np.zeros((TOK, D), f32)
        swab = np.stack([b_cur, b_prev, b_prev if hf == 1 else b_none], axis=0)
        maps.append({"x": xo, "xp": xp, "mem": np.ascontiguousarray(mem[b]), "w_in": w_in, "w_mem_kv": w_mem_kv,
                     "p_all": p_all, "w_out": w_out, "w_up": w_up, "w_down": w_down, "nw3": nw3, "cpack": cp,
                     "cb16": cb16, "swab": np.ascontiguousarray(swab)})
    return maps


_NC_CACHE = {}


def kernel(**inputs):
    maps = make_in_maps(inputs)
    if "nc" not in _NC_CACHE:
        _NC_CACHE["nc"] = build()
    res = run_bass_kernel_spmd(_NC_CACHE["nc"], maps, core_ids=list(range(8)))
    out = np.zeros((4, 2048, D), np.float32)
    for c in range(8):
        b, hf = c // 2, c % 2
        out[b, hf * TOK:(hf + 1) * TOK] = res.results[c]["y"]
    return out
```

```python
import os
import numpy as np
import ml_dtypes
from contextlib import ExitStack
import concourse.bass as bass
import concourse.mybir as mybir
from concourse.bass_utils import run_bass_kernel_spmd

F32 = mybir.dt.float32
BF16 = mybir.dt.bfloat16
AF = mybir.ActivationFunctionType
ALU = mybir.AluOpType

EPS = 1e-6
D = 2048
TOK = 1024
NT = 8
OFF_QKV, OFF_Z, OFF_B, OFF_A, OFF_SQ, OFF_SK, OFF_SV, OFF_MQ, OFF_G = 0, 2304, 3072, 3078, 3084, 3852, 4108, 4364, 4876
IN_W = 11020
NEG = -30000.0

C_ID, C_U, C_SL, C_NSL, C_ONE, C_CONV, C_DNW, C_WQ, C_WK, C_XQ, C_XK, C_ALOG, C_DTB, C_SINK, NCP = \
    0, 128, 256, 384, 512, 640, 712, 713, 714, 715, 716, 717, 723, 729, 736


class Buf:
    __slots__ = ("name", "lw", "rd")

    def __init__(self, name):
        self.name = name
        self.lw = None
        self.rd = {}


class Prog:
    ENG = ("pe", "act", "dve", "pool", "sp")

    def __init__(self):
        self.ops = {e: [] for e in self.ENG}
        self.cnt = {e: 0 for e in self.ENG}
        self.waited = {e: {} for e in self.ENG}
        self.dcnt = {}
        self.swq = []

    def _deps(self, reads, writes):
        toks = []
        for b in reads:
            if b.lw is not None:
                toks.append(b.lw)
        for b in writes:
            if b.lw is not None:
                toks.append(b.lw)
            toks.extend(b.rd.items())
        return toks

    def _filter(self, eng, toks):
        w = self.waited[eng]
        out = {}
        for (k, v) in toks:
            if eng == "pe" and k == "Epe":
                continue
            if w.get(k, 0) >= v:
                continue
            if out.get(k, 0) < v:
                out[k] = v
        for k, v in out.items():
            w[k] = v
        return list(out.items())

    def _record(self, tok, reads, writes):
        for b in writes:
            b.lw = tok
            b.rd = {}
        for b in reads:
            if b in writes:
                continue
            if b.rd.get(tok[0], 0) < tok[1]:
                b.rd[tok[0]] = tok[1]

    def op(self, eng, fn, reads=(), writes=()):
        extra = [b for b in reads if b.name.startswith("ps") and b not in writes]
        if extra:
            writes = list(writes) + extra
        waits = self._filter(eng, self._deps(reads, writes))
        self.cnt[eng] += 1
        tok = ("E" + eng, self.cnt[eng])
        self.ops[eng].append((waits, fn, ("E" + eng, 1)))
        self._record(tok, reads, writes)
        return tok

    def dma(self, eng, fn, sembuf, reads=(), writes=(), ndesc=0):
        toks = self._deps(reads, writes)
        if eng == "pool" and ndesc:
            while self.swq and sum(n for _, n in self.swq) + ndesc > 640:
                toks.append(self.swq.pop(0)[0])
        waits = self._filter(eng, toks)
        key = "D" + sembuf.name
        self.dcnt[key] = self.dcnt.get(key, 0) + 16
        tok = (key, self.dcnt[key])
        self.ops[eng].append((waits, fn, (key, 16)))
        self._record(tok, reads, writes)
        if eng == "pool" and ndesc:
            self.swq.append((tok, ndesc))
        return tok

    def barrier(self):
        toks = [("E" + e, self.cnt[e]) for e in self.ENG if self.cnt[e] > 0] + list(self.dcnt.items())
        for e in self.ENG:
            waits = self._filter(e, toks)
            if waits:
                self.ops[e].append((waits, None, None))

    def final_wait(self, eng="sp"):
        waits = self._filter(eng, list(self.dcnt.items()))
        self.ops[eng].append((waits, None, None))

    def emit(self, nc):
        keys = ["E" + e for e in self.ENG] + sorted(self.dcnt.keys())
        with ExitStack() as st:
            sems = {}
            for k in keys:
                sems[k] = st.enter_context(nc.semaphore("s_" + k))
            block = st.enter_context(nc.Block())
            binders = {"pe": block.tensor, "act": block.scalar, "dve": block.vector,
                       "pool": block.gpsimd, "sp": block.sync}
            for eng in self.ENG:
                ops = self.ops[eng]

                def body(e, ops=ops):
                    for waits, fn, inc in ops:
                        for k, v in waits:
                            e.wait_ge(sems[k], v)
                        if fn is None:
                            continue
                        ins = fn(e)
                        ins.then_inc(sems[inc[0]], inc[1])
                binders[eng](body)


class StopBuild(Exception):
    pass


class T:
    __slots__ = ("ap", "b", "off", "words")

    def __init__(self, ap, name, off=None, words=None):
        self.ap = ap
        self.b = Buf(name)
        self.off = off
        self.words = words

    def dump(self):
        return (self, self.off, self.words)


def build(stop_after=None, dbg=None):
    nc = bass.Bass("TRN2", target_bir_lowering=False)
    P = Prog()
    dr = {}

    def din(name, shape, dt=F32):
        dr[name] = nc.dram_tensor(name, shape, dt, kind="ExternalInput").ap()
        return dr[name]

    x_d = din("x", [TOK, D])
    xp_d = din("xp", [TOK, D])
    mem_d = din("mem", [256, D])
    win_d = din("w_in", [D, IN_W])
    wmkv_d = din("w_mem_kv", [D, 1024])
    pall_d = din("p_all", [D, D])
    wout_d = din("w_out", [D, D])
    wup_d = din("w_up", [D, 4 * D])
    wdn_d = din("w_down", [4 * D, D])
    nw3_d = din("nw3", [3, D])
    cpack_d = din("cpack", [128, NCP])
    cb16_d = din("cb16", [128, 256], BF16)
    swab_d = din("swab", [3, 128, 768])
    y_d = nc.dram_tensor("y", [TOK, D], F32, kind="ExternalOutput").ap()
    dbg_d = None
    if dbg is not None:
        dbg_d = nc.dram_tensor("dbg", [128, dbg], F32, kind="ExternalOutput").ap()

    with ExitStack() as st:
        AW = 52800
        arena = st.enter_context(nc.sbuf_tensor("arena", [128, AW], F32))
        psum = [st.enter_context(nc.psum_tensor(f"ps{i}", [128, 512], F32)) for i in range(8)]
        psB = [[Buf(f"ps{i}")] * 4 for i in range(8)]

        uid = [0]

        def view(off, words, name, dt=F32, shape=None):
            ap = arena[:, off:off + words]
            if dt == BF16:
                ap = ap.bitcast(BF16)
            if shape is not None:
                ap = ap[:, 0:int(np.prod(shape))]
                if len(shape) == 2:
                    ap = ap.rearrange("p (a b) -> p a b", a=shape[0])
                elif len(shape) == 3:
                    ap = ap.rearrange("p (a b c) -> p a b c", a=shape[0], b=shape[1])
            uid[0] += 1
            return T(ap, f"{name}_{uid[0]}", off, words)

        class Alloc:
            def __init__(self, lo, hi):
                self.lo, self.hi, self.p = lo, hi, lo

            def get(self, words, name, dt=F32, shape=None):
                o = self.p
                self.p += words
                assert self.p <= self.hi, (name, self.p, self.hi)
                return view(o, words, name, dt, shape)

        def bl(ts):
            out = []
            for t in ts:
                if isinstance(t, T):
                    out.append(t.b)
                elif isinstance(t, Buf):
                    out.append(t)
                elif isinstance(t, (list, tuple)):
                    out.extend(bl(t))
            return out

        def mm(out_ap, outb, lhsT, rhs, rd, start=True, stop=True):
            P.op("pe", lambda e: e.matmul(out_ap, lhsT=lhsT, rhs=rhs, start=start, stop=stop), reads=bl(rd), writes=bl([outb]))

        def tr(out_ap, outb, in_ap, ident_ap, rd):
            P.op("pe", lambda e: e.transpose(out=out_ap, in_=in_ap, identity=ident_ap), reads=bl(rd), writes=bl([outb]))

        def act(out_ap, in_ap, func, rd, wr, scale=None, bias=None, accum=None, eng="act"):
            kw = {}
            if scale is not None:
                kw["scale"] = scale
            if bias is not None:
                kw["bias"] = bias
            if accum is not None:
                kw["accum_out"] = accum
            P.op("act", lambda e: e.activation(out=out_ap, in_=in_ap, func=func, **kw), reads=bl(rd), writes=bl(wr))

        def tt(out_ap, a, b, op, rd, wr, eng="dve"):
            P.op(eng, lambda e: e.tensor_tensor(out=out_ap, in0=a, in1=b, op=op), reads=bl(rd), writes=bl(wr))

        def ts(out_ap, a, s1, op0, rd, wr, s2=None, op1=None, eng="dve"):
            if op1 is None:
                P.op(eng, lambda e: e.tensor_scalar(out=out_ap, in0=a, scalar1=s1, scalar2=None, op0=op0), reads=bl(rd), writes=bl(wr))
            else:
                P.op(eng, lambda e: e.tensor_scalar(out=out_ap, in0=a, scalar1=s1, scalar2=s2, op0=op0, op1=op1), reads=bl(rd), writes=bl(wr))

        def stt(out_ap, a, s, b, op0, op1, rd, wr):
            P.op("dve", lambda e: e.scalar_tensor_tensor(out=out_ap, in0=a, scalar=s, in1=b, op0=op0, op1=op1), reads=bl(rd), writes=bl(wr))

        def cp(out_ap, in_ap, rd, wr, eng="dve"):
            if eng == "act":
                P.op("act", lambda e: e.copy(out=out_ap, in_=in_ap), reads=bl(rd), writes=bl(wr))
            else:
                P.op(eng, lambda e: e.tensor_copy(out=out_ap, in_=in_ap), reads=bl(rd), writes=bl(wr))

        def recip(out_ap, in_ap, rd, wr):
            P.op("dve", lambda e: e.reciprocal(out=out_ap, in_=in_ap), reads=bl(rd), writes=bl(wr))

        def dma(q, out_ap, in_ap, semT, rd=(), wr=()):
            nd = 0
            if q == "pool":
                shp = list(out_ap.shape)
                nd = int(np.prod(shp[:-1])) // 16 + 2
            P.dma(q, lambda e: e.dma_start(out=out_ap, in_=in_ap), semT.b if isinstance(semT, T) else semT, reads=bl(rd), writes=bl(wr), ndesc=nd)

        dumps = []

        def chk(name, dl):
            if stop_after == name:
                dumps.extend(dl)
                raise StopBuild()

        def wload(slab, w_dram, c0, n, kt=16, r0=0):
            dma("pool", slab.ap, w_dram[r0:r0 + kt * 128, c0:c0 + n].rearrange("(k p) n -> p k n", p=128), slab, wr=[slab])

        def rstd_from_ssq(out_ap, tmp_ap, ssq_ap, n, rd, wr):
            act(tmp_ap, ssq_ap, AF.Ln, rd, wr, scale=1.0 / n, bias=EPS)
            act(out_ap, tmp_ap, AF.Exp, wr, wr, scale=-0.5)

        CONST = Alloc(0, 4096)
        cpk = CONST.get(NCP, "cpack")
        cb16 = CONST.get(128, "cb16", BF16)
        wbc = CONST.get(2048, "wbc")
        smalls = CONST.get(64, "smalls")
        identf = cpk.ap[:, C_ID:C_ID + 128]
        Umat = cpk.ap[:, C_U:C_U + 128]
        SLm = cpk.ap[:, C_SL:C_SL + 128]
        NSLm = cpk.ap[:, C_NSL:C_NSL + 128]
        onesf = cpk.ap[:, C_ONE:C_ONE + 128]
        identb = cb16.ap[:, 0:128]
        onesb = cb16.ap[:, 128:256]
        hT = view(4096, 8192, "hT", BF16, shape=(16, TOK))
        hP = view(4096 + 8192, 8192, "hP", BF16, shape=(16, TOK))
        oT = T(hP.ap, "oT", hP.off, hP.words)
        W0 = 4096 + 16384

        dma("sp", cpk.ap, cpack_d[:, :], cpk, wr=[cpk])
        dma("sp", cb16.ap, cb16_d[:, :], cb16, wr=[cb16])

        def norm_T(src_dram, tiles, nw_row, dst, WA, src_sb=None, load_w=True):
            xt = [WA.get(2048, "xt") for _ in range(2)] if src_sb is None else None
            junk = WA.get(1024, "junk", BF16)
            xn = WA.get(1024, "xn", BF16)
            stat = WA.get(8, "stat")
            if load_w:
                dma("sp", wbc.ap, nw3_d[nw_row].partition_broadcast(128), wbc, wr=[wbc])
            for i, (t, dt_) in enumerate(tiles):
                if src_sb is None:
                    xb = xt[i % 2]
                    dma("sp", xb.ap, src_dram[t * 128:(t + 1) * 128, :], xb, wr=[xb])
                    xin = xb.ap
                else:
                    xb, xin = src_sb(t)
                act(junk.ap, xin, AF.Square, [xb], [junk, stat], accum=stat.ap[:, 0:1])
                rstd_from_ssq(stat.ap[:, 2:3], stat.ap[:, 1:2], stat.ap[:, 0:1], D, [stat], [stat])
                stt(xn.ap, xin, stat.ap[:, 2:3], wbc.ap, ALU.mult, ALU.mult, [xb, stat, wbc], [xn])
                for half in range(2):
                    bank = half
                    pb = psum[bank][:].bitcast(BF16)
                    for j in range(8):
                        kt = half * 8 + j
                        tr(pb[:, j * 128:(j + 1) * 128], psB[bank], xn.ap[:, kt * 128:(kt + 1) * 128], identb, [xn, cb16])
                    cp(dst.ap[:, half * 8:(half + 1) * 8, dt_ * 128:(dt_ + 1) * 128], pb.rearrange("p (k t) -> p k t", k=8),
                       [psB[bank]], [dst], eng=("dve" if half == 0 else "act"))

        WA = Alloc(W0, AW)
        norm_T(x_d, [(t, t) for t in range(NT)], 0, hT, WA)
        norm_T(xp_d, [(t, t) for t in range(NT)], 0, hP, Alloc(WA.p, AW), load_w=False)
        hHalo = CONST.get(1024, "hHalo", BF16, shape=(16, 128))
        cp(hHalo.ap, hP.ap[:, :, 896:1024], [hP], [hHalo])
        if stop_after == "p0":
            return finish(nc, P, st, dbg_d, [(hT, 4096, 8192), (hP, 4096 + 8192, 8192)], dma, locals())
        P.barrier()

        GA = Alloc(W0, AW)
        qkvT = GA.get(9 * 1024, "qkvT", shape=(9, TOK))
        zs = GA.get(3 * 1024, "zs", shape=(3, TOK))
        raw = [GA.get(1028, "raw") for _ in range(2)]
        wsl = [GA.get(1024, "wsl", BF16, shape=(16, 128)) for _ in range(3)]
        wba = GA.get(128, "wba", BF16, shape=(16, 12))
        ba_all = GA.get(16 * 12, "ba_all", shape=(16, 12))
        beta_all = GA.get(16 * 6, "beta_all", shape=(16, 6))
        g_all = GA.get(16 * 6, "g_all", shape=(16, 6))
        sp_tmp = GA.get(16 * 6, "sp_tmp", shape=(16, 6))
        nexpA = GA.get(8, "nexpA")
        Sst = [GA.get(128, f"S{h}") for h in range(6)]
        carry = GA.get(64, "carry", shape=(18, 3))
        HT = []
        names = ["Xk", "kd", "Xv", "qg", "gSL", "gSLc", "eE", "eET", "t1", "pT", "R0", "R1", "RT0", "RT1", "Y", "wTn", "qgT", "delta", "on"]
        GB = Alloc(wbc.off, wbc.off + 2048)
        for s_ in range(6):
            dct = {}
            for n in names:
                al = GB if (GB.p + 128 <= GB.hi) else GA
                dct[n] = al.get(128, f"{n}{s_}")
            HT.append(dct)
        csmL = [GA.get(64, f"csm{i}") for i in range(2)]
        csmqL = [GA.get(32, f"csmq{i}") for i in range(2)]
        wsl_i = [0]

        wload_ba = lambda: dma("pool", wba.ap[:, :, 0:12], win_d[:, OFF_B:OFF_B + 12].rearrange("(k p) n -> p k n", p=128), wba, wr=[wba])
        wload_ba()
        for tt_i in range(16):
            src = hP if tt_i < 8 else hT
            tl = tt_i % 8
            for kt in range(16):
                mm(psum[7][:, 0:12], psB[7][0], src.ap[:, kt, tl * 128:(tl + 1) * 128], wba.ap[:, kt, 0:12], [src, wba], start=(kt == 0), stop=(kt == 15))
            cp(ba_all.ap[:, tt_i, :], psum[7][:, 0:12], [psB[7][0]], [ba_all])
        act(beta_all.ap, ba_all.ap[:, :, 0:6], AF.Exp, [ba_all], [beta_all], scale=-1.0)
        ts(beta_all.ap, beta_all.ap, 1.0, ALU.add, [beta_all], [beta_all])
        recip(beta_all.ap, beta_all.ap, [beta_all], [beta_all])
        act(nexpA.ap[:, 0:6], cpk.ap[:, C_ALOG:C_ALOG + 6], AF.Exp, [cpk], [nexpA])
        for tt_i in range(16):
            tt(sp_tmp.ap[:, tt_i, :], ba_all.ap[:, tt_i, 6:12], cpk.ap[:, C_DTB:C_DTB + 6], ALU.add, [ba_all, cpk], [sp_tmp])
        act(sp_tmp.ap, sp_tmp.ap, AF.Exp, [sp_tmp], [sp_tmp])
        act(sp_tmp.ap, sp_tmp.ap, AF.Ln, [sp_tmp], [sp_tmp], bias=1.0)
        for tt_i in range(16):
            stt(g_all.ap[:, tt_i, :], sp_tmp.ap[:, tt_i, :], -1.0, nexpA.ap[:, 0:6], ALU.mult, ALU.mult, [sp_tmp, nexpA], [g_all])
        if stop_after == "ba":
            return finish(nc, P, st, dbg_d, [ba_all.dump(), beta_all.dump(), g_all.dump()], dma, locals())
        for h in range(6):
            P.op("dve", lambda e, h=h: e.memset(Sst[h].ap, 0.0), writes=[Sst[h].b])
        P.op("dve", lambda e: e.memset(carry.ap, 0.0), writes=[carry.b])

        def gdn_proj(hh, stage):
            src = hP if stage == 0 else hT
            fts = [hh * 3 + j for j in range(3)] + [6 + hh * 3 + j for j in range(3)] + [12 + hh * 3 + j for j in range(3)]
            for li, ft in enumerate(fts):
                w = wsl[wsl_i[0] % 3]
                wsl_i[0] += 1
                wload(w, win_d, OFF_QKV + ft * 128, 128)
                rb = raw[li % 2]
                for c2 in range(2):
                    bank = 2 + c2
                    for kt in range(16):
                        mm(psum[bank][:, :], psB[bank], w.ap[:, kt, :], src.ap[:, kt, c2 * 512:(c2 + 1) * 512], [w, src], start=(kt == 0), stop=(kt == 15))
                    cp(rb.ap[:, 3 + c2 * 512:3 + (c2 + 1) * 512], psum[bank][:, :], [psB[bank]], [rb], eng=("act" if c2 == 0 else "dve"))
                cp(rb.ap[:, 0:3], carry.ap[:, ft, :], [carry], [rb])
                cp(carry.ap[:, ft, :], rb.ap[:, 1024:1027], [rb], [carry])
                cw = cpk.ap[:, C_CONV + ft * 4:C_CONV + ft * 4 + 4]
                acc = qkvT.ap[:, li, :]
                ts(acc, rb.ap[:, 0:1024], cw[:, 0:1], ALU.mult, [rb, cpk], [qkvT])
                for k in range(1, 4):
                    stt(acc, rb.ap[:, k:k + 1024], cw[:, k:k + 1], acc, ALU.mult, ALU.add, [rb, cpk, qkvT], [qkvT])
                act(acc, acc, AF.Silu, [qkvT], [qkvT])
            if stage == 1:
                for j in range(3):
                    w = wsl[wsl_i[0] % 3]
                    wsl_i[0] += 1
                    wload(w, win_d, OFF_Z + (hh * 3 + j) * 128, 128)
                    for c2 in range(2):
                        bank = 2 + c2
                        for kt in range(16):
                            mm(psum[bank][:, :], psB[bank], w.ap[:, kt, :], hT.ap[:, kt, c2 * 512:(c2 + 1) * 512], [w, hT], start=(kt == 0), stop=(kt == 15))
                        act(zs.ap[:, j, c2 * 512:(c2 + 1) * 512], psum[bank][:, :], AF.Silu, [psB[bank]], [zs])

        def gdn_chunks(hh, stage):
            own = stage == 1
            h0 = hh * 3
            for cp_ in range(4):
                chunks = (2 * cp_, 2 * cp_ + 1)
                streams = [(ci, h) for ci in range(2) for h in range(3)]
                CS = [slice(c * 128, (c + 1) * 128) for c in chunks]
                G3 = [g_all.ap[:, stage * 8 + c, h0:h0 + 3] for c in chunks]
                B3 = [beta_all.ap[:, stage * 8 + c, h0:h0 + 3] for c in chunks]
                for ci in range(2):
                    o = 16 * ci
                    mm(psum[0][:, o:o + 3], psB[0][0], Umat, G3[ci], [cpk, g_all])
                    mm(psum[0][:, o + 3:o + 6], psB[0][0], SLm, G3[ci], [cpk, g_all])
                    mm(psum[0][:, o + 6:o + 9], psB[0][0], onesf, G3[ci], [cpk, g_all])
                    act(csmL[ci].ap[:, 0:9], psum[0][:, o:o + 9], AF.Exp, [psB[0][0]], [csmL[ci]])
                for s_, (ci, h) in enumerate(streams):
                    Hh, bank, csmq = HT[s_], 2 + s_, csmqL[ci]
                    for j in range(3):
                        tr(psum[bank][:, j * 128:(j + 1) * 128], psB[bank][j], qkvT.ap[:, j * 3 + h, CS[ci]], identf, [qkvT, cpk])
                    act(Hh["t1"].ap, psum[bank][:, 0:128], AF.Square, [psB[bank][0]], [Hh["t1"], csmq], accum=csmq.ap[:, h:h + 1])
                    act(Hh["t1"].ap, psum[bank][:, 128:256], AF.Square, [psB[bank][1]], [Hh["t1"], csmq], accum=csmq.ap[:, 3 + h:4 + h])
                    cp(Hh["qg"].ap, psum[bank][:, 0:128], [psB[bank][0]], [Hh["qg"]], eng="dve")
                    cp(Hh["Xk"].ap, psum[bank][:, 128:256], [psB[bank][1]], [Hh["Xk"]], eng="act")
                    cp(Hh["Xv"].ap, psum[bank][:, 256:384], [psB[bank][2]], [Hh["Xv"]], eng="dve")
                SC = []
                for ci in range(2):
                    csm, csmq = csmL[ci], csmqL[ci]
                    eG, eGr, gc = csm.ap[:, 0:3], csm.ap[:, 3:6], csm.ap[:, 6:9]
                    act(csmq.ap[:, 6:12], csmq.ap[:, 0:6], AF.Ln, [csmq], [csmq], bias=EPS)
                    act(csmq.ap[:, 12:15], csmq.ap[:, 6:9], AF.Exp, [csmq], [csmq], scale=-0.5)
                    act(csm.ap[:, 12:15], csmq.ap[:, 9:12], AF.Exp, [csmq], [csm], scale=-0.5)
                    ts(csm.ap[:, 15:18], csmq.ap[:, 9:12], -0.5, ALU.mult, [csmq], [csm])
                    rk, lnrk = csm.ap[:, 12:15], csm.ap[:, 15:18]
                    tt(csm.ap[:, 18:21], B3[ci], rk, ALU.mult, [beta_all, csm], [csm])
                    tt(csm.ap[:, 21:24], csm.ap[:, 18:21], eG, ALU.mult, [csm], [csm])
                    tt(csm.ap[:, 24:27], rk, eGr, ALU.mult, [csm], [csm])
                    ts(csm.ap[:, 27:30], csmq.ap[:, 12:15], float(128 ** -0.5), ALU.mult, [csmq], [csm])
                    SC.append(dict(eG=eG, eGr=eGr, gc=gc, rk=rk, lnrk=lnrk, sA=csm.ap[:, 18:21], sXk=csm.ap[:, 21:24],
                                   skd=csm.ap[:, 24:27], so=csm.ap[:, 27:30]))
                for s_, (ci, h) in enumerate(streams):
                    Hh, sc_, csm = HT[s_], SC[ci], csmL[ci]
                    hcol = slice(h, h + 1)
                    ts(Hh["kd"].ap, Hh["Xk"].ap, sc_["skd"][:, hcol], ALU.mult, [Hh["Xk"], csm], [Hh["kd"]])
                    ts(Hh["Xk"].ap, Hh["Xk"].ap, sc_["sXk"][:, hcol], ALU.mult, [Hh["Xk"], csm], [Hh["Xk"]])
                    ts(Hh["Xv"].ap, Hh["Xv"].ap, B3[ci][:, hcol], ALU.mult, [Hh["Xv"], beta_all], [Hh["Xv"]])
                    if own:
                        ts(Hh["qg"].ap, Hh["qg"].ap, sc_["eG"][:, hcol], ALU.mult, [Hh["qg"], csm], [Hh["qg"]])
                    ts(Hh["gSL"].ap, SLm, G3[ci][:, hcol], ALU.mult, [cpk, g_all], [Hh["gSL"]])
                    stt(Hh["gSLc"].ap, identf, sc_["lnrk"][:, hcol], Hh["gSL"].ap, ALU.mult, ALU.add, [cpk, csm, Hh["gSL"]], [Hh["gSLc"]])
                for s_, (ci, h) in enumerate(streams):
                    Hh, sc_, csm, bank = HT[s_], SC[ci], csmL[ci], 2 + s_
                    hcol = slice(h, h + 1)
                    kTr = qkvT.ap[:, 3 + h, CS[ci]]
                    qTr = qkvT.ap[:, h, CS[ci]]
                    mm(psum[bank][:, 0:128], psB[bank][0], kTr, kTr, [qkvT])
                    mm(psum[bank][:, 128:256], psB[bank][1], Umat, Hh["gSLc"].ap, [cpk, Hh["gSLc"]])
                    if own:
                        mm(psum[bank][:, 256:384], psB[bank][2], kTr, qTr, [qkvT])
                        mm(psum[bank][:, 384:512], psB[bank][3], Hh["gSL"].ap, Umat, [cpk, Hh["gSL"]])
                    act(Hh["eE"].ap, psum[bank][:, 128:256], AF.Exp, [psB[bank][1]], [Hh["eE"]])
                    tt(Hh["t1"].ap, psum[bank][:, 0:128], Hh["eE"].ap, ALU.mult, [psB[bank][0], Hh["eE"]], [Hh["t1"]])
                    stt(Hh["RT0"].ap, Hh["t1"].ap, sc_["sA"][:, hcol], NSLm, ALU.mult, ALU.mult, [Hh["t1"], csm, cpk], [Hh["RT0"]])
                    if own:
                        act(Hh["eET"].ap, psum[bank][:, 384:512], AF.Exp, [psB[bank][3]], [Hh["eET"]])
                        tt(Hh["t1"].ap, psum[bank][:, 256:384], Hh["eET"].ap, ALU.mult, [psB[bank][2], Hh["eET"]], [Hh["t1"]])
                        stt(Hh["pT"].ap, Hh["t1"].ap, sc_["rk"][:, hcol], Umat, ALU.mult, ALU.mult, [Hh["t1"], csm, cpk], [Hh["pT"]])
                for s_, (ci, h) in enumerate(streams):
                    Hh, bank = HT[s_], 2 + s_
                    tr(psum[bank][:, 0:128], psB[bank][0], Hh["RT0"].ap, identf, [Hh["RT0"], cpk])
                    cp(Hh["R0"].ap, psum[bank][:, 0:128], [psB[bank][0]], [Hh["R0"]], eng="act")
                    tt(Hh["Y"].ap, psum[bank][:, 0:128], identf, ALU.add, [psB[bank][0], cpk], [Hh["Y"]])
                for lvl in range(1, 7):
                    pr, nx = (lvl - 1) % 2, lvl % 2
                    for s_, (ci, h) in enumerate(streams):
                        Hh, bank = HT[s_], 2 + s_
                        Rp, RTp = Hh[f"R{pr}"], Hh[f"RT{pr}"]
                        Rn, RTn = Hh[f"R{nx}"], Hh[f"RT{nx}"]
                        mm(psum[bank][:, 128:256], psB[bank][1], Rp.ap, RTp.ap, [Rp, RTp])
                        if lvl < 6:
                            mm(psum[bank][:, 0:128], psB[bank][0], RTp.ap, Rp.ap, [Rp, RTp])
                        cp(RTn.ap, psum[bank][:, 128:256], [psB[bank][1]], [RTn], eng="act")
                        if lvl < 6:
                            cp(Rn.ap, psum[bank][:, 0:128], [psB[bank][0]], [Rn], eng="dve")
                    for s_, (ci, h) in enumerate(streams):
                        Hh, bank = HT[s_], 2 + s_
                        RTn = Hh[f"RT{nx}"]
                        mm(psum[bank][:, 256:384], psB[bank][2], RTn.ap, Hh["Y"].ap, [RTn, Hh["Y"]])
                        tt(Hh["Y"].ap, psum[bank][:, 256:384], Hh["Y"].ap, ALU.add, [psB[bank][2], Hh["Y"]], [Hh["Y"]])
                for s_, (ci, h) in enumerate(streams):
                    Hh, bank = HT[s_], 2 + s_
                    mm(psum[bank][:, 0:128], psB[bank][0], Hh["Xk"].ap, Hh["Y"].ap, [Hh["Xk"], Hh["Y"]])
                    if own:
                        tr(psum[bank][:, 128:256], psB[bank][1], Hh["qg"].ap, identf, [Hh["qg"], cpk])
                    act(Hh["wTn"].ap, psum[bank][:, 0:128], AF.Copy, [psB[bank][0]], [Hh["wTn"]], scale=-1.0)
                    if own:
                        cp(Hh["qgT"].ap, psum[bank][:, 128:256], [psB[bank][1]], [Hh["qgT"]], eng="dve")
                for s_, (ci, h) in enumerate(streams):
                    Hh, sc_, csm, csmq, bank = HT[s_], SC[ci], csmL[ci], csmqL[ci], 2 + s_
                    S = Sst[hh * 3 + h]
                    mm(psum[bank][:, 0:128], psB[bank][0], Hh["Y"].ap, Hh["Xv"].ap, [Hh["Y"], Hh["Xv"]], start=True, stop=False)
                    mm(psum[bank][:, 0:128], psB[bank][0], Hh["wTn"].ap, S.ap, [Hh["wTn"], S], start=False, stop=True)
                    cp(Hh["delta"].ap, psum[bank][:, 0:128], [psB[bank][0]], [Hh["delta"]], eng="dve")
                    if own:
                        mm(psum[bank][:, 128:256], psB[bank][1], Hh["qgT"].ap, S.ap, [Hh["qgT"], S], start=True, stop=False)
                        mm(psum[bank][:, 128:256], psB[bank][1], Hh["pT"].ap, Hh["delta"].ap, [Hh["pT"], Hh["delta"]], start=False, stop=True)
                    mm(psum[bank][:, 256:384], psB[bank][2], Hh["kd"].ap, Hh["delta"].ap, [Hh["kd"], Hh["delta"]])
                    stt(S.ap, S.ap, sc_["gc"][:, h:h + 1], psum[bank][:, 256:384], ALU.mult, ALU.add, [S, csm, psB[bank][2]], [S])
                    if own:
                        ts(Hh["on"].ap, psum[bank][:, 128:256], sc_["so"][:, h:h + 1], ALU.mult, [psB[bank][1], csm], [Hh["on"]])
                        act(Hh["t1"].ap, Hh["on"].ap, AF.Square, [Hh["on"]], [Hh["t1"], csmq], accum=csmq.ap[:, 16 + h:17 + h])
                if own:
                    for ci in range(2):
                        csm, csmq = csmL[ci], csmqL[ci]
                        act(csm.ap[:, 32:35], csmq.ap[:, 16:19], AF.Ln, [csmq], [csm], scale=1.0 / 128, bias=EPS)
                        act(csm.ap[:, 32:35], csm.ap[:, 32:35], AF.Exp, [csm], [csm], scale=-0.5)
                    for s_, (ci, h) in enumerate(streams):
                        Hh, csm = HT[s_], csmL[ci]
                        ts(Hh["on"].ap, Hh["on"].ap, csm.ap[:, 32 + h:33 + h], ALU.mult, [Hh["on"], csm], [Hh["on"]])
                    for s_, (ci, h) in enumerate(streams):
                        Hh, bank = HT[s_], 2 + s_
                        tr(psum[bank][:, 128:256], psB[bank][1], Hh["on"].ap, identf, [Hh["on"], cpk])
                        stt(oT.ap[:, hh * 3 + h, CS[ci]], psum[bank][:, 128:256], cpk.ap[:, C_DNW:C_DNW + 1], zs.ap[:, h, CS[ci]],
                            ALU.mult, ALU.mult, [psB[bank][1], cpk, zs], [oT])

        SKIP = os.environ.get("KSKIP", "").split(",")
        try:
            for hh in range(2):
                if "gdn" in SKIP:
                    break
                gdn_proj(hh, 0)
                chk("gproj", [qkvT.dump()])
                gdn_chunks(hh, 0)
            chk("gpre", [Sst[0].dump(), Sst[5].dump()])
            P.barrier()
            for hh in range(2):
                if "gdn" in SKIP:
                    break
                gdn_proj(hh, 1)
                gdn_chunks(hh, 1)
            chk("gdn", [oT.dump()])
            P.barrier()
            SA = Alloc(W0, AW)
            swab = SA.get(3 * 768, "swab", shape=(3, 768))
            dma("sp", swab.ap, swab_d.rearrange("t k n -> k t n"), swab, wr=[swab])
            wg_ = SA.get(5120, "wg", BF16, shape=(16, 640))
            wgL = [wg_, wg_]
            qnL = [SA.get(192, f"qn{i}", BF16) for i in range(2)]
            knL = [SA.get(64, f"kn{i}", BF16) for i in range(2)]
            qTsL = [SA.get(192, f"qTs{i}", BF16) for i in range(2)]
            KT = [SA.get(64, f"KT{i}", BF16) for i in range(2)]
            VT = [SA.get(64, f"VT{i}", BF16) for i in range(2)]
            scL = [[SA.get(384, f"sc{i}{k}") for k in range(2)] for i in range(2)]
            ETL = [[SA.get(192, f"ET{i}{k}", BF16) for k in range(2)] for i in range(2)]
            denL = [SA.get(384, f"den{i}") for i in range(2)]
            ssmL = [SA.get(16, f"ssm{i}") for i in range(2)]
            sjL = [SA.get(128, f"sjunk{i}") for i in range(2)]
            esink = SA.get(8, "esink")
            act(esink.ap[:, 0:6], cpk.ap[:, C_SINK:C_SINK + 6], AF.Exp, [cpk], [esink])
            KT3 = [KT[0], KT[1], SA.get(64, "KT2", BF16)]
            VT3 = [VT[0], VT[1], SA.get(64, "VT2", BF16)]

            def swa_A(g, ti):
                wg = wgL[g]
                src, sl = (hHalo, slice(0, 128)) if ti == 0 else (hT, slice((ti - 1) * 128, ti * 128))
                p_ = ti % 2
                qn, kn, qTs, ssm, sjunk = qnL[p_], knL[p_], qTsL[p_], ssmL[p_], sjL[p_]
                Kc, Vc = KT3[ti % 3], VT3[ti % 3]
                bq, bv = p_, 2 + p_
                for kt in range(16):
                    mm(psum[bq][:, :], psB[bq], src.ap[:, kt, sl], wg.ap[:, kt, 0:512], [src, wg], start=(kt == 0), stop=(kt == 15))
                for kt in range(16):
                    mm(psum[bv][:, 0:128], psB[bv], src.ap[:, kt, sl], wg.ap[:, kt, 512:640], [src, wg], start=(kt == 0), stop=(kt == 15))
                for j in range(4):
                    act(sjunk.ap, psum[bq][:, j * 128:(j + 1) * 128], AF.Square, [psB[bq]], [sjunk, ssm], accum=ssm.ap[:, j:j + 1])
                rstd_from_ssq(ssm.ap[:, 8:12], ssm.ap[:, 4:8], ssm.ap[:, 0:4], 128, [ssm], [ssm])
                if ti > 0:
                    for j in range(3):
                        ts(qn.ap[:, j * 128:(j + 1) * 128], psum[bq][:, j * 128:(j + 1) * 128], ssm.ap[:, 8 + j:9 + j], ALU.mult, [psB[bq], ssm], [qn])
                ts(kn.ap, psum[bq][:, 384:512], ssm.ap[:, 11:12], ALU.mult, [psB[bq], ssm], [kn])
                cp(Vc.ap, psum[bv][:, 0:128], [psB[bv]], [Vc], eng="act")

            def swa_A2(g, ti):
                p_ = ti % 2
                qn, kn, qTs = qnL[p_], knL[p_], qTsL[p_]
                Kc = KT3[ti % 3]
                bv = 2 + p_
                pb = psum[bv][:].bitcast(BF16)
                if ti > 0:
                    for j in range(3):
                        tr(pb[:, j * 128:(j + 1) * 128], psB[bv], qn.ap[:, j * 128:(j + 1) * 128], identb, [qn, cb16])
                tr(pb[:, 384:512], psB[bv], kn.ap, identb, [kn, cb16])
                if ti > 0:
                    ts(qTs.ap, pb[:, 0:384], cpk.ap[:, C_WQ:C_WQ + 1], ALU.mult, [psB[bv], cpk], [qTs])
                ts(Kc.ap, pb[:, 384:512], cpk.ap[:, C_WK:C_WK + 1], ALU.mult, [psB[bv], cpk], [Kc])

            def swa_B(g, ti):
                sl = slice((ti - 1) * 128, ti * 128)
                p_ = ti % 2
                qTs, sc2, ET2, den = qTsL[p_], scL[p_], ETL[p_], denL[p_]
                Kc, Vc, Kp, Vp = KT3[ti % 3], VT3[ti % 3], KT3[(ti - 1) % 3], VT3[(ti - 1) % 3]
                for kb, (Kt, bank) in enumerate(((Kp, 4), (Kc, 5))):
                    mm(psum[bank][:, 0:384], psB[bank], Kt.ap, qTs.ap, [Kt, qTs])
                    bsel = 0 if kb == 1 else (2 if ti == 1 else 1)
                    stt(sc2[kb].ap, psum[bank][:, 0:384], float(128 ** -0.5), swab.ap[:, bsel, g * 384:(g + 1) * 384], ALU.mult, ALU.add,
                        [psB[bank], swab], [sc2[kb]])
                    act(ET2[kb].ap, sc2[kb].ap, AF.Exp, [sc2[kb]], [ET2[kb]])
                mm(psum[6][:, 0:384], psB[6], Vp.ap, ET2[0].ap, [Vp, ET2[0]], start=True, stop=False)
                mm(psum[6][:, 0:384], psB[6], Vc.ap, ET2[1].ap, [Vc, ET2[1]], start=False, stop=True)
                mm(psum[7][:, 0:384], psB[7], onesb, ET2[0].ap, [cb16, ET2[0]], start=True, stop=False)
                mm(psum[7][:, 0:384], psB[7], onesb, ET2[1].ap, [cb16, ET2[1]], start=False, stop=True)
                for j in range(3):
                    ts(den.ap[:, j * 128:(j + 1) * 128], psum[7][:, j * 128:(j + 1) * 128], esink.ap[:, 3 * g + j:3 * g + j + 1], ALU.add,
                       [psB[7], esink], [den])
                recip(den.ap, den.ap, [den], [den])
                tt(oT.ap[:, 6 + 3 * g:9 + 3 * g, sl], psum[6][:, 0:384].rearrange("p (a b) -> p a b", a=3),
                   den.ap.rearrange("p (a b) -> p a b", a=3), ALU.mult, [psB[6], den], [oT])

            for g in range(2):
                wg = wgL[g]
                for (c0, n, o0) in ((OFF_SQ + g * 384, 384, 0), (OFF_SK + g * 128, 128, 384), (OFF_SV + g * 128, 128, 512)):
                    dma("pool", wg.ap[:, :, o0:o0 + n], win_d[:, c0:c0 + n].rearrange("(k p) n -> p k n", p=128), wg, wr=[wg])
                swa_A(g, 0)
                swa_A(g, 1)
                swa_A2(g, 0)
                for ti in range(1, 9):
                    if ti + 1 < 9:
                        swa_A(g, ti + 1)
                    swa_A2(g, ti)
                    swa_B(g, ti)
            chk("swa", [oT.dump()])
            MA = Alloc(SA.p, AW)
            hmT = MA.get(2048, "hmT", BF16, shape=(16, 256))
            wm = MA.get(4096, "wm", BF16, shape=(16, 512))
            KM = MA.get(512, "KM", BF16, shape=(4, 256))
            VM = [MA.get(256, f"VM{i}", BF16) for i in range(2)]
            kmnL = [MA.get(256, f"kmn{i}", BF16) for i in range(2)]
            qmTL = [MA.get(256, f"qmT{i}", BF16) for i in range(2)]
            EML = [MA.get(512, f"EM{i}", BF16) for i in range(2)]
            dnmL = [MA.get(512, f"dnm{i}") for i in range(2)]
            msmL = [MA.get(16, f"msm{i}") for i in range(2)]
            mjL = [MA.get(128, f"mjunk{i}") for i in range(2)]
            wload(wm, wmkv_d, 0, 512)
            norm_T(mem_d, [(0, 0), (1, 1)], 1, hmT, Alloc(MA.p, AW))

            def headnorm_T(p_, dst3d, wcol):
                mjunk, msm, kmn = mjL[p_], msmL[p_], kmnL[p_]
                for j in range(4):
                    act(mjunk.ap, psum[p_][:, j * 128:(j + 1) * 128], AF.Square, [psB[p_]], [mjunk, msm], accum=msm.ap[:, j:j + 1])
                rstd_from_ssq(msm.ap[:, 8:12], msm.ap[:, 4:8], msm.ap[:, 0:4], 128, [msm], [msm])
                for j in range(4):
                    ts(kmn.ap[:, j * 128:(j + 1) * 128], psum[p_][:, j * 128:(j + 1) * 128], msm.ap[:, 8 + j:9 + j], ALU.mult,
                       [psB[p_], msm], [kmn])
                if dst3d is None:
                    return
                headnorm_T2(p_, dst3d, wcol)

            def headnorm_T2(p_, dst3d, wcol):
                kmn = kmnL[p_]
                pb = psum[2 + p_][:].bitcast(BF16)
                for j in range(4):
                    tr(pb[:, j * 128:(j + 1) * 128], psB[2 + p_], kmn.ap[:, j * 128:(j + 1) * 128], identb, [kmn, cb16])
                ts(dst3d[0], pb[:, 0:512].rearrange("p (a b) -> p a b", a=4), cpk.ap[:, wcol:wcol + 1], ALU.mult, [psB[2 + p_], cpk], [dst3d[1]])

            for mt in range(2):
                for kt in range(16):
                    mm(psum[mt][:, :], psB[mt], hmT.ap[:, kt, mt * 128:(mt + 1) * 128], wm.ap[:, kt, :], [hmT, wm], start=(kt == 0), stop=(kt == 15))
                headnorm_T(mt, (KM.ap[:, :, mt * 128:(mt + 1) * 128], KM), C_XK)
            wload(wm, wmkv_d, 512, 512)
            for mt in range(2):
                for kt in range(16):
                    mm(psum[mt][:, :], psB[mt], hmT.ap[:, kt, mt * 128:(mt + 1) * 128], wm.ap[:, kt, :], [hmT, wm], start=(kt == 0), stop=(kt == 15))
                cp(VM[mt].ap, psum[mt][:, :], [psB[mt]], [VM[mt]])
            wmq = wm
            wload(wmq, win_d, OFF_MQ, 512)
            def mem_A(t):
                p_ = t % 2
                tsl = slice(t * 128, (t + 1) * 128)
                for kt in range(16):
                    mm(psum[p_][:, :], psB[p_], hT.ap[:, kt, tsl], wmq.ap[:, kt, :], [hT, wmq], start=(kt == 0), stop=(kt == 15))
                headnorm_T(p_, None, C_XQ)

            def mem_A2(t):
                p_ = t % 2
                headnorm_T2(p_, (qmTL[p_].ap.rearrange("p (a b) -> p a b", a=4), qmTL[p_]), C_XQ)

            def mem_B(t):
                p_ = t % 2
                qmT, EM, dnm = qmTL[p_], EML[p_], dnmL[p_]
                tsl = slice(t * 128, (t + 1) * 128)
                for mt in range(2):
                    for h in range(4):
                        mm(psum[4 + mt][:, h * 128:(h + 1) * 128], psB[4 + mt], KM.ap[:, h, mt * 128:(mt + 1) * 128], qmT.ap[:, h * 128:(h + 1) * 128], [KM, qmT])
                    act(EM.ap[:, mt * 512:(mt + 1) * 512], psum[4 + mt][:, :], AF.Exp, [psB[4 + mt]], [EM], scale=float(128 ** -0.5))
                for h in range(4):
                    for mt in range(2):
                        mm(psum[6][:, h * 128:(h + 1) * 128], psB[6], VM[mt].ap[:, h * 128:(h + 1) * 128], EM.ap[:, mt * 512 + h * 128:mt * 512 + (h + 1) * 128],
                           [VM[mt], EM], start=(mt == 0), stop=(mt == 1))
                for mt in range(2):
                    mm(psum[7][:, :], psB[7], onesb, EM.ap[:, mt * 512:(mt + 1) * 512], [cb16, EM], start=(mt == 0), stop=(mt == 1))
                recip(dnm.ap, psum[7][:, :], [psB[7]], [dnm])
                tt(oT.ap[:, 12:16, tsl], psum[6][:, :].rearrange("p (a b) -> p a b", a=4), dnm.ap.rearrange("p (a b) -> p a b", a=4), ALU.mult,
                   [psB[6], dnm], [oT])

            mem_A(0)
            for t in range(8):
                if t + 1 < 8:
                    mem_A(t + 1)
                mem_A2(t)
                mem_B(t)
            chk("mem", [oT.dump()])
            P.barrier()
            GA2 = Alloc(W0, AW)
            mT = GA2.get(8192, "mT", BF16, shape=(16, TOK))
            mslab = [GA2.get(4096, f"msl{i}", BF16, shape=(16, 512)) for i in range(3)]
            sg = [GA2.get(512, f"sg{i}") for i in range(6)]
            accs = [GA2.get(512, f"acc{i}") for i in range(2)]
            wo = [GA2.get(4096, "wo0", BF16, shape=(16, 512)), mslab[1]]
            brk = (range(0, 6), range(6, 12), range(12, 16))

            def load_mslab(mt):
                sl = mslab[mt % 3]
                dma("pool", sl.ap[:, :, 0:128], pall_d[:, mt * 128:(mt + 1) * 128].rearrange("(k p) n -> p k n", p=128), sl, wr=[sl])
                for br in range(3):
                    c0 = OFF_G + br * 2048 + mt * 128
                    dma("pool", sl.ap[:, :, 128 * (br + 1):128 * (br + 2)], win_d[:, c0:c0 + 128].rearrange("(k p) n -> p k n", p=128), sl, wr=[sl])

            load_mslab(0)
            load_mslab(1)
            pcount = 0
            for mt in range(16):
                if mt + 2 < 16:
                    load_mslab(mt + 2)
                elif mt + 2 < 18:
                    wload(wo[mt + 2 - 16], wout_d, (mt + 2 - 16) * 512, 512)
                sl = mslab[mt % 3]
                for c2 in range(2):
                    cols = slice(c2 * 512, (c2 + 1) * 512)
                    acc = accs[c2]
                    for br in range(3):
                        gb = c2 * 3 + br
                        pbk = 6 + (pcount % 2)
                        pcount += 1
                        sgb = sg[c2 * 3 + br]
                        for kt in range(16):
                            mm(psum[gb][:, :], psB[gb], sl.ap[:, kt, 128 * (br + 1):128 * (br + 2)], hT.ap[:, kt, cols], [sl, hT], start=(kt == 0), stop=(kt == 15))
                        act(sgb.ap, psum[gb][:, :], AF.Sigmoid, [psB[gb]], [sgb])
                        kts = list(brk[br])
                        for kt in kts:
                            mm(psum[pbk][:, :], psB[pbk], sl.ap[:, kt, 0:128], oT.ap[:, kt, cols], [sl, oT], start=(kt == kts[0]), stop=(kt == kts[-1]))
                        if br == 0:
                            tt(acc.ap, psum[pbk][:, :], sgb.ap, ALU.mult, [psB[pbk], sgb], [acc])
                        else:
                            tt(sgb.ap, psum[pbk][:, :], sgb.ap, ALU.mult, [psB[pbk], sgb], [sgb])
                            if br == 1:
                                tt(acc.ap, acc.ap, sgb.ap, ALU.add, [acc, sgb], [acc])
                            else:
                                tt(mT.ap[:, mt, cols], acc.ap, sgb.ap, ALU.add, [acc, sgb], [mT])
            chk("merge", [mT.dump()])
            x1 = T(arena[:, 4096:4096 + 16384].rearrange("p (t d) -> p t d", t=8), "x1", 4096, 16384)
            x1t = [Buf(f"x1t{t}") for t in range(8)]
            for t in range(8):
                dma("sp", x1.ap[:, t, :], x_d[t * 128:(t + 1) * 128, :], x1t[t], wr=[x1t[t], hT, oT, hP])
            pc = 0
            for cc in range(4):
                w = wo[cc % 2]
                if cc >= 2:
                    wload(w, wout_d, cc * 512, 512)
                for t in range(8):
                    bank = pc % 6
                    pc += 1
                    for kt in range(16):
                        mm(psum[bank][:, :], psB[bank], mT.ap[:, kt, t * 128:(t + 1) * 128], w.ap[:, kt, :], [mT, w], start=(kt == 0), stop=(kt == 15))
                    tt(x1.ap[:, t, cc * 512:(cc + 1) * 512], psum[bank][:, :], x1.ap[:, t, cc * 512:(cc + 1) * 512], ALU.add, [psB[bank], x1t[t]], [x1t[t]])
            chk("x1", [(x1t[t], 4096 + t * 2048, 2048) for t in range(8)])
            P.barrier()
            ML = Alloc(W0, AW)
            h2T = ML.get(8192, "h2T", BF16, shape=(16, TOK))
            wu = [ML.get(4096, f"wu{i}", BF16, shape=(16, 512)) for i in range(2)]
            wd = [ML.get(4096, f"wd{i}", BF16, shape=(4, 2048)) for i in range(2)]
            aT = [ML.get(2048, f"aT{i}", BF16, shape=(4, TOK)) for i in range(2)]
            rl = [ML.get(512, f"rl{i}") for i in range(2)]

            def load_wu(fb):
                wload(wu[fb % 2], wup_d, fb * 512, 512)

            def load_wd(fb):
                dma("pool", wd[fb % 2].ap, wdn_d[fb * 512:(fb + 1) * 512, :].rearrange("(k p) n -> p k n", p=128), wd[fb % 2], wr=[wd[fb % 2]])

            load_wu(0)
            load_wd(0)
            norm_T(None, [(t, t) for t in range(8)], 2, h2T, Alloc(ML.p, AW), src_sb=lambda t: (x1t[t], x1.ap[:, t, :]))
            upc = [0]

            def mlp_up(fb):
                w = wu[fb % 2]
                a = aT[fb % 2]
                for f4 in range(4):
                    for tc in range(2):
                        bank = upc[0] % 2
                        r = rl[upc[0] % 2]
                        upc[0] += 1
                        for kt in range(16):
                            mm(psum[bank][:, :], psB[bank], w.ap[:, kt, f4 * 128:(f4 + 1) * 128], h2T.ap[:, kt, tc * 512:(tc + 1) * 512], [w, h2T],
                               start=(kt == 0), stop=(kt == 15))
                        act(r.ap, psum[bank][:, :], AF.Relu, [psB[bank]], [r])
                        tt(a.ap[:, f4, tc * 512:(tc + 1) * 512], r.ap, r.ap, ALU.mult, [r], [a])

            dnc = [0]

            def mlp_down(fb):
                w = wd[fb % 2]
                a = aT[fb % 2]
                for t in range(8):
                    for cc in range(4):
                        bank = 2 + dnc[0] % 6
                        dnc[0] += 1
                        for k in range(4):
                            mm(psum[bank][:, :], psB[bank], a.ap[:, k, t * 128:(t + 1) * 128], w.ap[:, k, cc * 512:(cc + 1) * 512], [a, w],
                               start=(k == 0), stop=(k == 3))
                        tt(x1.ap[:, t, cc * 512:(cc + 1) * 512], psum[bank][:, :], x1.ap[:, t, cc * 512:(cc + 1) * 512], ALU.add, [psB[bank], x1t[t]], [x1t[t]])

            for fb in range(16):
                if fb + 1 < 16:
                    load_wu(fb + 1)
                mlp_up(fb)
                if fb > 0:
                    mlp_down(fb - 1)
                if fb + 1 < 16:
                    load_wd(fb + 1)
            mlp_down(15)
            for t in range(8):
                dma("sp", y_d[t * 128:(t + 1) * 128, :], x1.ap[:, t, :], x1t[t], rd=[x1t[t]])
        except StopBuild:
            pass
        return finish(nc, P, st, dbg_d, dumps, dma, locals())


def finish(nc, P, st, dbg_d, dumps, dma, L):
    if dbg_d is not None:
        off = 0
        arena = L["arena"]
        for (t, a0, words) in dumps:
            dma("sp", dbg_d[:, off:off + words], arena[:, a0:a0 + words], t, rd=[t])
            off += words
    P.final_wait()
    P.emit(nc)
    return nc


def _t5_bucket(dist):
    import math
    n = np.maximum(dist, 0)
    max_exact = 16
    nf = np.maximum(n, 1).astype(np.float32)
    large = max_exact + (np.log(nf / max_exact) / math.log(128 / max_exact) * (32 - max_exact)).astype(np.int32)
    large = np.minimum(large, 31)
    return np.where(n < max_exact, n, large)


def make_in_maps(inp):
    f32 = np.float32
    x = np.asarray(inp["x"], f32)
    mem = np.asarray(inp["mem"], f32)
    w_in = np.ascontiguousarray(np.asarray(inp["w_in"], f32)[0])
    w_mem_kv = np.ascontiguousarray(np.asarray(inp["w_mem_kv"], f32)[0])
    p_all = np.ascontiguousarray(np.concatenate([np.asarray(inp["p_dn"], f32)[0], np.asarray(inp["p_swa"], f32)[0],
                                                 np.asarray(inp["p_mem"], f32)[0]], axis=0))
    w_out = np.ascontiguousarray(np.asarray(inp["w_out"], f32)[0])
    w_up = np.ascontiguousarray(np.asarray(inp["w_mlp_up"], f32)[0])
    w_down = np.ascontiguousarray(np.asarray(inp["w_mlp_down"], f32)[0])
    nw3 = np.ascontiguousarray(np.stack([np.asarray(inp["attn_norm_w"], f32)[0], np.asarray(inp["mem_norm_w"], f32)[0],
                                         np.asarray(inp["mlp_norm_w"], f32)[0]], axis=0))
    cp = np.zeros((128, NCP), f32)
    idx = np.arange(128)
    cp[:, C_ID:C_ID + 128] = np.eye(128, dtype=f32)
    cp[:, C_U:C_U + 128] = (idx[:, None] <= idx[None, :]).astype(f32)
    cp[:, C_SL:C_SL + 128] = (idx[:, None] > idx[None, :]).astype(f32)
    cp[:, C_NSL:C_NSL + 128] = -(idx[:, None] > idx[None, :]).astype(f32)
    cp[:, C_ONE:C_ONE + 128] = 1.0
    cw = np.asarray(inp["dn_conv_w"], f32)[0]
    cp[:, C_CONV:C_CONV + 72] = cw.reshape(4, 18, 128).transpose(2, 1, 0).reshape(128, 72)
    cp[:, C_DNW] = np.asarray(inp["dn_out_norm_w"], f32)[0]
    cp[:, C_WQ] = np.asarray(inp["swa_q_norm_w"], f32)[0]
    cp[:, C_WK] = np.asarray(inp["swa_k_norm_w"], f32)[0]
    cp[:, C_XQ] = np.asarray(inp["xq_norm_w"], f32)[0]
    cp[:, C_XK] = np.asarray(inp["xk_norm_w"], f32)[0]
    cp[:, C_ALOG:C_ALOG + 6] = np.asarray(inp["dn_A_log"], f32)[0][None, :]
    cp[:, C_DTB:C_DTB + 6] = np.asarray(inp["dn_dt_bias"], f32)[0][None, :]
    cp[:, C_SINK:C_SINK + 6] = np.asarray(inp["swa_sinks"], f32)[0][None, :]
    cb16 = np.concatenate([np.eye(128, dtype=f32), np.ones((128, 128), f32)], axis=1).astype(ml_dtypes.bfloat16)
    rb = np.asarray(inp["rel_bias"], f32)
    qi = np.arange(128)[:, None]
    kj = np.arange(256)[None, :]
    dist = qi - kj + 128
    valid = (dist >= 0) & (dist < 128)
    bias = rb[_t5_bucket(dist)]
    bias = np.where(valid[:, :, None], bias, f32(NEG)).astype(f32)
    biasT = bias.transpose(1, 2, 0)
    b_prev = np.ascontiguousarray(biasT[0:128].reshape(128, 768))
    b_cur = np.ascontiguousarray(biasT[128:256].reshape(128, 768))
    b_none = np.full((128, 768), NEG, f32)
    maps = []
    for c in range(8):
        b, hf = c // 2, c % 2
        xo = np.ascontiguousarray(x[b, hf * TOK:(hf + 1) * TOK])
        xp = np.ascontiguousarray(x[b, 0:TOK]) if hf == 1 else np.zeros((TOK, D), f32)
        swab = np.stack([b_cur, b_prev, b_prev if hf == 1 else b_none], axis=0)
        maps.append({"x": xo, "xp": xp, "mem": np.ascontiguousarray(mem[b]), "w_in": w_in, "w_mem_kv": w_mem_kv,
                     "p_all": p_all, "w_out": w_out, "w_up": w_up, "w_down": w_down, "nw3": nw3, "cpack": cp,
                     "cb16": cb16, "swab": np.ascontiguousarray(swab)})
    return maps


_NC_CACHE = {}


def kernel(**inputs):
    maps = make_in_maps(inputs)
    if "nc" not in _NC_CACHE:
        _NC_CACHE["nc"] = build()
    res = run_bass_kernel_spmd(_NC_CACHE["nc"], maps, core_ids=list(range(8)))
    out = np.zeros((4, 2048, D), np.float32)
    for c in range(8):
        b, hf = c // 2, c % 2
        out[b, hf * TOK:(hf + 1) * TOK] = res.results[c]["y"]
    return out
```

```python
import os
import numpy as np
import ml_dtypes
from contextlib import ExitStack
import concourse.bass as bass
import concourse.mybir as mybir
from concourse.bass_utils import run_bass_kernel_spmd

F32 = mybir.dt.float32
BF16 = mybir.dt.bfloat16
AF = mybir.ActivationFunctionType
ALU = mybir.AluOpType

EPS = 1e-6
D = 2048
TOK = 1024
NT = 8
OFF_QKV, OFF_Z, OFF_B, OFF_A, OFF_SQ, OFF_SK, OFF_SV, OFF_MQ, OFF_G = 0, 2304, 3072, 3078, 3084, 3852, 4108, 4364, 4876
IN_W = 11020
NEG = -30000.0

C_ID, C_U, C_SL, C_NSL, C_ONE, C_CONV, C_DNW, C_WQ, C_WK, C_XQ, C_XK, C_ALOG, C_DTB, C_SINK, NCP = \
    0, 128, 256, 384, 512, 640, 712, 713, 714, 715, 716, 717, 723, 729, 736


class Buf:
    __slots__ = ("name", "lw", "rd")

    def __init__(self, name):
        self.name = name
        self.lw = None
        self.rd = {}


class Prog:
    ENG = ("pe", "act", "dve", "pool", "sp")

    def __init__(self):
        self.ops = {e: [] for e in self.ENG}
        self.cnt = {e: 0 for e in self.ENG}
        self.waited = {e: {} for e in self.ENG}
        self.dcnt = {}
        self.swq = []

    def _deps(self, reads, writes):
        toks = []
        for b in reads:
            if b.lw is not None:
                toks.append(b.lw)
        for b in writes:
            if b.lw is not None:
                toks.append(b.lw)
            toks.extend(b.rd.items())
        return toks

    def _filter(self, eng, toks):
        w = self.waited[eng]
        out = {}
        for (k, v) in toks:
            if eng == "pe" and k == "Epe":
                continue
            if w.get(k, 0) >= v:
                continue
            if out.get(k, 0) < v:
                out[k] = v
        for k, v in out.items():
            w[k] = v
        return list(out.items())

    def _record(self, tok, reads, writes):
        for b in writes:
            b.lw = tok
            b.rd = {}
        for b in reads:
            if b in writes:
                continue
            if b.rd.get(tok[0], 0) < tok[1]:
                b.rd[tok[0]] = tok[1]

    def op(self, eng, fn, reads=(), writes=()):
        extra = [b for b in reads if b.name.startswith("ps") and b not in writes]
        if extra:
            writes = list(writes) + extra
        waits = self._filter(eng, self._deps(reads, writes))
        self.cnt[eng] += 1
        tok = ("E" + eng, self.cnt[eng])
        self.ops[eng].append((waits, fn, ("E" + eng, 1)))
        self._record(tok, reads, writes)
        return tok

    def dma(self, eng, fn, sembuf, reads=(), writes=(), ndesc=0):
        toks = self._deps(reads, writes)
        if eng == "pool" and ndesc:
            while self.swq and sum(n for _, n in self.swq) + ndesc > 640:
                toks.append(self.swq.pop(0)[0])
        waits = self._filter(eng, toks)
        key = "D" + sembuf.name
        self.dcnt[key] = self.dcnt.get(key, 0) + 16
        tok = (key, self.dcnt[key])
        self.ops[eng].append((waits, fn, (key, 16)))
        self._record(tok, reads, writes)
        if eng == "pool" and ndesc:
            self.swq.append((tok, ndesc))
        return tok

    def barrier(self):
        toks = [("E" + e, self.cnt[e]) for e in self.ENG if self.cnt[e] > 0] + list(self.dcnt.items())
        for e in self.ENG:
            waits = self._filter(e, toks)
            if waits:
                self.ops[e].append((waits, None, None))

    def final_wait(self, eng="sp"):
        waits = self._filter(eng, list(self.dcnt.items()))
        self.ops[eng].append((waits, None, None))

    def emit(self, nc):
        keys = ["E" + e for e in self.ENG] + sorted(self.dcnt.keys())
        with ExitStack() as st:
            sems = {}
            for k in keys:
                sems[k] = st.enter_context(nc.semaphore("s_" + k))
            block = st.enter_context(nc.Block())
            binders = {"pe": block.tensor, "act": block.scalar, "dve": block.vector,
                       "pool": block.gpsimd, "sp": block.sync}
            for eng in self.ENG:
                ops = self.ops[eng]

                def body(e, ops=ops):
                    for waits, fn, inc in ops:
                        for k, v in waits:
                            e.wait_ge(sems[k], v)
                        if fn is None:
                            continue
                        ins = fn(e)
                        ins.then_inc(sems[inc[0]], inc[1])
                binders[eng](body)


class StopBuild(Exception):
    pass


class T:
    __slots__ = ("ap", "b", "off", "words")

    def __init__(self, ap, name, off=None, words=None):
        self.ap = ap
        self.b = Buf(name)
        self.off = off
        self.words = words

    def dump(self):
        return (self, self.off, self.words)


def build(stop_after=None, dbg=None):
    nc = bass.Bass("TRN2", target_bir_lowering=False)
    P = Prog()
    dr = {}

    def din(name, shape, dt=F32):
        dr[name] = nc.dram_tensor(name, shape, dt, kind="ExternalInput").ap()
        return dr[name]

    x_d = din("x", [TOK, D])
    xp_d = din("xp", [TOK, D])
    mem_d = din("mem", [256, D])
    win_d = din("w_in", [D, IN_W])
    wmkv_d = din("w_mem_kv", [D, 1024])
    pall_d = din("p_all", [D, D])
    wout_d = din("w_out", [D, D])
    wup_d = din("w_up", [D, 4 * D])
    wdn_d = din("w_down", [4 * D, D])
    nw3_d = din("nw3", [3, D])
    cpack_d = din("cpack", [128, NCP])
    cb16_d = din("cb16", [128, 256], BF16)
    swab_d = din("swab", [3, 128, 768])
    y_d = nc.dram_tensor("y", [TOK, D], F32, kind="ExternalOutput").ap()
    dbg_d = None
    if dbg is not None:
        dbg_d = nc.dram_tensor("dbg", [128, dbg], F32, kind="ExternalOutput").ap()

    with ExitStack() as st:
        AW = 52800
        arena = st.enter_context(nc.sbuf_tensor("arena", [128, AW], F32))
        psum = [st.enter_context(nc.psum_tensor(f"ps{i}", [128, 512], F32)) for i in range(8)]
        psB = [[Buf(f"ps{i}")] * 4 for i in range(8)]

        uid = [0]

        def view(off, words, name, dt=F32, shape=None):
            ap = arena[:, off:off + words]
            if dt == BF16:
                ap = ap.bitcast(BF16)
            if shape is not None:
                ap = ap[:, 0:int(np.prod(shape))]
                if len(shape) == 2:
                    ap = ap.rearrange("p (a b) -> p a b", a=shape[0])
                elif len(shape) == 3:
                    ap = ap.rearrange("p (a b c) -> p a b c", a=shape[0], b=shape[1])
            uid[0] += 1
            return T(ap, f"{name}_{uid[0]}", off, words)

        class Alloc:
            def __init__(self, lo, hi):
                self.lo, self.hi, self.p = lo, hi, lo

            def get(self, words, name, dt=F32, shape=None):
                o = self.p
                self.p += words
                assert self.p <= self.hi, (name, self.p, self.hi)
                return view(o, words, name, dt, shape)

        def bl(ts):
            out = []
            for t in ts:
                if isinstance(t, T):
                    out.append(t.b)
                elif isinstance(t, Buf):
                    out.append(t)
                elif isinstance(t, (list, tuple)):
                    out.extend(bl(t))
            return out

        def mm(out_ap, outb, lhsT, rhs, rd, start=True, stop=True):
            P.op("pe", lambda e: e.matmul(out_ap, lhsT=lhsT, rhs=rhs, start=start, stop=stop), reads=bl(rd), writes=bl([outb]))

        def tr(out_ap, outb, in_ap, ident_ap, rd):
            P.op("pe", lambda e: e.transpose(out=out_ap, in_=in_ap, identity=ident_ap), reads=bl(rd), writes=bl([outb]))

        def act(out_ap, in_ap, func, rd, wr, scale=None, bias=None, accum=None, eng="act"):
            kw = {}
            if scale is not None:
                kw["scale"] = scale
            if bias is not None:
                kw["bias"] = bias
            if accum is not None:
                kw["accum_out"] = accum
            P.op("act", lambda e: e.activation(out=out_ap, in_=in_ap, func=func, **kw), reads=bl(rd), writes=bl(wr))

        def tt(out_ap, a, b, op, rd, wr, eng="dve"):
            P.op(eng, lambda e: e.tensor_tensor(out=out_ap, in0=a, in1=b, op=op), reads=bl(rd), writes=bl(wr))

        def ts(out_ap, a, s1, op0, rd, wr, s2=None, op1=None, eng="dve"):
            if op1 is None:
                P.op(eng, lambda e: e.tensor_scalar(out=out_ap, in0=a, scalar1=s1, scalar2=None, op0=op0), reads=bl(rd), writes=bl(wr))
            else:
                P.op(eng, lambda e: e.tensor_scalar(out=out_ap, in0=a, scalar1=s1, scalar2=s2, op0=op0, op1=op1), reads=bl(rd), writes=bl(wr))

        def stt(out_ap, a, s, b, op0, op1, rd, wr):
            P.op("dve", lambda e: e.scalar_tensor_tensor(out=out_ap, in0=a, scalar=s, in1=b, op0=op0, op1=op1), reads=bl(rd), writes=bl(wr))

        def cp(out_ap, in_ap, rd, wr, eng="dve"):
            if eng == "act":
                P.op("act", lambda e: e.copy(out=out_ap, in_=in_ap), reads=bl(rd), writes=bl(wr))
            else:
                P.op(eng, lambda e: e.tensor_copy(out=out_ap, in_=in_ap), reads=bl(rd), writes=bl(wr))

        def recip(out_ap, in_ap, rd, wr):
            P.op("dve", lambda e: e.reciprocal(out=out_ap, in_=in_ap), reads=bl(rd), writes=bl(wr))

        def dma(q, out_ap, in_ap, semT, rd=(), wr=()):
            nd = 0
            if q == "pool":
                shp = list(out_ap.shape)
                nd = int(np.prod(shp[:-1])) // 16 + 2
            P.dma(q, lambda e: e.dma_start(out=out_ap, in_=in_ap), semT.b if isinstance(semT, T) else semT, reads=bl(rd), writes=bl(wr), ndesc=nd)

        dumps = []

        def chk(name, dl):
            if stop_after == name:
                dumps.extend(dl)
                raise StopBuild()

        def wload(slab, w_dram, c0, n, kt=16, r0=0):
            dma("pool", slab.ap, w_dram[r0:r0 + kt * 128, c0:c0 + n].rearrange("(k p) n -> p k n", p=128), slab, wr=[slab])

        def rstd_from_ssq(out_ap, tmp_ap, ssq_ap, n, rd, wr):
            act(tmp_ap, ssq_ap, AF.Ln, rd, wr, scale=1.0 / n, bias=EPS)
            act(out_ap, tmp_ap, AF.Exp, wr, wr, scale=-0.5)

        CONST = Alloc(0, 4096)
        cpk = CONST.get(NCP, "cpack")
        cb16 = CONST.get(128, "cb16", BF16)
        wbc = CONST.get(2048, "wbc")
        smalls = CONST.get(64, "smalls")
        identf = cpk.ap[:, C_ID:C_ID + 128]
        Umat = cpk.ap[:, C_U:C_U + 128]
        SLm = cpk.ap[:, C_SL:C_SL + 128]
        NSLm = cpk.ap[:, C_NSL:C_NSL + 128]
        onesf = cpk.ap[:, C_ONE:C_ONE + 128]
        identb = cb16.ap[:, 0:128]
        onesb = cb16.ap[:, 128:256]
        hT = view(4096, 8192, "hT", BF16, shape=(16, TOK))
        hP = view(4096 + 8192, 8192, "hP", BF16, shape=(16, TOK))
        oT = T(hP.ap, "oT", hP.off, hP.words)
        W0 = 4096 + 16384

        dma("sp", cpk.ap, cpack_d[:, :], cpk, wr=[cpk])
        dma("sp", cb16.ap, cb16_d[:, :], cb16, wr=[cb16])

        def norm_T(src_dram, tiles, nw_row, dst, WA, src_sb=None, load_w=True, junk=None):
            xt = [WA.get(2048, "xt") for _ in range(2)] if src_sb is None else None
            if junk is None:
                junk = WA.get(1024, "junk", BF16)
            xnL = [WA.get(1024, "xn", BF16) for _ in range(2)]
            statL = [WA.get(8, "stat") for _ in range(2)]
            if load_w:
                dma("sp", wbc.ap, nw3_d[nw_row].partition_broadcast(128), wbc, wr=[wbc])
            for i, (t, dt_) in enumerate(tiles):
                xn, stat = xnL[i % 2], statL[i % 2]
                if src_sb is None:
                    xb = xt[i % 2]
                    dma("sp", xb.ap, src_dram[t * 128:(t + 1) * 128, :], xb, wr=[xb])
                    xin = xb.ap
                else:
                    xb, xin = src_sb(t)
                act(junk.ap, xin, AF.Square, [xb], [junk, stat], accum=stat.ap[:, 0:1])
                rstd_from_ssq(stat.ap[:, 2:3], stat.ap[:, 1:2], stat.ap[:, 0:1], D, [stat], [stat])
                stt(xn.ap, xin, stat.ap[:, 2:3], wbc.ap, ALU.mult, ALU.mult, [xb, stat, wbc], [xn])
                for half in range(2):
                    bank = (2 * i + half) % 4
                    pb = psum[bank][:].bitcast(BF16)
                    for j in range(8):
                        kt = half * 8 + j
                        tr(pb[:, j * 128:(j + 1) * 128], psB[bank], xn.ap[:, kt * 128:(kt + 1) * 128], identb, [xn, cb16])
                    cp(dst.ap[:, half * 8:(half + 1) * 8, dt_ * 128:(dt_ + 1) * 128], pb.rearrange("p (k t) -> p k t", k=8),
                       [psB[bank]], [dst], eng=("dve" if half == 0 else "act"))

        WA = Alloc(W0, AW)
        norm_T(x_d, [(t, t) for t in range(NT)], 0, hT, WA)
        norm_T(xp_d, [(t, t) for t in range(NT)], 0, hP, Alloc(WA.p, AW), load_w=False)
        hHalo = CONST.get(1024, "hHalo", BF16, shape=(16, 128))
        cp(hHalo.ap, hP.ap[:, :, 896:1024], [hP], [hHalo])
        if stop_after == "p0":
            return finish(nc, P, st, dbg_d, [(hT, 4096, 8192), (hP, 4096 + 8192, 8192)], dma, locals())
        P.barrier()

        GA = Alloc(W0, AW)
        qkvT = GA.get(9 * 1024, "qkvT", shape=(9, TOK))
        zs = GA.get(3 * 1024, "zs", shape=(3, TOK))
        raw = [GA.get(1028, "raw") for _ in range(2)]
        wsl = [GA.get(1024, "wsl", BF16, shape=(16, 128)) for _ in range(3)]
        wba = GA.get(128, "wba", BF16, shape=(16, 12))
        ba_all = GA.get(16 * 12, "ba_all", shape=(16, 12))
        beta_all = GA.get(16 * 6, "beta_all", shape=(16, 6))
        g_all = GA.get(16 * 6, "g_all", shape=(16, 6))
        sp_tmp = GA.get(16 * 6, "sp_tmp", shape=(16, 6))
        nexpA = GA.get(8, "nexpA")
        Sst = [GA.get(128, f"S{h}") for h in range(6)]
        carry = GA.get(64, "carry", shape=(18, 3))
        HT = []
        names = ["Xk", "kd", "Xv", "qg", "gSL", "gSLc", "eE", "eET", "t1", "pT", "R0", "R1", "RT0", "RT1", "Y", "wTn", "qgT", "delta", "on"]
        GB = Alloc(wbc.off, wbc.off + 2048)
        for s_ in range(6):
            dct = {}
            for n in names:
                al = GB if (GB.p + 128 <= GB.hi) else GA
                dct[n] = al.get(128, f"{n}{s_}")
            HT.append(dct)
        csmL = [GA.get(64, f"csm{i}") for i in range(2)]
        csmqL = [GA.get(32, f"csmq{i}") for i in range(2)]
        wsl_i = [0]

        wload_ba = lambda: dma("pool", wba.ap[:, :, 0:12], win_d[:, OFF_B:OFF_B + 12].rearrange("(k p) n -> p k n", p=128), wba, wr=[wba])
        wload_ba()
        for tt_i in range(16):
            src = hP if tt_i < 8 else hT
            tl = tt_i % 8
            for kt in range(16):
                mm(psum[7][:, 0:12], psB[7][0], src.ap[:, kt, tl * 128:(tl + 1) * 128], wba.ap[:, kt, 0:12], [src, wba], start=(kt == 0), stop=(kt == 15))
            cp(ba_all.ap[:, tt_i, :], psum[7][:, 0:12], [psB[7][0]], [ba_all])
        act(beta_all.ap, ba_all.ap[:, :, 0:6], AF.Exp, [ba_all], [beta_all], scale=-1.0)
        ts(beta_all.ap, beta_all.ap, 1.0, ALU.add, [beta_all], [beta_all])
        recip(beta_all.ap, beta_all.ap, [beta_all], [beta_all])
        act(nexpA.ap[:, 0:6], cpk.ap[:, C_ALOG:C_ALOG + 6], AF.Exp, [cpk], [nexpA])
        for tt_i in range(16):
            tt(sp_tmp.ap[:, tt_i, :], ba_all.ap[:, tt_i, 6:12], cpk.ap[:, C_DTB:C_DTB + 6], ALU.add, [ba_all, cpk], [sp_tmp])
        act(sp_tmp.ap, sp_tmp.ap, AF.Exp, [sp_tmp], [sp_tmp])
        act(sp_tmp.ap, sp_tmp.ap, AF.Ln, [sp_tmp], [sp_tmp], bias=1.0)
        for tt_i in range(16):
            stt(g_all.ap[:, tt_i, :], sp_tmp.ap[:, tt_i, :], -1.0, nexpA.ap[:, 0:6], ALU.mult, ALU.mult, [sp_tmp, nexpA], [g_all])
        if stop_after == "ba":
            return finish(nc, P, st, dbg_d, [ba_all.dump(), beta_all.dump(), g_all.dump()], dma, locals())
        for h in range(6):
            P.op("dve", lambda e, h=h: e.memset(Sst[h].ap, 0.0), writes=[Sst[h].b])
        P.op("dve", lambda e: e.memset(carry.ap, 0.0), writes=[carry.b])

        def gdn_proj(hh, stage):
            src = hP if stage == 0 else hT
            fts = [hh * 3 + j for j in range(3)] + [6 + hh * 3 + j for j in range(3)] + [12 + hh * 3 + j for j in range(3)]
            for li, ft in enumerate(fts):
                w = wsl[wsl_i[0] % 3]
                wsl_i[0] += 1
                wload(w, win_d, OFF_QKV + ft * 128, 128)
                rb = raw[li % 2]
                for c2 in range(2):
                    bank = 2 + c2
                    for kt in range(16):
                        mm(psum[bank][:, :], psB[bank], w.ap[:, kt, :], src.ap[:, kt, c2 * 512:(c2 + 1) * 512], [w, src], start=(kt == 0), stop=(kt == 15))
                    cp(rb.ap[:, 3 + c2 * 512:3 + (c2 + 1) * 512], psum[bank][:, :], [psB[bank]], [rb], eng=("act" if c2 == 0 else "dve"))
                cp(rb.ap[:, 0:3], carry.ap[:, ft, :], [carry], [rb])
                cp(carry.ap[:, ft, :], rb.ap[:, 1024:1027], [rb], [carry])
                cw = cpk.ap[:, C_CONV + ft * 4:C_CONV + ft * 4 + 4]
                acc = qkvT.ap[:, li, :]
                ts(acc, rb.ap[:, 0:1024], cw[:, 0:1], ALU.mult, [rb, cpk], [qkvT])
                for k in range(1, 4):
                    stt(acc, rb.ap[:, k:k + 1024], cw[:, k:k + 1], acc, ALU.mult, ALU.add, [rb, cpk, qkvT], [qkvT])
                act(acc, acc, AF.Silu, [qkvT], [qkvT])
            if stage == 1:
                for j in range(3):
                    w = wsl[wsl_i[0] % 3]
                    wsl_i[0] += 1
                    wload(w, win_d, OFF_Z + (hh * 3 + j) * 128, 128)
                    for c2 in range(2):
                        bank = 2 + c2
                        for kt in range(16):
                            mm(psum[bank][:, :], psB[bank], w.ap[:, kt, :], hT.ap[:, kt, c2 * 512:(c2 + 1) * 512], [w, hT], start=(kt == 0), stop=(kt == 15))
                        act(zs.ap[:, j, c2 * 512:(c2 + 1) * 512], psum[bank][:, :], AF.Silu, [psB[bank]], [zs])

        def gdn_chunks(hh, stage):
            own = stage == 1
            h0 = hh * 3
            for cp_ in range(4):
                chunks = (2 * cp_, 2 * cp_ + 1)
                streams = [(ci, h) for ci in range(2) for h in range(3)]
                CS = [slice(c * 128, (c + 1) * 128) for c in chunks]
                G3 = [g_all.ap[:, stage * 8 + c, h0:h0 + 3] for c in chunks]
                B3 = [beta_all.ap[:, stage * 8 + c, h0:h0 + 3] for c in chunks]
                for ci in range(2):
                    o = 16 * ci
                    mm(psum[0][:, o:o + 3], psB[0][0], Umat, G3[ci], [cpk, g_all])
                    mm(psum[0][:, o + 3:o + 6], psB[0][0], SLm, G3[ci], [cpk, g_all])
                    mm(psum[0][:, o + 6:o + 9], psB[0][0], onesf, G3[ci], [cpk, g_all])
                    act(csmL[ci].ap[:, 0:9], psum[0][:, o:o + 9], AF.Exp, [psB[0][0]], [csmL[ci]])
                for s_, (ci, h) in enumerate(streams):
                    Hh, bank, csmq = HT[s_], 2 + s_, csmqL[ci]
                    for j in range(3):
                        tr(psum[bank][:, j * 128:(j + 1) * 128], psB[bank][j], qkvT.ap[:, j * 3 + h, CS[ci]], identf, [qkvT, cpk])
                    act(Hh["t1"].ap, psum[bank][:, 0:128], AF.Square, [psB[bank][0]], [Hh["t1"], csmq], accum=csmq.ap[:, h:h + 1])
                    act(Hh["t1"].ap, psum[bank][:, 128:256], AF.Square, [psB[bank][1]], [Hh["t1"], csmq], accum=csmq.ap[:, 3 + h:4 + h])
                    cp(Hh["qg"].ap, psum[bank][:, 0:128], [psB[bank][0]], [Hh["qg"]], eng="dve")
                    cp(Hh["Xk"].ap, psum[bank][:, 128:256], [psB[bank][1]], [Hh["Xk"]], eng="act")
                    cp(Hh["Xv"].ap, psum[bank][:, 256:384], [psB[bank][2]], [Hh["Xv"]], eng="dve")
                SC = []
                for ci in range(2):
                    csm, csmq = csmL[ci], csmqL[ci]
                    eG, eGr, gc = csm.ap[:, 0:3], csm.ap[:, 3:6], csm.ap[:, 6:9]
                    act(csmq.ap[:, 6:12], csmq.ap[:, 0:6], AF.Ln, [csmq], [csmq], bias=EPS)
                    act(csmq.ap[:, 12:15], csmq.ap[:, 6:9], AF.Exp, [csmq], [csmq], scale=-0.5)
                    act(csm.ap[:, 12:15], csmq.ap[:, 9:12], AF.Exp, [csmq], [csm], scale=-0.5)
                    ts(csm.ap[:, 15:18], csmq.ap[:, 9:12], -0.5, ALU.mult, [csmq], [csm])
                    rk, lnrk = csm.ap[:, 12:15], csm.ap[:, 15:18]
                    tt(csm.ap[:, 18:21], B3[ci], rk, ALU.mult, [beta_all, csm], [csm])
                    tt(csm.ap[:, 21:24], csm.ap[:, 18:21], eG, ALU.mult, [csm], [csm])
                    tt(csm.ap[:, 24:27], rk, eGr, ALU.mult, [csm], [csm])
                    ts(csm.ap[:, 27:30], csmq.ap[:, 12:15], float(128 ** -0.5), ALU.mult, [csmq], [csm])
                    SC.append(dict(eG=eG, eGr=eGr, gc=gc, rk=rk, lnrk=lnrk, sA=csm.ap[:, 18:21], sXk=csm.ap[:, 21:24],
                                   skd=csm.ap[:, 24:27], so=csm.ap[:, 27:30]))
                for s_, (ci, h) in enumerate(streams):
                    Hh, sc_, csm = HT[s_], SC[ci], csmL[ci]
                    hcol = slice(h, h + 1)
                    ts(Hh["kd"].ap, Hh["Xk"].ap, sc_["skd"][:, hcol], ALU.mult, [Hh["Xk"], csm], [Hh["kd"]], s2=1.0, op1=ALU.mult, eng="pool")
                    ts(Hh["Xk"].ap, Hh["Xk"].ap, sc_["sXk"][:, hcol], ALU.mult, [Hh["Xk"], csm], [Hh["Xk"]], s2=1.0, op1=ALU.mult, eng="pool")
                    ts(Hh["Xv"].ap, Hh["Xv"].ap, B3[ci][:, hcol], ALU.mult, [Hh["Xv"], beta_all], [Hh["Xv"]], s2=1.0, op1=ALU.mult, eng="pool")
                    if own:
                        ts(Hh["qg"].ap, Hh["qg"].ap, sc_["eG"][:, hcol], ALU.mult, [Hh["qg"], csm], [Hh["qg"]], s2=1.0, op1=ALU.mult, eng="pool")
                    ts(Hh["gSL"].ap, SLm, G3[ci][:, hcol], ALU.mult, [cpk, g_all], [Hh["gSL"]])
                    stt(Hh["gSLc"].ap, identf, sc_["lnrk"][:, hcol], Hh["gSL"].ap, ALU.mult, ALU.add, [cpk, csm, Hh["gSL"]], [Hh["gSLc"]])
                for s_, (ci, h) in enumerate(streams):
                    Hh, sc_, csm, bank = HT[s_], SC[ci], csmL[ci], 2 + s_
                    hcol = slice(h, h + 1)
                    kTr = qkvT.ap[:, 3 + h, CS[ci]]
                    qTr = qkvT.ap[:, h, CS[ci]]
                    mm(psum[bank][:, 0:128], psB[bank][0], kTr, kTr, [qkvT])
                    mm(psum[bank][:, 128:256], psB[bank][1], Umat, Hh["gSLc"].ap, [cpk, Hh["gSLc"]])
                    if own:
                        mm(psum[bank][:, 256:384], psB[bank][2], kTr, qTr, [qkvT])
                        mm(psum[bank][:, 384:512], psB[bank][3], Hh["gSL"].ap, Umat, [cpk, Hh["gSL"]])
                    act(Hh["eE"].ap, psum[bank][:, 128:256], AF.Exp, [psB[bank][1]], [Hh["eE"]])
                    tt(Hh["t1"].ap, psum[bank][:, 0:128], Hh["eE"].ap, ALU.mult, [psB[bank][0], Hh["eE"]], [Hh["t1"]])
                    stt(Hh["RT0"].ap, Hh["t1"].ap, sc_["sA"][:, hcol], NSLm, ALU.mult, ALU.mult, [Hh["t1"], csm, cpk], [Hh["RT0"]])
                    if own:
                        act(Hh["eET"].ap, psum[bank][:, 384:512], AF.Exp, [psB[bank][3]], [Hh["eET"]])
                        tt(Hh["t1"].ap, psum[bank][:, 256:384], Hh["eET"].ap, ALU.mult, [psB[bank][2], Hh["eET"]], [Hh["t1"]])
                        stt(Hh["pT"].ap, Hh["t1"].ap, sc_["rk"][:, hcol], Umat, ALU.mult, ALU.mult, [Hh["t1"], csm, cpk], [Hh["pT"]])
                for s_, (ci, h) in enumerate(streams):
                    Hh, bank = HT[s_], 2 + s_
                    tr(psum[bank][:, 0:128], psB[bank][0], Hh["RT0"].ap, identf, [Hh["RT0"], cpk])
                    cp(Hh["R0"].ap, psum[bank][:, 0:128], [psB[bank][0]], [Hh["R0"]], eng="act")
                    tt(Hh["Y"].ap, psum[bank][:, 0:128], identf, ALU.add, [psB[bank][0], cpk], [Hh["Y"]])
                for lvl in range(1, 7):
                    pr, nx = (lvl - 1) % 2, lvl % 2
                    for s_, (ci, h) in enumerate(streams):
                        Hh, bank = HT[s_], 2 + s_
                        Rp, RTp = Hh[f"R{pr}"], Hh[f"RT{pr}"]
                        Rn, RTn = Hh[f"R{nx}"], Hh[f"RT{nx}"]
                        mm(psum[bank][:, 128:256], psB[bank][1], Rp.ap, RTp.ap, [Rp, RTp])
                        if lvl < 6:
                            mm(psum[bank][:, 0:128], psB[bank][0], RTp.ap, Rp.ap, [Rp, RTp])
                        cp(RTn.ap, psum[bank][:, 128:256], [psB[bank][1]], [RTn], eng="act")
                        if lvl < 6:
                            cp(Rn.ap, psum[bank][:, 0:128], [psB[bank][0]], [Rn], eng="dve")
                    for s_, (ci, h) in enumerate(streams):
                        Hh, bank = HT[s_], 2 + s_
                        RTn = Hh[f"RT{nx}"]
                        mm(psum[bank][:, 256:384], psB[bank][2], RTn.ap, Hh["Y"].ap, [RTn, Hh["Y"]])
                        tt(Hh["Y"].ap, psum[bank][:, 256:384], Hh["Y"].ap, ALU.add, [psB[bank][2], Hh["Y"]], [Hh["Y"]])
                for s_, (ci, h) in enumerate(streams):
                    Hh, bank = HT[s_], 2 + s_
                    mm(psum[bank][:, 0:128], psB[bank][0], Hh["Xk"].ap, Hh["Y"].ap, [Hh["Xk"], Hh["Y"]])
                    if own:
                        tr(psum[bank][:, 128:256], psB[bank][1], Hh["qg"].ap, identf, [Hh["qg"], cpk])
                    act(Hh["wTn"].ap, psum[bank][:, 0:128], AF.Copy, [psB[bank][0]], [Hh["wTn"]], scale=-1.0)
                    if own:
                        cp(Hh["qgT"].ap, psum[bank][:, 128:256], [psB[bank][1]], [Hh["qgT"]], eng="dve")
                for s_, (ci, h) in enumerate(streams):
                    Hh, sc_, csm, csmq, bank = HT[s_], SC[ci], csmL[ci], csmqL[ci], 2 + s_
                    S = Sst[hh * 3 + h]
                    mm(psum[bank][:, 0:128], psB[bank][0], Hh["Y"].ap, Hh["Xv"].ap, [Hh["Y"], Hh["Xv"]], start=True, stop=False)
                    mm(psum[bank][:, 0:128], psB[bank][0], Hh["wTn"].ap, S.ap, [Hh["wTn"], S], start=False, stop=True)
                    cp(Hh["delta"].ap, psum[bank][:, 0:128], [psB[bank][0]], [Hh["delta"]], eng="dve")
                    if own:
                        mm(psum[bank][:, 128:256], psB[bank][1], Hh["qgT"].ap, S.ap, [Hh["qgT"], S], start=True, stop=False)
                        mm(psum[bank][:, 128:256], psB[bank][1], Hh["pT"].ap, Hh["delta"].ap, [Hh["pT"], Hh["delta"]], start=False, stop=True)
                    mm(psum[bank][:, 256:384], psB[bank][2], Hh["kd"].ap, Hh["delta"].ap, [Hh["kd"], Hh["delta"]])
                    stt(S.ap, S.ap, sc_["gc"][:, h:h + 1], psum[bank][:, 256:384], ALU.mult, ALU.add, [S, csm, psB[bank][2]], [S])
                    if own:
                        ts(Hh["on"].ap, psum[bank][:, 128:256], sc_["so"][:, h:h + 1], ALU.mult, [psB[bank][1], csm], [Hh["on"]])
                        act(Hh["t1"].ap, Hh["on"].ap, AF.Square, [Hh["on"]], [Hh["t1"], csmq], accum=csmq.ap[:, 16 + h:17 + h])
                if own:
                    for ci in range(2):
                        csm, csmq = csmL[ci], csmqL[ci]
                        act(csm.ap[:, 32:35], csmq.ap[:, 16:19], AF.Ln, [csmq], [csm], scale=1.0 / 128, bias=EPS)
                        act(csm.ap[:, 32:35], csm.ap[:, 32:35], AF.Exp, [csm], [csm], scale=-0.5)
                    for s_, (ci, h) in enumerate(streams):
                        Hh, csm = HT[s_], csmL[ci]
                        ts(Hh["on"].ap, Hh["on"].ap, csm.ap[:, 32 + h:33 + h], ALU.mult, [Hh["on"], csm], [Hh["on"]])
                    for s_, (ci, h) in enumerate(streams):
                        Hh, bank = HT[s_], 2 + s_
                        tr(psum[bank][:, 128:256], psB[bank][1], Hh["on"].ap, identf, [Hh["on"], cpk])
                        stt(oT.ap[:, hh * 3 + h, CS[ci]], psum[bank][:, 128:256], cpk.ap[:, C_DNW:C_DNW + 1], zs.ap[:, h, CS[ci]],
                            ALU.mult, ALU.mult, [psB[bank][1], cpk, zs], [oT])

        SKIP = os.environ.get("KSKIP", "").split(",")
        try:
            for hh in range(2):
                if "gdn" in SKIP:
                    break
                gdn_proj(hh, 0)
                chk("gproj", [qkvT.dump()])
                gdn_chunks(hh, 0)
            chk("gpre", [Sst[0].dump(), Sst[5].dump()])
            P.barrier()
            for hh in range(2):
                if "gdn" in SKIP:
                    break
                gdn_proj(hh, 1)
                gdn_chunks(hh, 1)
            chk("gdn", [oT.dump()])
            P.barrier()
            SA = Alloc(W0, AW)
            swab = SA.get(3 * 768, "swab", shape=(3, 768))
            dma("sp", swab.ap, swab_d.rearrange("t k n -> k t n"), swab, wr=[swab])
            wg_ = SA.get(5120, "wg", BF16, shape=(16, 640))
            wgL = [wg_, wg_]
            qnL = [SA.get(192, f"qn{i}", BF16) for i in range(2)]
            knL = [SA.get(64, f"kn{i}", BF16) for i in range(2)]
            qTsL = [SA.get(192, f"qTs{i}", BF16) for i in range(2)]
            KT = [SA.get(64, f"KT{i}", BF16) for i in range(2)]
            VT = [SA.get(64, f"VT{i}", BF16) for i in range(2)]
            scL = [[SA.get(384, f"sc{i}{k}") for k in range(2)] for i in range(2)]
            ETL = [[SA.get(192, f"ET{i}{k}", BF16) for k in range(2)] for i in range(2)]
            denL = [SA.get(384, f"den{i}") for i in range(2)]
            ssmL = [SA.get(16, f"ssm{i}") for i in range(2)]
            sjL = [SA.get(128, f"sjunk{i}") for i in range(2)]
            esink = SA.get(8, "esink")
            act(esink.ap[:, 0:6], cpk.ap[:, C_SINK:C_SINK + 6], AF.Exp, [cpk], [esink])
            KT3 = [KT[0], KT[1], SA.get(64, "KT2", BF16)]
            VT3 = [VT[0], VT[1], SA.get(64, "VT2", BF16)]

            def swa_A(g, ti):
                wg = wgL[g]
                src, sl = (hHalo, slice(0, 128)) if ti == 0 else (hT, slice((ti - 1) * 128, ti * 128))
                p_ = ti % 2
                qn, kn, qTs, ssm, sjunk = qnL[p_], knL[p_], qTsL[p_], ssmL[p_], sjL[p_]
                Kc, Vc = KT3[ti % 3], VT3[ti % 3]
                bq, bv = p_, 2 + p_
                for kt in range(16):
                    mm(psum[bq][:, :], psB[bq], src.ap[:, kt, sl], wg.ap[:, kt, 0:512], [src, wg], start=(kt == 0), stop=(kt == 15))
                for kt in range(16):
                    mm(psum[bv][:, 0:128], psB[bv], src.ap[:, kt, sl], wg.ap[:, kt, 512:640], [src, wg], start=(kt == 0), stop=(kt == 15))
                for j in range(4):
                    act(sjunk.ap, psum[bq][:, j * 128:(j + 1) * 128], AF.Square, [psB[bq]], [sjunk, ssm], accum=ssm.ap[:, j:j + 1])
                rstd_from_ssq(ssm.ap[:, 8:12], ssm.ap[:, 4:8], ssm.ap[:, 0:4], 128, [ssm], [ssm])
                if ti > 0:
                    for j in range(3):
                        ts(qn.ap[:, j * 128:(j + 1) * 128], psum[bq][:, j * 128:(j + 1) * 128], ssm.ap[:, 8 + j:9 + j], ALU.mult, [psB[bq], ssm], [qn])
                ts(kn.ap, psum[bq][:, 384:512], ssm.ap[:, 11:12], ALU.mult, [psB[bq], ssm], [kn])
                cp(Vc.ap, psum[bv][:, 0:128], [psB[bv]], [Vc], eng="act")

            def swa_A2(g, ti):
                p_ = ti % 2
                qn, kn, qTs = qnL[p_], knL[p_], qTsL[p_]
                Kc = KT3[ti % 3]
                bv = 2 + p_
                pb = psum[bv][:].bitcast(BF16)
                if ti > 0:
                    for j in range(3):
                        tr(pb[:, j * 128:(j + 1) * 128], psB[bv], qn.ap[:, j * 128:(j + 1) * 128], identb, [qn, cb16])
                tr(pb[:, 384:512], psB[bv], kn.ap, identb, [kn, cb16])
                if ti > 0:
                    ts(qTs.ap, pb[:, 0:384], cpk.ap[:, C_WQ:C_WQ + 1], ALU.mult, [psB[bv], cpk], [qTs])
                ts(Kc.ap, pb[:, 384:512], cpk.ap[:, C_WK:C_WK + 1], ALU.mult, [psB[bv], cpk], [Kc])

            def swa_B(g, ti):
                sl = slice((ti - 1) * 128, ti * 128)
                p_ = ti % 2
                qTs, sc2, ET2, den = qTsL[p_], scL[p_], ETL[p_], denL[p_]
                Kc, Vc, Kp, Vp = KT3[ti % 3], VT3[ti % 3], KT3[(ti - 1) % 3], VT3[(ti - 1) % 3]
                for kb, (Kt, bank) in enumerate(((Kp, 4), (Kc, 5))):
                    mm(psum[bank][:, 0:384], psB[bank], Kt.ap, qTs.ap, [Kt, qTs])
                    bsel = 0 if kb == 1 else (2 if ti == 1 else 1)
                    stt(sc2[kb].ap, psum[bank][:, 0:384], float(128 ** -0.5), swab.ap[:, bsel, g * 384:(g + 1) * 384], ALU.mult, ALU.add,
                        [psB[bank], swab], [sc2[kb]])
                    act(ET2[kb].ap, sc2[kb].ap, AF.Exp, [sc2[kb]], [ET2[kb]])
                mm(psum[6][:, 0:384], psB[6], Vp.ap, ET2[0].ap, [Vp, ET2[0]], start=True, stop=False)
                mm(psum[6][:, 0:384], psB[6], Vc.ap, ET2[1].ap, [Vc, ET2[1]], start=False, stop=True)
                mm(psum[7][:, 0:384], psB[7], onesb, ET2[0].ap, [cb16, ET2[0]], start=True, stop=False)
                mm(psum[7][:, 0:384], psB[7], onesb, ET2[1].ap, [cb16, ET2[1]], start=False, stop=True)
                for j in range(3):
                    ts(den.ap[:, j * 128:(j + 1) * 128], psum[7][:, j * 128:(j + 1) * 128], esink.ap[:, 3 * g + j:3 * g + j + 1], ALU.add,
                       [psB[7], esink], [den])
                act(den.ap, den.ap, AF.Ln, [den], [den])
                act(den.ap, den.ap, AF.Exp, [den], [den], scale=-1.0)
                tt(oT.ap[:, 6 + 3 * g:9 + 3 * g, sl], psum[6][:, 0:384].rearrange("p (a b) -> p a b", a=3),
                   den.ap.rearrange("p (a b) -> p a b", a=3), ALU.mult, [psB[6], den], [oT])

            for g in range(2):
                wg = wgL[g]
                for (c0, n, o0) in ((OFF_SQ + g * 384, 384, 0), (OFF_SK + g * 128, 128, 384), (OFF_SV + g * 128, 128, 512)):
                    dma("pool", wg.ap[:, :, o0:o0 + n], win_d[:, c0:c0 + n].rearrange("(k p) n -> p k n", p=128), wg, wr=[wg])
                swa_A(g, 0)
                swa_A(g, 1)
                swa_A2(g, 0)
                for ti in range(1, 9):
                    if ti + 1 < 9:
                        swa_A(g, ti + 1)
                    swa_A2(g, ti)
                    swa_B(g, ti)
            chk("swa", [oT.dump()])
            MA = Alloc(SA.p, AW)
            hmT = MA.get(2048, "hmT", BF16, shape=(16, 256))
            wm = MA.get(4096, "wm", BF16, shape=(16, 512))
            KM = MA.get(512, "KM", BF16, shape=(4, 256))
            VM = [MA.get(256, f"VM{i}", BF16) for i in range(2)]
            kmnL = [MA.get(256, f"kmn{i}", BF16) for i in range(2)]
            qmTL = [MA.get(256, f"qmT{i}", BF16) for i in range(2)]
            EML = [MA.get(512, f"EM{i}", BF16) for i in range(2)]
            dnmL = [MA.get(512, f"dnm{i}") for i in range(2)]
            msmL = [MA.get(16, f"msm{i}") for i in range(2)]
            mjL = [MA.get(128, f"mjunk{i}") for i in range(2)]
            wload(wm, wmkv_d, 0, 512)
            norm_T(mem_d, [(0, 0), (1, 1)], 1, hmT, Alloc(MA.p, AW))

            def headnorm_T(p_, dst3d, wcol):
                mjunk, msm, kmn = mjL[p_], msmL[p_], kmnL[p_]
                for j in range(4):
                    act(mjunk.ap, psum[p_][:, j * 128:(j + 1) * 128], AF.Square, [psB[p_]], [mjunk, msm], accum=msm.ap[:, j:j + 1])
                rstd_from_ssq(msm.ap[:, 8:12], msm.ap[:, 4:8], msm.ap[:, 0:4], 128, [msm], [msm])
                for j in range(4):
                    ts(kmn.ap[:, j * 128:(j + 1) * 128], psum[p_][:, j * 128:(j + 1) * 128], msm.ap[:, 8 + j:9 + j], ALU.mult,
                       [psB[p_], msm], [kmn])
                if dst3d is None:
                    return
                headnorm_T2(p_, dst3d, wcol)

            def headnorm_T2(p_, dst3d, wcol):
                kmn = kmnL[p_]
                pb = psum[2 + p_][:].bitcast(BF16)
                for j in range(4):
                    tr(pb[:, j * 128:(j + 1) * 128], psB[2 + p_], kmn.ap[:, j * 128:(j + 1) * 128], identb, [kmn, cb16])
                ts(dst3d[0], pb[:, 0:512].rearrange("p (a b) -> p a b", a=4), cpk.ap[:, wcol:wcol + 1], ALU.mult, [psB[2 + p_], cpk], [dst3d[1]])

            for mt in range(2):
                for kt in range(16):
                    mm(psum[mt][:, :], psB[mt], hmT.ap[:, kt, mt * 128:(mt + 1) * 128], wm.ap[:, kt, :], [hmT, wm], start=(kt == 0), stop=(kt == 15))
                headnorm_T(mt, (KM.ap[:, :, mt * 128:(mt + 1) * 128], KM), C_XK)
            wload(wm, wmkv_d, 512, 512)
            for mt in range(2):
                for kt in range(16):
                    mm(psum[mt][:, :], psB[mt], hmT.ap[:, kt, mt * 128:(mt + 1) * 128], wm.ap[:, kt, :], [hmT, wm], start=(kt == 0), stop=(kt == 15))
                cp(VM[mt].ap, psum[mt][:, :], [psB[mt]], [VM[mt]])
            wmq = wm
            wload(wmq, win_d, OFF_MQ, 512)
            def mem_A(t):
                p_ = t % 2
                tsl = slice(t * 128, (t + 1) * 128)
                for kt in range(16):
                    mm(psum[p_][:, :], psB[p_], hT.ap[:, kt, tsl], wmq.ap[:, kt, :], [hT, wmq], start=(kt == 0), stop=(kt == 15))
                headnorm_T(p_, None, C_XQ)

            def mem_A2(t):
                p_ = t % 2
                headnorm_T2(p_, (qmTL[p_].ap.rearrange("p (a b) -> p a b", a=4), qmTL[p_]), C_XQ)

            def mem_B(t):
                p_ = t % 2
                qmT, EM, dnm = qmTL[p_], EML[p_], dnmL[p_]
                tsl = slice(t * 128, (t + 1) * 128)
                for mt in range(2):
                    for h in range(4):
                        mm(psum[4 + mt][:, h * 128:(h + 1) * 128], psB[4 + mt], KM.ap[:, h, mt * 128:(mt + 1) * 128], qmT.ap[:, h * 128:(h + 1) * 128], [KM, qmT])
                    act(EM.ap[:, mt * 512:(mt + 1) * 512], psum[4 + mt][:, :], AF.Exp, [psB[4 + mt]], [EM], scale=float(128 ** -0.5))
                for h in range(4):
                    for mt in range(2):
                        mm(psum[6][:, h * 128:(h + 1) * 128], psB[6], VM[mt].ap[:, h * 128:(h + 1) * 128], EM.ap[:, mt * 512 + h * 128:mt * 512 + (h + 1) * 128],
                           [VM[mt], EM], start=(mt == 0), stop=(mt == 1))
                for mt in range(2):
                    mm(psum[7][:, :], psB[7], onesb, EM.ap[:, mt * 512:(mt + 1) * 512], [cb16, EM], start=(mt == 0), stop=(mt == 1))
                act(dnm.ap, psum[7][:, :], AF.Ln, [psB[7]], [dnm])
                act(dnm.ap, dnm.ap, AF.Exp, [dnm], [dnm], scale=-1.0)
                tt(oT.ap[:, 12:16, tsl], psum[6][:, :].rearrange("p (a b) -> p a b", a=4), dnm.ap.rearrange("p (a b) -> p a b", a=4), ALU.mult,
                   [psB[6], dnm], [oT])

            mem_A(0)
            for t in range(8):
                if t + 1 < 8:
                    mem_A(t + 1)
                mem_A2(t)
                mem_B(t)
            chk("mem", [oT.dump()])
            P.barrier()
            GA2 = Alloc(W0, AW)
            mT = GA2.get(8192, "mT", BF16, shape=(16, TOK))
            mslab = [GA2.get(4096, f"msl{i}", BF16, shape=(16, 512)) for i in range(3)]
            sg = [GA2.get(512, f"sg{i}") for i in range(6)]
            accs = [GA2.get(512, f"acc{i}") for i in range(2)]
            wo = [GA2.get(4096, "wo0", BF16, shape=(16, 512)), mslab[1]]
            brk = (range(0, 6), range(6, 12), range(12, 16))

            def load_mslab(mt):
                sl = mslab[mt % 3]
                dma("pool", sl.ap[:, :, 0:128], pall_d[:, mt * 128:(mt + 1) * 128].rearrange("(k p) n -> p k n", p=128), sl, wr=[sl])
                for br in range(3):
                    c0 = OFF_G + br * 2048 + mt * 128
                    dma("pool", sl.ap[:, :, 128 * (br + 1):128 * (br + 2)], win_d[:, c0:c0 + 128].rearrange("(k p) n -> p k n", p=128), sl, wr=[sl])

            load_mslab(0)
            load_mslab(1)
            pcount = 0
            for mt in range(16):
                if mt + 2 < 16:
                    load_mslab(mt + 2)
                elif mt + 2 < 18:
                    wload(wo[mt + 2 - 16], wout_d, (mt + 2 - 16) * 512, 512)
                sl = mslab[mt % 3]
                for c2 in range(2):
                    cols = slice(c2 * 512, (c2 + 1) * 512)
                    acc = accs[c2]
                    for br in range(3):
                        gb = c2 * 3 + br
                        pbk = 6 + (pcount % 2)
                        pcount += 1
                        sgb = sg[c2 * 3 + br]
                        for kt in range(16):
                            mm(psum[gb][:, :], psB[gb], sl.ap[:, kt, 128 * (br + 1):128 * (br + 2)], hT.ap[:, kt, cols], [sl, hT], start=(kt == 0), stop=(kt == 15))
                        act(sgb.ap, psum[gb][:, :], AF.Sigmoid, [psB[gb]], [sgb])
                        kts = list(brk[br])
                        for kt in kts:
                            mm(psum[pbk][:, :], psB[pbk], sl.ap[:, kt, 0:128], oT.ap[:, kt, cols], [sl, oT], start=(kt == kts[0]), stop=(kt == kts[-1]))
                        if br == 0:
                            tt(acc.ap, psum[pbk][:, :], sgb.ap, ALU.mult, [psB[pbk], sgb], [acc])
                        else:
                            tt(sgb.ap, psum[pbk][:, :], sgb.ap, ALU.mult, [psB[pbk], sgb], [sgb])
                            if br == 1:
                                tt(acc.ap, acc.ap, sgb.ap, ALU.add, [acc, sgb], [acc])
                            else:
                                tt(mT.ap[:, mt, cols], acc.ap, sgb.ap, ALU.add, [acc, sgb], [mT])
            chk("merge", [mT.dump()])
            x1 = T(arena[:, 4096:4096 + 16384].rearrange("p (t d) -> p t d", t=8), "x1", 4096, 16384)
            x1t = [Buf(f"x1t{t}") for t in range(8)]
            for t in range(8):
                dma("sp", x1.ap[:, t, :], x_d[t * 128:(t + 1) * 128, :], x1t[t], wr=[x1t[t], hT, oT, hP])
            pc = 0
            for cc in range(4):
                w = wo[cc % 2]
                if cc >= 2:
                    wload(w, wout_d, cc * 512, 512)
                for t in range(8):
                    bank = pc % 6
                    pc += 1
                    for kt in range(16):
                        mm(psum[bank][:, :], psB[bank], mT.ap[:, kt, t * 128:(t + 1) * 128], w.ap[:, kt, :], [mT, w], start=(kt == 0), stop=(kt == 15))
                    tt(x1.ap[:, t, cc * 512:(cc + 1) * 512], psum[bank][:, :], x1.ap[:, t, cc * 512:(cc + 1) * 512], ALU.add, [psB[bank], x1t[t]], [x1t[t]])
            chk("x1", [(x1t[t], 4096 + t * 2048, 2048) for t in range(8)])
            P.barrier()
            ML = Alloc(W0, AW)
            h2T = ML.get(8192, "h2T", BF16, shape=(16, TOK))
            wu = [ML.get(4096, f"wu{i}", BF16, shape=(16, 512)) for i in range(2)]
            wd = [ML.get(4096, f"wd{i}", BF16, shape=(4, 2048)) for i in range(2)]
            aT = [ML.get(2048, f"aT{i}", BF16, shape=(4, TOK)) for i in range(2)]
            rl = [ML.get(512, f"rl{i}") for i in range(2)]

            def load_wu(fb):
                wload(wu[fb % 2], wup_d, fb * 512, 512)

            def load_wd(fb):
                dma("pool", wd[fb % 2].ap, wdn_d[fb * 512:(fb + 1) * 512, :].rearrange("(k p) n -> p k n", p=128), wd[fb % 2], wr=[wd[fb % 2]])

            load_wu(0)
            load_wd(0)
            norm_T(None, [(t, t) for t in range(8)], 2, h2T, Alloc(ML.p, AW), src_sb=lambda t: (x1t[t], x1.ap[:, t, :]),
                   junk=T(arena[:, rl[0].off:rl[0].off + 1024].bitcast(BF16), "junk_rl"))
            upc = [0]

            def mlp_up(fb):
                w = wu[fb % 2]
                a = aT[fb % 2]
                for f4 in range(4):
                    for tc in range(2):
                        bank = upc[0] % 2
                        r = rl[upc[0] % 2]
                        upc[0] += 1
                        for kt in range(16):
                            mm(psum[bank][:, :], psB[bank], w.ap[:, kt, f4 * 128:(f4 + 1) * 128], h2T.ap[:, kt, tc * 512:(tc + 1) * 512], [w, h2T],
                               start=(kt == 0), stop=(kt == 15))
                        act(r.ap, psum[bank][:, :], AF.Relu, [psB[bank]], [r])
                        tt(a.ap[:, f4, tc * 512:(tc + 1) * 512], r.ap, r.ap, ALU.mult, [r], [a])

            dnc = [0]

            def mlp_down(fb):
                w = wd[fb % 2]
                a = aT[fb % 2]
                for t in range(8):
                    for cc in range(4):
                        bank = 2 + dnc[0] % 6
                        dnc[0] += 1
                        for k in range(4):
                            mm(psum[bank][:, :], psB[bank], a.ap[:, k, t * 128:(t + 1) * 128], w.ap[:, k, cc * 512:(cc + 1) * 512], [a, w],
                               start=(k == 0), stop=(k == 3))
                        tt(x1.ap[:, t, cc * 512:(cc + 1) * 512], psum[bank][:, :], x1.ap[:, t, cc * 512:(cc + 1) * 512], ALU.add, [psB[bank], x1t[t]], [x1t[t]])

            for fb in range(16):
                if fb + 1 < 16:
                    load_wu(fb + 1)
                mlp_up(fb)
                if fb > 0:
                    mlp_down(fb - 1)
                if fb + 1 < 16:
                    load_wd(fb + 1)
            mlp_down(15)
            for t in range(8):
                dma("sp", y_d[t * 128:(t + 1) * 128, :], x1.ap[:, t, :], x1t[t], rd=[x1t[t]])
        except StopBuild:
            pass
        return finish(nc, P, st, dbg_d, dumps, dma, locals())


def finish(nc, P, st, dbg_d, dumps, dma, L):
    if dbg_d is not None:
        off = 0
        arena = L["arena"]
        for (t, a0, words) in dumps:
            dma("sp", dbg_d[:, off:off + words], arena[:, a0:a0 + words], t, rd=[t])
            off += words
    P.final_wait()
    P.emit(nc)
    return nc


def _t5_bucket(dist):
    import math
    n = np.maximum(dist, 0)
    max_exact = 16
    nf = np.maximum(n, 1).astype(np.float32)
    large = max_exact + (np.log(nf / max_exact) / math.log(128 / max_exact) * (32 - max_exact)).astype(np.int32)
    large = np.minimum(large, 31)
    return np.where(n < max_exact, n, large)


def make_in_maps(inp):
    f32 = np.float32
    x = np.asarray(inp["x"], f32)
    mem = np.asarray(inp["mem"], f32)
    w_in = np.ascontiguousarray(np.asarray(inp["w_in"], f32)[0])
    w_mem_kv = np.ascontiguousarray(np.asarray(inp["w_mem_kv"], f32)[0])
    p_all = np.ascontiguousarray(np.concatenate([np.asarray(inp["p_dn"], f32)[0], np.asarray(inp["p_swa"], f32)[0],
                                                 np.asarray(inp["p_mem"], f32)[0]], axis=0))
    w_out = np.ascontiguousarray(np.asarray(inp["w_out"], f32)[0])
    w_up = np.ascontiguousarray(np.asarray(inp["w_mlp_up"], f32)[0])
    w_down = np.ascontiguousarray(np.asarray(inp["w_mlp_down"], f32)[0])
    nw3 = np.ascontiguousarray(np.stack([np.asarray(inp["attn_norm_w"], f32)[0], np.asarray(inp["mem_norm_w"], f32)[0],
                                         np.asarray(inp["mlp_norm_w"], f32)[0]], axis=0))
    cp = np.zeros((128, NCP), f32)
    idx = np.arange(128)
    cp[:, C_ID:C_ID + 128] = np.eye(128, dtype=f32)
    cp[:, C_U:C_U + 128] = (idx[:, None] <= idx[None, :]).astype(f32)
    cp[:, C_SL:C_SL + 128] = (idx[:, None] > idx[None, :]).astype(f32)
    cp[:, C_NSL:C_NSL + 128] = -(idx[:, None] > idx[None, :]).astype(f32)
    cp[:, C_ONE:C_ONE + 128] = 1.0
    cw = np.asarray(inp["dn_conv_w"], f32)[0]
    cp[:, C_CONV:C_CONV + 72] = cw.reshape(4, 18, 128).transpose(2, 1, 0).reshape(128, 72)
    cp[:, C_DNW] = np.asarray(inp["dn_out_norm_w"], f32)[0]
    cp[:, C_WQ] = np.asarray(inp["swa_q_norm_w"], f32)[0]
    cp[:, C_WK] = np.asarray(inp["swa_k_norm_w"], f32)[0]
    cp[:, C_XQ] = np.asarray(inp["xq_norm_w"], f32)[0]
    cp[:, C_XK] = np.asarray(inp["xk_norm_w"], f32)[0]
    cp[:, C_ALOG:C_ALOG + 6] = np.asarray(inp["dn_A_log"], f32)[0][None, :]
    cp[:, C_DTB:C_DTB + 6] = np.asarray(inp["dn_dt_bias"], f32)[0][None, :]
    cp[:, C_SINK:C_SINK + 6] = np.asarray(inp["swa_sinks"], f32)[0][None, :]
    cb16 = np.concatenate([np.eye(128, dtype=f32), np.ones((128, 128), f32)], axis=1).astype(ml_dtypes.bfloat16)
    rb = np.asarray(inp["rel_bias"], f32)
    qi = np.arange(128)[:, None]
    kj = np.arange(256)[None, :]
    dist = qi - kj + 128
    valid = (dist >= 0) & (dist < 128)
    bias = rb[_t5_bucket(dist)]
    bias = np.where(valid[:, :, None], bias, f32(NEG)).astype(f32)
    biasT = bias.transpose(1, 2, 0)
    b_prev = np.ascontiguousarray(biasT[0:128].reshape(128, 768))
    b_cur = np.ascontiguousarray(biasT[128:256].reshape(128, 768))
    b_none = np.full((128, 768), NEG, f32)
    maps = []
    for c in range(8):
        b, hf = c // 2, c % 2
        xo = np.ascontiguousarray(x[b, hf * TOK:(hf + 1) * TOK])
        xp = np.ascontiguousarray(x[b, 0:TOK]) if hf == 1 else np.zeros((TOK, D), f32)
        swab = np.stack([b_cur, b_prev, b_prev if hf == 1 else b_none], axis=0)
        maps.append({"x": xo, "xp": xp, "mem": np.ascontiguousarray(mem[b]), "w_in": w_in, "w_mem_kv": w_mem_kv,
                     "p_all": p_all, "w_out": w_out, "w_up": w_up, "w_down": w_down, "nw3": nw3, "cpack": cp,
                     "cb16": cb16, "swab": np.ascontiguousarray(swab)})
    return maps


_NC_CACHE = {}


def kernel(**inputs):
    maps = make_in_maps(inputs)
    if "nc" not in _NC_CACHE:
        _NC_CACHE["nc"] = build()
    res = run_bass_kernel_spmd(_NC_CACHE["nc"], maps, core_ids=list(range(8)))
    out = np.zeros((4, 2048, D), np.float32)
    for c in range(8):
        b, hf = c // 2, c % 2
        out[b, hf * TOK:(hf + 1) * TOK] = res.results[c]["y"]
    return out
```

```python
import os
import numpy as np
import ml_dtypes
from contextlib import ExitStack
import concourse.bass as bass
import concourse.mybir as mybir
from concourse.bass_utils import run_bass_kernel_spmd

F32 = mybir.dt.float32
BF16 = mybir.dt.bfloat16
AF = mybir.ActivationFunctionType
ALU = mybir.AluOpType

EPS = 1e-6
D = 2048
TOK = 1024
NT = 8
OFF_QKV, OFF_Z, OFF_B, OFF_A, OFF_SQ, OFF_SK, OFF_SV, OFF_MQ, OFF_G = 0, 2304, 3072, 3078, 3084, 3852, 4108, 4364, 4876
IN_W = 11020
NEG = -30000.0

C_ID, C_U, C_SL, C_NSL, C_ONE, C_CONV, C_DNW, C_WQ, C_WK, C_XQ, C_XK, C_ALOG, C_DTB, C_SINK, NCP = \
    0, 128, 256, 384, 512, 640, 712, 713, 714, 715, 716, 717, 723, 729, 736


class Buf:
    __slots__ = ("name", "lw", "rd")

    def __init__(self, name):
        self.name = name
        self.lw = None
        self.rd = {}


class Prog:
    ENG = ("pe", "act", "dve", "pool", "sp")

    def __init__(self):
        self.ops = {e: [] for e in self.ENG}
        self.cnt = {e: 0 for e in self.ENG}
        self.waited = {e: {} for e in self.ENG}
        self.dcnt = {}
        self.swq = []

    def _deps(self, reads, writes, eng=None):
        toks = []
        for b in reads:
            if b.lw is not None:
                toks.append(b.lw)
        own = None if eng is None else "E" + eng
        for b in writes:
            if b.lw is not None and b.lw[0] != own:
                toks.append(b.lw)
            toks.extend((k, v) for k, v in b.rd.items() if k != own)
        return toks

    def _filter(self, eng, toks):
        w = self.waited[eng]
        out = {}
        for (k, v) in toks:
            if eng == "pe" and k == "Epe":
                continue
            if w.get(k, 0) >= v:
                continue
            if out.get(k, 0) < v:
                out[k] = v
        for k, v in out.items():
            w[k] = v
        return list(out.items())

    def _record(self, tok, reads, writes):
        for b in writes:
            b.lw = tok
            b.rd = {}
        for b in reads:
            if b in writes:
                continue
            if b.rd.get(tok[0], 0) < tok[1]:
                b.rd[tok[0]] = tok[1]

    def op(self, eng, fn, reads=(), writes=()):
        extra = [b for b in reads if b.name.startswith("ps") and b not in writes]
        if extra:
            writes = list(writes) + extra
        waits = self._filter(eng, self._deps(reads, writes, eng if eng in ("act", "dve", "pool") else None))
        self.cnt[eng] += 1
        tok = ("E" + eng, self.cnt[eng])
        self.ops[eng].append((waits, fn, ("E" + eng, 1)))
        self._record(tok, reads, writes)
        return tok

    def dma(self, eng, fn, sembuf, reads=(), writes=(), ndesc=0):
        toks = self._deps(reads, writes)
        if eng == "pool" and ndesc:
            while self.swq and sum(n for _, n in self.swq) + ndesc > 640:
                toks.append(self.swq.pop(0)[0])
        waits = self._filter(eng, toks)
        key = "D" + sembuf.name
        self.dcnt[key] = self.dcnt.get(key, 0) + 16
        tok = (key, self.dcnt[key])
        self.ops[eng].append((waits, fn, (key, 16)))
        self._record(tok, reads, writes)
        if eng == "pool" and ndesc:
            self.swq.append((tok, ndesc))
        return tok

    def barrier(self):
        toks = [("E" + e, self.cnt[e]) for e in self.ENG if self.cnt[e] > 0] + list(self.dcnt.items())
        for e in self.ENG:
            waits = self._filter(e, toks)
            if waits:
                self.ops[e].append((waits, None, None))

    def final_wait(self, eng="sp"):
        waits = self._filter(eng, list(self.dcnt.items()))
        self.ops[eng].append((waits, None, None))

    def emit(self, nc):
        keys = ["E" + e for e in self.ENG] + sorted(self.dcnt.keys())
        with ExitStack() as st:
            sems = {}
            for k in keys:
                sems[k] = st.enter_context(nc.semaphore("s_" + k))
            block = st.enter_context(nc.Block())
            binders = {"pe": block.tensor, "act": block.scalar, "dve": block.vector,
                       "pool": block.gpsimd, "sp": block.sync}
            for eng in self.ENG:
                ops = self.ops[eng]

                def body(e, ops=ops):
                    for waits, fn, inc in ops:
                        for k, v in waits:
                            e.wait_ge(sems[k], v)
                        if fn is None:
                            continue
                        ins = fn(e)
                        ins.then_inc(sems[inc[0]], inc[1])
                binders[eng](body)


class StopBuild(Exception):
    pass


class T:
    __slots__ = ("ap", "b", "off", "words")

    def __init__(self, ap, name, off=None, words=None):
        self.ap = ap
        self.b = Buf(name)
        self.off = off
        self.words = words

    def dump(self):
        return (self, self.off, self.words)


def build(stop_after=None, dbg=None):
    nc = bass.Bass("TRN2", target_bir_lowering=False)
    P = Prog()
    dr = {}

    def din(name, shape, dt=F32):
        dr[name] = nc.dram_tensor(name, shape, dt, kind="ExternalInput").ap()
        return dr[name]

    x_d = din("x", [TOK, D])
    xp_d = din("xp", [TOK, D])
    mem_d = din("mem", [256, D])
    win_d = din("w_in", [D, IN_W])
    wmkv_d = din("w_mem_kv", [D, 1024])
    pall_d = din("p_all", [D, D])
    wout_d = din("w_out", [D, D])
    wup_d = din("w_up", [D, 4 * D])
    wdn_d = din("w_down", [4 * D, D])
    nw3_d = din("nw3", [3, D])
    cpack_d = din("cpack", [128, NCP])
    cb16_d = din("cb16", [128, 256], BF16)
    swab_d = din("swab", [3, 128, 768])
    y_d = nc.dram_tensor("y", [TOK, D], F32, kind="ExternalOutput").ap()
    dbg_d = None
    if dbg is not None:
        dbg_d = nc.dram_tensor("dbg", [128, dbg], F32, kind="ExternalOutput").ap()

    with ExitStack() as st:
        AW = 52800
        arena = st.enter_context(nc.sbuf_tensor("arena", [128, AW], F32))
        psum = [st.enter_context(nc.psum_tensor(f"ps{i}", [128, 512], F32)) for i in range(8)]
        psB = [[Buf(f"ps{i}")] * 4 for i in range(8)]

        uid = [0]

        def view(off, words, name, dt=F32, shape=None):
            ap = arena[:, off:off + words]
            if dt == BF16:
                ap = ap.bitcast(BF16)
            if shape is not None:
                ap = ap[:, 0:int(np.prod(shape))]
                if len(shape) == 2:
                    ap = ap.rearrange("p (a b) -> p a b", a=shape[0])
                elif len(shape) == 3:
                    ap = ap.rearrange("p (a b c) -> p a b c", a=shape[0], b=shape[1])
            uid[0] += 1
            return T(ap, f"{name}_{uid[0]}", off, words)

        class Alloc:
            def __init__(self, lo, hi):
                self.lo, self.hi, self.p = lo, hi, lo

            def get(self, words, name, dt=F32, shape=None):
                o = self.p
                self.p += words
                assert self.p <= self.hi, (name, self.p, self.hi)
                return view(o, words, name, dt, shape)

        def bl(ts):
            out = []
            for t in ts:
                if isinstance(t, T):
                    out.append(t.b)
                elif isinstance(t, Buf):
                    out.append(t)
                elif isinstance(t, (list, tuple)):
                    out.extend(bl(t))
            return out

        def mm(out_ap, outb, lhsT, rhs, rd, start=True, stop=True):
            P.op("pe", lambda e: e.matmul(out_ap, lhsT=lhsT, rhs=rhs, start=start, stop=stop), reads=bl(rd), writes=bl([outb]))

        def tr(out_ap, outb, in_ap, ident_ap, rd):
            P.op("pe", lambda e: e.transpose(out=out_ap, in_=in_ap, identity=ident_ap), reads=bl(rd), writes=bl([outb]))

        def act(out_ap, in_ap, func, rd, wr, scale=None, bias=None, accum=None, eng="act"):
            kw = {}
            if scale is not None:
                kw["scale"] = scale
            if bias is not None:
                kw["bias"] = bias
            if accum is not None:
                kw["accum_out"] = accum
            P.op("act", lambda e: e.activation(out=out_ap, in_=in_ap, func=func, **kw), reads=bl(rd), writes=bl(wr))

        def tt(out_ap, a, b, op, rd, wr, eng="dve"):
            P.op(eng, lambda e: e.tensor_tensor(out=out_ap, in0=a, in1=b, op=op), reads=bl(rd), writes=bl(wr))

        def ts(out_ap, a, s1, op0, rd, wr, s2=None, op1=None, eng="dve"):
            if op1 is None:
                P.op(eng, lambda e: e.tensor_scalar(out=out_ap, in0=a, scalar1=s1, scalar2=None, op0=op0), reads=bl(rd), writes=bl(wr))
            else:
                P.op(eng, lambda e: e.tensor_scalar(out=out_ap, in0=a, scalar1=s1, scalar2=s2, op0=op0, op1=op1), reads=bl(rd), writes=bl(wr))

        def stt(out_ap, a, s, b, op0, op1, rd, wr):
            P.op("dve", lambda e: e.scalar_tensor_tensor(out=out_ap, in0=a, scalar=s, in1=b, op0=op0, op1=op1), reads=bl(rd), writes=bl(wr))

        def cp(out_ap, in_ap, rd, wr, eng="dve"):
            if eng == "act":
                P.op("act", lambda e: e.copy(out=out_ap, in_=in_ap), reads=bl(rd), writes=bl(wr))
            else:
                P.op(eng, lambda e: e.tensor_copy(out=out_ap, in_=in_ap), reads=bl(rd), writes=bl(wr))

        def recip(out_ap, in_ap, rd, wr):
            P.op("dve", lambda e: e.reciprocal(out=out_ap, in_=in_ap), reads=bl(rd), writes=bl(wr))

        def dma(q, out_ap, in_ap, semT, rd=(), wr=()):
            nd = 0
            if q == "pool":
                shp = list(out_ap.shape)
                nd = int(np.prod(shp[:-1])) // 16 + 2
            P.dma(q, lambda e: e.dma_start(out=out_ap, in_=in_ap), semT.b if isinstance(semT, T) else semT, reads=bl(rd), writes=bl(wr), ndesc=nd)

        dumps = []

        def chk(name, dl):
            if stop_after == name:
                dumps.extend(dl)
                raise StopBuild()

        def wload(slab, w_dram, c0, n, kt=16, r0=0):
            dma("pool", slab.ap, w_dram[r0:r0 + kt * 128, c0:c0 + n].rearrange("(k p) n -> p k n", p=128), slab, wr=[slab])

        def rstd_from_ssq(out_ap, tmp_ap, ssq_ap, n, rd, wr):
            act(tmp_ap, ssq_ap, AF.Ln, rd, wr, scale=1.0 / n, bias=EPS)
            act(out_ap, tmp_ap, AF.Exp, wr, wr, scale=-0.5)

        CONST = Alloc(0, 4096)
        cpk = CONST.get(NCP, "cpack")
        cb16 = CONST.get(128, "cb16", BF16)
        wbc = CONST.get(2048, "wbc")
        smalls = CONST.get(64, "smalls")
        identf = cpk.ap[:, C_ID:C_ID + 128]
        Umat = cpk.ap[:, C_U:C_U + 128]
        SLm = cpk.ap[:, C_SL:C_SL + 128]
        NSLm = cpk.ap[:, C_NSL:C_NSL + 128]
        onesf = cpk.ap[:, C_ONE:C_ONE + 128]
        identb = cb16.ap[:, 0:128]
        onesb = cb16.ap[:, 128:256]
        hT = view(4096, 8192, "hT", BF16, shape=(16, TOK))
        hP = view(4096 + 8192, 8192, "hP", BF16, shape=(16, TOK))
        oT = T(hP.ap, "oT", hP.off, hP.words)
        W0 = 4096 + 16384

        dma("sp", cpk.ap, cpack_d[:, :], cpk, wr=[cpk])
        dma("sp", cb16.ap, cb16_d[:, :], cb16, wr=[cb16])

        def norm_T(src_dram, tiles, nw_row, dst, WA, src_sb=None, load_w=True, junk=None):
            xt = [WA.get(2048, "xt") for _ in range(2)] if src_sb is None else None
            if junk is None:
                junk = WA.get(1024, "junk", BF16)
            xnL = [WA.get(1024, "xn", BF16) for _ in range(2)]
            statL = [WA.get(8, "stat") for _ in range(2)]
            if load_w:
                dma("sp", wbc.ap, nw3_d[nw_row].partition_broadcast(128), wbc, wr=[wbc])
            for i, (t, dt_) in enumerate(tiles):
                xn, stat = xnL[i % 2], statL[i % 2]
                if src_sb is None:
                    xb = xt[i % 2]
                    dma("sp", xb.ap, src_dram[t * 128:(t + 1) * 128, :], xb, wr=[xb])
                    xin = xb.ap
                else:
                    xb, xin = src_sb(t)
                act(junk.ap, xin, AF.Square, [xb], [junk, stat], accum=stat.ap[:, 0:1])
                rstd_from_ssq(stat.ap[:, 2:3], stat.ap[:, 1:2], stat.ap[:, 0:1], D, [stat], [stat])
                stt(xn.ap, xin, stat.ap[:, 2:3], wbc.ap, ALU.mult, ALU.mult, [xb, stat, wbc], [xn])
                for half in range(2):
                    bank = (2 * i + half) % 4
                    pb = psum[bank][:].bitcast(BF16)
                    for j in range(8):
                        kt = half * 8 + j
                        tr(pb[:, j * 128:(j + 1) * 128], psB[bank], xn.ap[:, kt * 128:(kt + 1) * 128], identb, [xn, cb16])
                    cp(dst.ap[:, half * 8:(half + 1) * 8, dt_ * 128:(dt_ + 1) * 128], pb.rearrange("p (k t) -> p k t", k=8),
                       [psB[bank]], [dst], eng=("dve" if half == 0 else "act"))

        WA = Alloc(W0, AW)
        norm_T(x_d, [(t, t) for t in range(NT)], 0, hT, WA)
        norm_T(xp_d, [(t, t) for t in range(NT)], 0, hP, Alloc(WA.p, AW), load_w=False)
        hHalo = CONST.get(1024, "hHalo", BF16, shape=(16, 128))
        cp(hHalo.ap, hP.ap[:, :, 896:1024], [hP], [hHalo])
        if stop_after == "p0":
            return finish(nc, P, st, dbg_d, [(hT, 4096, 8192), (hP, 4096 + 8192, 8192)], dma, locals())
        P.barrier()

        GA = Alloc(W0, AW)
        qkvT = GA.get(9 * 1024, "qkvT", shape=(9, TOK))
        zs = GA.get(3 * 1024, "zs", shape=(3, TOK))
        raw = [GA.get(1028, "raw") for _ in range(2)]
        wsl = [GA.get(1024, "wsl", BF16, shape=(16, 128)) for _ in range(3)]
        wba = GA.get(128, "wba", BF16, shape=(16, 12))
        ba_all = GA.get(16 * 12, "ba_all", shape=(16, 12))
        beta_all = GA.get(16 * 6, "beta_all", shape=(16, 6))
        g_all = GA.get(16 * 6, "g_all", shape=(16, 6))
        sp_tmp = GA.get(16 * 6, "sp_tmp", shape=(16, 6))
        nexpA = GA.get(8, "nexpA")
        Sst = [GA.get(128, f"S{h}") for h in range(6)]
        carry = GA.get(64, "carry", shape=(18, 3))
        HT = []
        names = ["Xk", "kd", "Xv", "qg", "gSL", "gSLc", "eE", "eET", "t1", "pT", "R0", "R1", "RT0", "RT1", "Y", "wTn", "qgT", "delta", "on"]
        GB = Alloc(wbc.off, wbc.off + 2048)
        for s_ in range(6):
            dct = {}
            for n in names:
                al = GB if (GB.p + 128 <= GB.hi) else GA
                dct[n] = al.get(128, f"{n}{s_}")
            HT.append(dct)
        csmL = [GA.get(64, f"csm{i}") for i in range(2)]
        csmqL = [GA.get(32, f"csmq{i}") for i in range(2)]
        wsl_i = [0]

        wload_ba = lambda: dma("pool", wba.ap[:, :, 0:12], win_d[:, OFF_B:OFF_B + 12].rearrange("(k p) n -> p k n", p=128), wba, wr=[wba])
        wload_ba()
        for tt_i in range(16):
            src = hP if tt_i < 8 else hT
            tl = tt_i % 8
            for kt in range(16):
                mm(psum[7][:, 0:12], psB[7][0], src.ap[:, kt, tl * 128:(tl + 1) * 128], wba.ap[:, kt, 0:12], [src, wba], start=(kt == 0), stop=(kt == 15))
            cp(ba_all.ap[:, tt_i, :], psum[7][:, 0:12], [psB[7][0]], [ba_all])
        act(beta_all.ap, ba_all.ap[:, :, 0:6], AF.Exp, [ba_all], [beta_all], scale=-1.0)
        ts(beta_all.ap, beta_all.ap, 1.0, ALU.add, [beta_all], [beta_all])
        recip(beta_all.ap, beta_all.ap, [beta_all], [beta_all])
        act(nexpA.ap[:, 0:6], cpk.ap[:, C_ALOG:C_ALOG + 6], AF.Exp, [cpk], [nexpA])
        for tt_i in range(16):
            tt(sp_tmp.ap[:, tt_i, :], ba_all.ap[:, tt_i, 6:12], cpk.ap[:, C_DTB:C_DTB + 6], ALU.add, [ba_all, cpk], [sp_tmp])
        act(sp_tmp.ap, sp_tmp.ap, AF.Exp, [sp_tmp], [sp_tmp])
        act(sp_tmp.ap, sp_tmp.ap, AF.Ln, [sp_tmp], [sp_tmp], bias=1.0)
        for tt_i in range(16):
            stt(g_all.ap[:, tt_i, :], sp_tmp.ap[:, tt_i, :], -1.0, nexpA.ap[:, 0:6], ALU.mult, ALU.mult, [sp_tmp, nexpA], [g_all])
        if stop_after == "ba":
            return finish(nc, P, st, dbg_d, [ba_all.dump(), beta_all.dump(), g_all.dump()], dma, locals())
        for h in range(6):
            P.op("dve", lambda e, h=h: e.memset(Sst[h].ap, 0.0), writes=[Sst[h].b])
        P.op("dve", lambda e: e.memset(carry.ap, 0.0), writes=[carry.b])

        def gdn_proj(hh, stage):
            src = hP if stage == 0 else hT
            fts = [hh * 3 + j for j in range(3)] + [6 + hh * 3 + j for j in range(3)] + [12 + hh * 3 + j for j in range(3)]
            for li, ft in enumerate(fts):
                w = wsl[wsl_i[0] % 3]
                wsl_i[0] += 1
                wload(w, win_d, OFF_QKV + ft * 128, 128)
                rb = raw[li % 2]
                for c2 in range(2):
                    bank = 2 + c2
                    for kt in range(16):
                        mm(psum[bank][:, :], psB[bank], w.ap[:, kt, :], src.ap[:, kt, c2 * 512:(c2 + 1) * 512], [w, src], start=(kt == 0), stop=(kt == 15))
                    cp(rb.ap[:, 3 + c2 * 512:3 + (c2 + 1) * 512], psum[bank][:, :], [psB[bank]], [rb], eng=("act" if c2 == 0 else "dve"))
                cp(rb.ap[:, 0:3], carry.ap[:, ft, :], [carry], [rb])
                cp(carry.ap[:, ft, :], rb.ap[:, 1024:1027], [rb], [carry])
                cw = cpk.ap[:, C_CONV + ft * 4:C_CONV + ft * 4 + 4]
                acc = qkvT.ap[:, li, :]
                ts(acc, rb.ap[:, 0:1024], cw[:, 0:1], ALU.mult, [rb, cpk], [qkvT])
                for k in range(1, 4):
                    stt(acc, rb.ap[:, k:k + 1024], cw[:, k:k + 1], acc, ALU.mult, ALU.add, [rb, cpk, qkvT], [qkvT])
                act(acc, acc, AF.Silu, [qkvT], [qkvT])
            if stage == 1:
                for j in range(3):
                    w = wsl[wsl_i[0] % 3]
                    wsl_i[0] += 1
                    wload(w, win_d, OFF_Z + (hh * 3 + j) * 128, 128)
                    for c2 in range(2):
                        bank = 2 + c2
                        for kt in range(16):
                            mm(psum[bank][:, :], psB[bank], w.ap[:, kt, :], hT.ap[:, kt, c2 * 512:(c2 + 1) * 512], [w, hT], start=(kt == 0), stop=(kt == 15))
                        act(zs.ap[:, j, c2 * 512:(c2 + 1) * 512], psum[bank][:, :], AF.Silu, [psB[bank]], [zs])

        def gdn_chunks(hh, stage):
            own = stage == 1
            h0 = hh * 3
            for cp_ in range(4):
                chunks = (2 * cp_, 2 * cp_ + 1)
                streams = [(ci, h) for ci in range(2) for h in range(3)]
                CS = [slice(c * 128, (c + 1) * 128) for c in chunks]
                G3 = [g_all.ap[:, stage * 8 + c, h0:h0 + 3] for c in chunks]
                B3 = [beta_all.ap[:, stage * 8 + c, h0:h0 + 3] for c in chunks]
                for ci in range(2):
                    o = 16 * ci
                    mm(psum[0][:, o:o + 3], psB[0][0], Umat, G3[ci], [cpk, g_all])
                    mm(psum[0][:, o + 3:o + 6], psB[0][0], SLm, G3[ci], [cpk, g_all])
                    mm(psum[0][:, o + 6:o + 9], psB[0][0], onesf, G3[ci], [cpk, g_all])
                    act(csmL[ci].ap[:, 0:9], psum[0][:, o:o + 9], AF.Exp, [psB[0][0]], [csmL[ci]])
                for s_, (ci, h) in enumerate(streams):
                    Hh, bank, csmq = HT[s_], 2 + s_, csmqL[ci]
                    for j in range(3):
                        tr(psum[bank][:, j * 128:(j + 1) * 128], psB[bank][j], qkvT.ap[:, j * 3 + h, CS[ci]], identf, [qkvT, cpk])
                    act(Hh["t1"].ap, psum[bank][:, 0:128], AF.Square, [psB[bank][0]], [Hh["t1"], csmq], accum=csmq.ap[:, h:h + 1])
                    act(Hh["t1"].ap, psum[bank][:, 128:256], AF.Square, [psB[bank][1]], [Hh["t1"], csmq], accum=csmq.ap[:, 3 + h:4 + h])
                    cp(Hh["qg"].ap, psum[bank][:, 0:128], [psB[bank][0]], [Hh["qg"]], eng="dve")
                    cp(Hh["Xk"].ap, psum[bank][:, 128:256], [psB[bank][1]], [Hh["Xk"]], eng="act")
                    cp(Hh["Xv"].ap, psum[bank][:, 256:384], [psB[bank][2]], [Hh["Xv"]], eng="dve")
                SC = []
                for ci in range(2):
                    csm, csmq = csmL[ci], csmqL[ci]
                    eG, eGr, gc = csm.ap[:, 0:3], csm.ap[:, 3:6], csm.ap[:, 6:9]
                    act(csmq.ap[:, 6:12], csmq.ap[:, 0:6], AF.Ln, [csmq], [csmq], bias=EPS)
                    act(csmq.ap[:, 12:15], csmq.ap[:, 6:9], AF.Exp, [csmq], [csmq], scale=-0.5)
                    act(csm.ap[:, 12:15], csmq.ap[:, 9:12], AF.Exp, [csmq], [csm], scale=-0.5)
                    ts(csm.ap[:, 15:18], csmq.ap[:, 9:12], -0.5, ALU.mult, [csmq], [csm])
                    rk, lnrk = csm.ap[:, 12:15], csm.ap[:, 15:18]
                    tt(csm.ap[:, 18:21], B3[ci], rk, ALU.mult, [beta_all, csm], [csm])
                    tt(csm.ap[:, 21:24], csm.ap[:, 18:21], eG, ALU.mult, [csm], [csm])
                    tt(csm.ap[:, 24:27], rk, eGr, ALU.mult, [csm], [csm])
                    ts(csm.ap[:, 27:30], csmq.ap[:, 12:15], float(128 ** -0.5), ALU.mult, [csmq], [csm])
                    SC.append(dict(eG=eG, eGr=eGr, gc=gc, rk=rk, lnrk=lnrk, sA=csm.ap[:, 18:21], sXk=csm.ap[:, 21:24],
                                   skd=csm.ap[:, 24:27], so=csm.ap[:, 27:30]))
                for s_, (ci, h) in enumerate(streams):
                    Hh, sc_, csm = HT[s_], SC[ci], csmL[ci]
                    hcol = slice(h, h + 1)
                    ts(Hh["kd"].ap, Hh["Xk"].ap, sc_["skd"][:, hcol], ALU.mult, [Hh["Xk"], csm], [Hh["kd"]], s2=1.0, op1=ALU.mult, eng="pool")
                    ts(Hh["Xk"].ap, Hh["Xk"].ap, sc_["sXk"][:, hcol], ALU.mult, [Hh["Xk"], csm], [Hh["Xk"]], s2=1.0, op1=ALU.mult, eng="pool")
                    ts(Hh["Xv"].ap, Hh["Xv"].ap, B3[ci][:, hcol], ALU.mult, [Hh["Xv"], beta_all], [Hh["Xv"]], s2=1.0, op1=ALU.mult, eng="pool")
                    if own:
                        ts(Hh["qg"].ap, Hh["qg"].ap, sc_["eG"][:, hcol], ALU.mult, [Hh["qg"], csm], [Hh["qg"]], s2=1.0, op1=ALU.mult, eng="pool")
                    ts(Hh["gSL"].ap, SLm, G3[ci][:, hcol], ALU.mult, [cpk, g_all], [Hh["gSL"]])
                    stt(Hh["gSLc"].ap, identf, sc_["lnrk"][:, hcol], Hh["gSL"].ap, ALU.mult, ALU.add, [cpk, csm, Hh["gSL"]], [Hh["gSLc"]])
                for s_, (ci, h) in enumerate(streams):
                    Hh, sc_, csm, bank = HT[s_], SC[ci], csmL[ci], 2 + s_
                    hcol = slice(h, h + 1)
                    kTr = qkvT.ap[:, 3 + h, CS[ci]]
                    qTr = qkvT.ap[:, h, CS[ci]]
                    mm(psum[bank][:, 0:128], psB[bank][0], kTr, kTr, [qkvT])
                    mm(psum[bank][:, 128:256], psB[bank][1], Umat, Hh["gSLc"].ap, [cpk, Hh["gSLc"]])
                    if own:
                        mm(psum[bank][:, 256:384], psB[bank][2], kTr, qTr, [qkvT])
                        mm(psum[bank][:, 384:512], psB[bank][3], Hh["gSL"].ap, Umat, [cpk, Hh["gSL"]])
                    act(Hh["eE"].ap, psum[bank][:, 128:256], AF.Exp, [psB[bank][1]], [Hh["eE"]])
                    tt(Hh["t1"].ap, psum[bank][:, 0:128], Hh["eE"].ap, ALU.mult, [psB[bank][0], Hh["eE"]], [Hh["t1"]])
                    stt(Hh["RT0"].ap, Hh["t1"].ap, sc_["sA"][:, hcol], NSLm, ALU.mult, ALU.mult, [Hh["t1"], csm, cpk], [Hh["RT0"]])
                    if own:
                        act(Hh["eET"].ap, psum[bank][:, 384:512], AF.Exp, [psB[bank][3]], [Hh["eET"]])
                        tt(Hh["t1"].ap, psum[bank][:, 256:384], Hh["eET"].ap, ALU.mult, [psB[bank][2], Hh["eET"]], [Hh["t1"]])
                        stt(Hh["pT"].ap, Hh["t1"].ap, sc_["rk"][:, hcol], Umat, ALU.mult, ALU.mult, [Hh["t1"], csm, cpk], [Hh["pT"]])
                for s_, (ci, h) in enumerate(streams):
                    Hh, bank = HT[s_], 2 + s_
                    tr(psum[bank][:, 0:128], psB[bank][0], Hh["RT0"].ap, identf, [Hh["RT0"], cpk])
                    cp(Hh["R0"].ap, psum[bank][:, 0:128], [psB[bank][0]], [Hh["R0"]], eng="act")
                    tt(Hh["Y"].ap, psum[bank][:, 0:128], identf, ALU.add, [psB[bank][0], cpk], [Hh["Y"]])
                for lvl in range(1, 7):
                    pr, nx = (lvl - 1) % 2, lvl % 2
                    for s_, (ci, h) in enumerate(streams):
                        Hh, bank = HT[s_], 2 + s_
                        Rp, RTp = Hh[f"R{pr}"], Hh[f"RT{pr}"]
                        Rn, RTn = Hh[f"R{nx}"], Hh[f"RT{nx}"]
                        mm(psum[bank][:, 128:256], psB[bank][1], Rp.ap, RTp.ap, [Rp, RTp])
                        if lvl < 6:
                            mm(psum[bank][:, 0:128], psB[bank][0], RTp.ap, Rp.ap, [Rp, RTp])
                        cp(RTn.ap, psum[bank][:, 128:256], [psB[bank][1]], [RTn], eng="act")
                        if lvl < 6:
                            cp(Rn.ap, psum[bank][:, 0:128], [psB[bank][0]], [Rn], eng="dve")
                    for s_, (ci, h) in enumerate(streams):
                        Hh, bank = HT[s_], 2 + s_
                        RTn = Hh[f"RT{nx}"]
                        mm(psum[bank][:, 256:384], psB[bank][2], RTn.ap, Hh["Y"].ap, [RTn, Hh["Y"]])
                        tt(Hh["Y"].ap, psum[bank][:, 256:384], Hh["Y"].ap, ALU.add, [psB[bank][2], Hh["Y"]], [Hh["Y"]])
                for s_, (ci, h) in enumerate(streams):
                    Hh, bank = HT[s_], 2 + s_
                    mm(psum[bank][:, 0:128], psB[bank][0], Hh["Xk"].ap, Hh["Y"].ap, [Hh["Xk"], Hh["Y"]])
                    if own:
                        tr(psum[bank][:, 128:256], psB[bank][1], Hh["qg"].ap, identf, [Hh["qg"], cpk])
                    act(Hh["wTn"].ap, psum[bank][:, 0:128], AF.Copy, [psB[bank][0]], [Hh["wTn"]], scale=-1.0)
                    if own:
                        cp(Hh["qgT"].ap, psum[bank][:, 128:256], [psB[bank][1]], [Hh["qgT"]], eng="dve")
                for s_, (ci, h) in enumerate(streams):
                    Hh, sc_, csm, csmq, bank = HT[s_], SC[ci], csmL[ci], csmqL[ci], 2 + s_
                    S = Sst[hh * 3 + h]
                    mm(psum[bank][:, 0:128], psB[bank][0], Hh["Y"].ap, Hh["Xv"].ap, [Hh["Y"], Hh["Xv"]], start=True, stop=False)
                    mm(psum[bank][:, 0:128], psB[bank][0], Hh["wTn"].ap, S.ap, [Hh["wTn"], S], start=False, stop=True)
                    cp(Hh["delta"].ap, psum[bank][:, 0:128], [psB[bank][0]], [Hh["delta"]], eng="dve")
                    if own:
                        mm(psum[bank][:, 128:256], psB[bank][1], Hh["qgT"].ap, S.ap, [Hh["qgT"], S], start=True, stop=False)
                        mm(psum[bank][:, 128:256], psB[bank][1], Hh["pT"].ap, Hh["delta"].ap, [Hh["pT"], Hh["delta"]], start=False, stop=True)
                    mm(psum[bank][:, 256:384], psB[bank][2], Hh["kd"].ap, Hh["delta"].ap, [Hh["kd"], Hh["delta"]])
                    stt(S.ap, S.ap, sc_["gc"][:, h:h + 1], psum[bank][:, 256:384], ALU.mult, ALU.add, [S, csm, psB[bank][2]], [S])
                    if own:
                        ts(Hh["on"].ap, psum[bank][:, 128:256], sc_["so"][:, h:h + 1], ALU.mult, [psB[bank][1], csm], [Hh["on"]])
                        act(Hh["t1"].ap, Hh["on"].ap, AF.Square, [Hh["on"]], [Hh["t1"], csmq], accum=csmq.ap[:, 16 + h:17 + h])
                if own:
                    for ci in range(2):
                        csm, csmq = csmL[ci], csmqL[ci]
                        act(csm.ap[:, 32:35], csmq.ap[:, 16:19], AF.Ln, [csmq], [csm], scale=1.0 / 128, bias=EPS)
                        act(csm.ap[:, 32:35], csm.ap[:, 32:35], AF.Exp, [csm], [csm], scale=-0.5)
                    for s_, (ci, h) in enumerate(streams):
                        Hh, csm = HT[s_], csmL[ci]
                        ts(Hh["on"].ap, Hh["on"].ap, csm.ap[:, 32 + h:33 + h], ALU.mult, [Hh["on"], csm], [Hh["on"]])
                    for s_, (ci, h) in enumerate(streams):
                        Hh, bank = HT[s_], 2 + s_
                        tr(psum[bank][:, 128:256], psB[bank][1], Hh["on"].ap, identf, [Hh["on"], cpk])
                        stt(oT.ap[:, hh * 3 + h, CS[ci]], psum[bank][:, 128:256], cpk.ap[:, C_DNW:C_DNW + 1], zs.ap[:, h, CS[ci]],
                            ALU.mult, ALU.mult, [psB[bank][1], cpk, zs], [oT])

        SKIP = os.environ.get("KSKIP", "").split(",")
        try:
            for hh in range(2):
                if "gdn" in SKIP:
                    break
                gdn_proj(hh, 0)
                chk("gproj", [qkvT.dump()])
                gdn_chunks(hh, 0)
            chk("gpre", [Sst[0].dump(), Sst[5].dump()])
            P.barrier()
            for hh in range(2):
                if "gdn" in SKIP:
                    break
                gdn_proj(hh, 1)
                gdn_chunks(hh, 1)
            chk("gdn", [oT.dump()])
            P.barrier()
            SA = Alloc(W0, AW)
            swab = SA.get(3 * 768, "swab", shape=(3, 768))
            dma("sp", swab.ap, swab_d.rearrange("t k n -> k t n"), swab, wr=[swab])
            wg_ = SA.get(5120, "wg", BF16, shape=(16, 640))
            wgL = [wg_, wg_]
            qnL = [SA.get(192, f"qn{i}", BF16) for i in range(2)]
            knL = [SA.get(64, f"kn{i}", BF16) for i in range(2)]
            qTsL = [SA.get(192, f"qTs{i}", BF16) for i in range(2)]
            KT = [SA.get(64, f"KT{i}", BF16) for i in range(2)]
            VT = [SA.get(64, f"VT{i}", BF16) for i in range(2)]
            scL = [[SA.get(384, f"sc{i}{k}") for k in range(2)] for i in range(2)]
            ETL = [[SA.get(192, f"ET{i}{k}", BF16) for k in range(2)] for i in range(2)]
            denL = [SA.get(384, f"den{i}") for i in range(2)]
            ssmL = [SA.get(16, f"ssm{i}") for i in range(2)]
            sjL = [SA.get(128, f"sjunk{i}") for i in range(2)]
            esink = SA.get(8, "esink")
            act(esink.ap[:, 0:6], cpk.ap[:, C_SINK:C_SINK + 6], AF.Exp, [cpk], [esink])
            KT3 = [KT[0], KT[1], SA.get(64, "KT2", BF16)]
            VT3 = [VT[0], VT[1], SA.get(64, "VT2", BF16)]

            def swa_A(g, ti):
                wg = wgL[g]
                src, sl = (hHalo, slice(0, 128)) if ti == 0 else (hT, slice((ti - 1) * 128, ti * 128))
                p_ = ti % 2
                qn, kn, qTs, ssm, sjunk = qnL[p_], knL[p_], qTsL[p_], ssmL[p_], sjL[p_]
                Kc, Vc = KT3[ti % 3], VT3[ti % 3]
                bq, bv = p_, 2 + p_
                for kt in range(16):
                    mm(psum[bq][:, :], psB[bq], src.ap[:, kt, sl], wg.ap[:, kt, 0:512], [src, wg], start=(kt == 0), stop=(kt == 15))
                for kt in range(16):
                    mm(psum[bv][:, 0:128], psB[bv], src.ap[:, kt, sl], wg.ap[:, kt, 512:640], [src, wg], start=(kt == 0), stop=(kt == 15))
                for j in range(4):
                    act(sjunk.ap, psum[bq][:, j * 128:(j + 1) * 128], AF.Square, [psB[bq]], [sjunk, ssm], accum=ssm.ap[:, j:j + 1])
                rstd_from_ssq(ssm.ap[:, 8:12], ssm.ap[:, 4:8], ssm.ap[:, 0:4], 128, [ssm], [ssm])
                if ti > 0:
                    for j in range(3):
                        ts(qn.ap[:, j * 128:(j + 1) * 128], psum[bq][:, j * 128:(j + 1) * 128], ssm.ap[:, 8 + j:9 + j], ALU.mult, [psB[bq], ssm], [qn])
                ts(kn.ap, psum[bq][:, 384:512], ssm.ap[:, 11:12], ALU.mult, [psB[bq], ssm], [kn])
                cp(Vc.ap, psum[bv][:, 0:128], [psB[bv]], [Vc], eng="act")

            def swa_A2(g, ti):
                p_ = ti % 2
                qn, kn, qTs = qnL[p_], knL[p_], qTsL[p_]
                Kc = KT3[ti % 3]
                bv = 2 + p_
                pb = psum[bv][:].bitcast(BF16)
                if ti > 0:
                    for j in range(3):
                        tr(pb[:, j * 128:(j + 1) * 128], psB[bv], qn.ap[:, j * 128:(j + 1) * 128], identb, [qn, cb16])
                tr(pb[:, 384:512], psB[bv], kn.ap, identb, [kn, cb16])
                if ti > 0:
                    ts(qTs.ap, pb[:, 0:384], cpk.ap[:, C_WQ:C_WQ + 1], ALU.mult, [psB[bv], cpk], [qTs])
                ts(Kc.ap, pb[:, 384:512], cpk.ap[:, C_WK:C_WK + 1], ALU.mult, [psB[bv], cpk], [Kc])

            def swa_B(g, ti):
                sl = slice((ti - 1) * 128, ti * 128)
                p_ = ti % 2
                qTs, sc2, ET2, den = qTsL[p_], scL[p_], ETL[p_], denL[p_]
                Kc, Vc, Kp, Vp = KT3[ti % 3], VT3[ti % 3], KT3[(ti - 1) % 3], VT3[(ti - 1) % 3]
                for kb, (Kt, bank) in enumerate(((Kp, 4), (Kc, 5))):
                    mm(psum[bank][:, 0:384], psB[bank], Kt.ap, qTs.ap, [Kt, qTs])
                    bsel = 0 if kb == 1 else (2 if ti == 1 else 1)
                    stt(sc2[kb].ap, psum[bank][:, 0:384], float(128 ** -0.5), swab.ap[:, bsel, g * 384:(g + 1) * 384], ALU.mult, ALU.add,
                        [psB[bank], swab], [sc2[kb]])
                    act(ET2[kb].ap, sc2[kb].ap, AF.Exp, [sc2[kb]], [ET2[kb]])
                mm(psum[6][:, 0:384], psB[6], Vp.ap, ET2[0].ap, [Vp, ET2[0]], start=True, stop=False)
                mm(psum[6][:, 0:384], psB[6], Vc.ap, ET2[1].ap, [Vc, ET2[1]], start=False, stop=True)
                mm(psum[7][:, 0:384], psB[7], onesb, ET2[0].ap, [cb16, ET2[0]], start=True, stop=False)
                mm(psum[7][:, 0:384], psB[7], onesb, ET2[1].ap, [cb16, ET2[1]], start=False, stop=True)
                for j in range(3):
                    ts(den.ap[:, j * 128:(j + 1) * 128], psum[7][:, j * 128:(j + 1) * 128], esink.ap[:, 3 * g + j:3 * g + j + 1], ALU.add,
                       [psB[7], esink], [den])
                act(den.ap, den.ap, AF.Ln, [den], [den])
                act(den.ap, den.ap, AF.Exp, [den], [den], scale=-1.0)
                tt(oT.ap[:, 6 + 3 * g:9 + 3 * g, sl], psum[6][:, 0:384].rearrange("p (a b) -> p a b", a=3),
                   den.ap.rearrange("p (a b) -> p a b", a=3), ALU.mult, [psB[6], den], [oT])

            for g in range(2):
                wg = wgL[g]
                for (c0, n, o0) in ((OFF_SQ + g * 384, 384, 0), (OFF_SK + g * 128, 128, 384), (OFF_SV + g * 128, 128, 512)):
                    dma("pool", wg.ap[:, :, o0:o0 + n], win_d[:, c0:c0 + n].rearrange("(k p) n -> p k n", p=128), wg, wr=[wg])
                swa_A(g, 0)
                swa_A(g, 1)
                swa_A2(g, 0)
                for ti in range(1, 9):
                    if ti + 1 < 9:
                        swa_A(g, ti + 1)
                    swa_A2(g, ti)
                    swa_B(g, ti)
            chk("swa", [oT.dump()])
            MA = Alloc(SA.p, AW)
            hmT = MA.get(2048, "hmT", BF16, shape=(16, 256))
            wm = MA.get(4096, "wm", BF16, shape=(16, 512))
            KM = MA.get(512, "KM", BF16, shape=(4, 256))
            VM = [MA.get(256, f"VM{i}", BF16) for i in range(2)]
            kmnL = [MA.get(256, f"kmn{i}", BF16) for i in range(2)]
            qmTL = [MA.get(256, f"qmT{i}", BF16) for i in range(2)]
            EML = [MA.get(512, f"EM{i}", BF16) for i in range(2)]
            dnmL = [MA.get(512, f"dnm{i}") for i in range(2)]
            msmL = [MA.get(16, f"msm{i}") for i in range(2)]
            mjL = [MA.get(128, f"mjunk{i}") for i in range(2)]
            wload(wm, wmkv_d, 0, 512)
            norm_T(mem_d, [(0, 0), (1, 1)], 1, hmT, Alloc(MA.p, AW))

            def headnorm_T(p_, dst3d, wcol):
                mjunk, msm, kmn = mjL[p_], msmL[p_], kmnL[p_]
                for j in range(4):
                    act(mjunk.ap, psum[p_][:, j * 128:(j + 1) * 128], AF.Square, [psB[p_]], [mjunk, msm], accum=msm.ap[:, j:j + 1])
                rstd_from_ssq(msm.ap[:, 8:12], msm.ap[:, 4:8], msm.ap[:, 0:4], 128, [msm], [msm])
                for j in range(4):
                    ts(kmn.ap[:, j * 128:(j + 1) * 128], psum[p_][:, j * 128:(j + 1) * 128], msm.ap[:, 8 + j:9 + j], ALU.mult,
                       [psB[p_], msm], [kmn])
                if dst3d is None:
                    return
                headnorm_T2(p_, dst3d, wcol)

            def headnorm_T2(p_, dst3d, wcol):
                kmn = kmnL[p_]
                pb = psum[2 + p_][:].bitcast(BF16)
                for j in range(4):
                    tr(pb[:, j * 128:(j + 1) * 128], psB[2 + p_], kmn.ap[:, j * 128:(j + 1) * 128], identb, [kmn, cb16])
                ts(dst3d[0], pb[:, 0:512].rearrange("p (a b) -> p a b", a=4), cpk.ap[:, wcol:wcol + 1], ALU.mult, [psB[2 + p_], cpk], [dst3d[1]])

            for mt in range(2):
                for kt in range(16):
                    mm(psum[mt][:, :], psB[mt], hmT.ap[:, kt, mt * 128:(mt + 1) * 128], wm.ap[:, kt, :], [hmT, wm], start=(kt == 0), stop=(kt == 15))
                headnorm_T(mt, (KM.ap[:, :, mt * 128:(mt + 1) * 128], KM), C_XK)
            wload(wm, wmkv_d, 512, 512)
            for mt in range(2):
                for kt in range(16):
                    mm(psum[mt][:, :], psB[mt], hmT.ap[:, kt, mt * 128:(mt + 1) * 128], wm.ap[:, kt, :], [hmT, wm], start=(kt == 0), stop=(kt == 15))
                cp(VM[mt].ap, psum[mt][:, :], [psB[mt]], [VM[mt]])
            wmq = wm
            wload(wmq, win_d, OFF_MQ, 512)
            def mem_A(t):
                p_ = t % 2
                tsl = slice(t * 128, (t + 1) * 128)
                for kt in range(16):
                    mm(psum[p_][:, :], psB[p_], hT.ap[:, kt, tsl], wmq.ap[:, kt, :], [hT, wmq], start=(kt == 0), stop=(kt == 15))
                headnorm_T(p_, None, C_XQ)

            def mem_A2(t):
                p_ = t % 2
                headnorm_T2(p_, (qmTL[p_].ap.rearrange("p (a b) -> p a b", a=4), qmTL[p_]), C_XQ)

            def mem_B(t):
                p_ = t % 2
                qmT, EM, dnm = qmTL[p_], EML[p_], dnmL[p_]
                tsl = slice(t * 128, (t + 1) * 128)
                for mt in range(2):
                    for h in range(4):
                        mm(psum[4 + mt][:, h * 128:(h + 1) * 128], psB[4 + mt], KM.ap[:, h, mt * 128:(mt + 1) * 128], qmT.ap[:, h * 128:(h + 1) * 128], [KM, qmT])
                    act(EM.ap[:, mt * 512:(mt + 1) * 512], psum[4 + mt][:, :], AF.Exp, [psB[4 + mt]], [EM], scale=float(128 ** -0.5))
                for h in range(4):
                    for mt in range(2):
                        mm(psum[6][:, h * 128:(h + 1) * 128], psB[6], VM[mt].ap[:, h * 128:(h + 1) * 128], EM.ap[:, mt * 512 + h * 128:mt * 512 + (h + 1) * 128],
                           [VM[mt], EM], start=(mt == 0), stop=(mt == 1))
                for mt in range(2):
                    mm(psum[7][:, :], psB[7], onesb, EM.ap[:, mt * 512:(mt + 1) * 512], [cb16, EM], start=(mt == 0), stop=(mt == 1))
                act(dnm.ap, psum[7][:, :], AF.Ln, [psB[7]], [dnm])
                act(dnm.ap, dnm.ap, AF.Exp, [dnm], [dnm], scale=-1.0)
                tt(oT.ap[:, 12:16, tsl], psum[6][:, :].rearrange("p (a b) -> p a b", a=4), dnm.ap.rearrange("p (a b) -> p a b", a=4), ALU.mult,
                   [psB[6], dnm], [oT])

            mem_A(0)
            for t in range(8):
                if t + 1 < 8:
                    mem_A(t + 1)
                mem_A2(t)
                mem_B(t)
            chk("mem", [oT.dump()])
            P.barrier()
            GA2 = Alloc(W0, AW)
            mT = GA2.get(8192, "mT", BF16, shape=(16, TOK))
            mslab = [GA2.get(4096, f"msl{i}", BF16, shape=(16, 512)) for i in range(3)]
            sg = [GA2.get(512, f"sg{i}") for i in range(6)]
            accs = [GA2.get(512, f"acc{i}") for i in range(2)]
            wo = [GA2.get(4096, "wo0", BF16, shape=(16, 512)), mslab[1]]
            brk = (range(0, 6), range(6, 12), range(12, 16))

            def load_mslab(mt):
                sl = mslab[mt % 3]
                dma("pool", sl.ap[:, :, 0:128], pall_d[:, mt * 128:(mt + 1) * 128].rearrange("(k p) n -> p k n", p=128), sl, wr=[sl])
                for br in range(3):
                    c0 = OFF_G + br * 2048 + mt * 128
                    dma("pool", sl.ap[:, :, 128 * (br + 1):128 * (br + 2)], win_d[:, c0:c0 + 128].rearrange("(k p) n -> p k n", p=128), sl, wr=[sl])

            load_mslab(0)
            load_mslab(1)
            pcount = 0
            for mt in range(16):
                if mt + 2 < 16:
                    load_mslab(mt + 2)
                elif mt + 2 < 18:
                    wload(wo[mt + 2 - 16], wout_d, (mt + 2 - 16) * 512, 512)
                sl = mslab[mt % 3]
                for c2 in range(2):
                    cols = slice(c2 * 512, (c2 + 1) * 512)
                    acc = accs[c2]
                    for br in range(3):
                        gb = c2 * 3 + br
                        pbk = 6 + (pcount % 2)
                        pcount += 1
                        sgb = sg[c2 * 3 + br]
                        for kt in range(16):
                            mm(psum[gb][:, :], psB[gb], sl.ap[:, kt, 128 * (br + 1):128 * (br + 2)], hT.ap[:, kt, cols], [sl, hT], start=(kt == 0), stop=(kt == 15))
                        act(sgb.ap, psum[gb][:, :], AF.Sigmoid, [psB[gb]], [sgb])
                        kts = list(brk[br])
                        for kt in kts:
                            mm(psum[pbk][:, :], psB[pbk], sl.ap[:, kt, 0:128], oT.ap[:, kt, cols], [sl, oT], start=(kt == kts[0]), stop=(kt == kts[-1]))
                        if br == 0:
                            tt(acc.ap, psum[pbk][:, :], sgb.ap, ALU.mult, [psB[pbk], sgb], [acc])
                        else:
                            tt(sgb.ap, psum[pbk][:, :], sgb.ap, ALU.mult, [psB[pbk], sgb], [sgb])
                            if br == 1:
                                tt(acc.ap, acc.ap, sgb.ap, ALU.add, [acc, sgb], [acc])
                            else:
                                tt(mT.ap[:, mt, cols], acc.ap, sgb.ap, ALU.add, [acc, sgb], [mT])
            chk("merge", [mT.dump()])
            x1 = T(arena[:, 4096:4096 + 16384].rearrange("p (t d) -> p t d", t=8), "x1", 4096, 16384)
            x1t = [Buf(f"x1t{t}") for t in range(8)]
            for t in range(8):
                dma("sp", x1.ap[:, t, :], x_d[t * 128:(t + 1) * 128, :], x1t[t], wr=[x1t[t], hT, oT, hP])
            pc = 0
            for cc in range(4):
                w = wo[cc % 2]
                if cc >= 2:
                    wload(w, wout_d, cc * 512, 512)
                for t in range(8):
                    bank = pc % 6
                    pc += 1
                    for kt in range(16):
                        mm(psum[bank][:, :], psB[bank], mT.ap[:, kt, t * 128:(t + 1) * 128], w.ap[:, kt, :], [mT, w], start=(kt == 0), stop=(kt == 15))
                    tt(x1.ap[:, t, cc * 512:(cc + 1) * 512], psum[bank][:, :], x1.ap[:, t, cc * 512:(cc + 1) * 512], ALU.add, [psB[bank], x1t[t]], [x1t[t]])
            chk("x1", [(x1t[t], 4096 + t * 2048, 2048) for t in range(8)])
            P.barrier()
            ML = Alloc(W0, AW)
            h2T = ML.get(8192, "h2T", BF16, shape=(16, TOK))
            wu = [ML.get(4096, f"wu{i}", BF16, shape=(16, 512)) for i in range(2)]
            wd = [ML.get(4096, f"wd{i}", BF16, shape=(4, 2048)) for i in range(2)]
            aT = [ML.get(2048, f"aT{i}", BF16, shape=(4, TOK)) for i in range(2)]
            rl = [ML.get(512, f"rl{i}") for i in range(2)]

            def load_wu(fb):
                wload(wu[fb % 2], wup_d, fb * 512, 512)

            def load_wd(fb):
                dma("pool", wd[fb % 2].ap, wdn_d[fb * 512:(fb + 1) * 512, :].rearrange("(k p) n -> p k n", p=128), wd[fb % 2], wr=[wd[fb % 2]])

            load_wu(0)
            load_wd(0)
            norm_T(None, [(t, t) for t in range(8)], 2, h2T, Alloc(ML.p, AW), src_sb=lambda t: (x1t[t], x1.ap[:, t, :]),
                   junk=T(arena[:, rl[0].off:rl[0].off + 1024].bitcast(BF16), "junk_rl"))
            upc = [0]

            def mlp_up(fb):
                w = wu[fb % 2]
                a = aT[fb % 2]
                for f4 in range(4):
                    for tc in range(2):
                        bank = upc[0] % 2
                        r = rl[upc[0] % 2]
                        upc[0] += 1
                        for kt in range(16):
                            mm(psum[bank][:, :], psB[bank], w.ap[:, kt, f4 * 128:(f4 + 1) * 128], h2T.ap[:, kt, tc * 512:(tc + 1) * 512], [w, h2T],
                               start=(kt == 0), stop=(kt == 15))
                        act(r.ap, psum[bank][:, :], AF.Relu, [psB[bank]], [r])
                        tt(a.ap[:, f4, tc * 512:(tc + 1) * 512], r.ap, r.ap, ALU.mult, [r], [a])

            dnc = [0]

            def mlp_down(fb):
                w = wd[fb % 2]
                a = aT[fb % 2]
                for t in range(8):
                    for cc in range(4):
                        bank = 2 + dnc[0] % 6
                        dnc[0] += 1
                        for k in range(4):
                            mm(psum[bank][:, :], psB[bank], a.ap[:, k, t * 128:(t + 1) * 128], w.ap[:, k, cc * 512:(cc + 1) * 512], [a, w],
                               start=(k == 0), stop=(k == 3))
                        tt(x1.ap[:, t, cc * 512:(cc + 1) * 512], psum[bank][:, :], x1.ap[:, t, cc * 512:(cc + 1) * 512], ALU.add, [psB[bank], x1t[t]], [x1t[t]])

            for fb in range(16):
                if fb + 1 < 16:
                    load_wu(fb + 1)
                mlp_up(fb)
                if fb > 0:
                    mlp_down(fb - 1)
                if fb + 1 < 16:
                    load_wd(fb + 1)
            mlp_down(15)
            for t in range(8):
                dma("sp", y_d[t * 128:(t + 1) * 128, :], x1.ap[:, t, :], x1t[t], rd=[x1t[t]])
        except StopBuild:
            pass
        return finish(nc, P, st, dbg_d, dumps, dma, locals())


def finish(nc, P, st, dbg_d, dumps, dma, L):
    if dbg_d is not None:
        off = 0
        arena = L["arena"]
        for (t, a0, words) in dumps:
            dma("sp", dbg_d[:, off:off + words], arena[:, a0:a0 + words], t, rd=[t])
            off += words
    P.final_wait()
    P.emit(nc)
    return nc


def _t5_bucket(dist):
    import math
    n = np.maximum(dist, 0)
    max_exact = 16
    nf = np.maximum(n, 1).astype(np.float32)
    large = max_exact + (np.log(nf / max_exact) / math.log(128 / max_exact) * (32 - max_exact)).astype(np.int32)
    large = np.minimum(large, 31)
    return np.where(n < max_exact, n, large)


def make_in_maps(inp):
    f32 = np.float32
    x = np.asarray(inp["x"], f32)
    mem = np.asarray(inp["mem"], f32)
    w_in = np.ascontiguousarray(np.asarray(inp["w_in"], f32)[0])
    w_mem_kv = np.ascontiguousarray(np.asarray(inp["w_mem_kv"], f32)[0])
    p_all = np.ascontiguousarray(np.concatenate([np.asarray(inp["p_dn"], f32)[0], np.asarray(inp["p_swa"], f32)[0],
                                                 np.asarray(inp["p_mem"], f32)[0]], axis=0))
    w_out = np.ascontiguousarray(np.asarray(inp["w_out"], f32)[0])
    w_up = np.ascontiguousarray(np.asarray(inp["w_mlp_up"], f32)[0])
    w_down = np.ascontiguousarray(np.asarray(inp["w_mlp_down"], f32)[0])
    nw3 = np.ascontiguousarray(np.stack([np.asarray(inp["attn_norm_w"], f32)[0], np.asarray(inp["mem_norm_w"], f32)[0],
                                         np.asarray(inp["mlp_norm_w"], f32)[0]], axis=0))
    cp = np.zeros((128, NCP), f32)
    idx = np.arange(128)
    cp[:, C_ID:C_ID + 128] = np.eye(128, dtype=f32)
    cp[:, C_U:C_U + 128] = (idx[:, None] <= idx[None, :]).astype(f32)
    cp[:, C_SL:C_SL + 128] = (idx[:, None] > idx[None, :]).astype(f32)
    cp[:, C_NSL:C_NSL + 128] = -(idx[:, None] > idx[None, :]).astype(f32)
    cp[:, C_ONE:C_ONE + 128] = 1.0
    cw = np.asarray(inp["dn_conv_w"], f32)[0]
    cp[:, C_CONV:C_CONV + 72] = cw.reshape(4, 18, 128).transpose(2, 1, 0).reshape(128, 72)
    cp[:, C_DNW] = np.asarray(inp["dn_out_norm_w"], f32)[0]
    cp[:, C_WQ] = np.asarray(inp["swa_q_norm_w"], f32)[0]
    cp[:, C_WK] = np.asarray(inp["swa_k_norm_w"], f32)[0]
    cp[:, C_XQ] = np.asarray(inp["xq_norm_w"], f32)[0]
    cp[:, C_XK] = np.asarray(inp["xk_norm_w"], f32)[0]
    cp[:, C_ALOG:C_ALOG + 6] = np.asarray(inp["dn_A_log"], f32)[0][None, :]
    cp[:, C_DTB:C_DTB + 6] = np.asarray(inp["dn_dt_bias"], f32)[0][None, :]
    cp[:, C_SINK:C_SINK + 6] = np.asarray(inp["swa_sinks"], f32)[0][None, :]
    cb16 = np.concatenate([np.eye(128, dtype=f32), np.ones((128, 128), f32)], axis=1).astype(ml_dtypes.bfloat16)
    rb = np.asarray(inp["rel_bias"], f32)
    qi = np.arange(128)[:, None]
    kj = np.arange(256)[None, :]
    dist = qi - kj + 128
    valid = (dist >= 0) & (dist < 128)
    bias = rb[_t5_bucket(dist)]
    bias = np.where(valid[:, :, None], bias, f32(NEG)).astype(f32)
    biasT = bias.transpose(1, 2, 0)
    b_prev = np.ascontiguousarray(biasT[0:128].reshape(128, 768))
    b_cur = np.ascontiguousarray(biasT[128:256].reshape(128, 768))
    b_none = np.full((128, 768), NEG, f32)
    maps = []
    for c in range(8):
        b, hf = c // 2, c % 2
        xo = np.ascontiguousarray(x[b, hf * TOK:(hf + 1) * TOK])
        xp = np.ascontiguousarray(x[b, 0:TOK]) if hf == 1 else np.zeros((TOK, D), f32)
        swab = np.stack([b_cur, b_prev, b_prev if hf == 1 else b_none], axis=0)
        maps.append({"x": xo, "xp": xp, "mem": np.ascontiguousarray(mem[b]), "w_in": w_in, "w_mem_kv": w_mem_kv,
                     "p_all": p_all, "w_out": w_out, "w_up": w_up, "w_down": w_down, "nw3": nw3, "cpack": cp,
                     "cb16": cb16, "swab": np.ascontiguousarray(swab)})
    return maps


_NC_CACHE = {}


def kernel(**inputs):
    maps = make_in_maps(inputs)
    if "nc" not in _NC_CACHE:
        _NC_CACHE["nc"] = build()
    res = run_bass_kernel_spmd(_NC_CACHE["nc"], maps, core_ids=list(range(8)))
    out = np.zeros((4, 2048, D), np.float32)
    for c in range(8):
        b, hf = c // 2, c % 2
        out[b, hf * TOK:(hf + 1) * TOK] = res.results[c]["y"]
    return out
```
